# Optimizing a Trainium2 kernel written in Bass

```python
import jax, jax.numpy as jnp
from jax import lax
import numpy as np

D_MODEL = 1024
BATCH = 4
SEQ = 4096
DEPTH = 4

N_MIXERS = 2
ALPHA = (2 * DEPTH) ** 0.25
BETA = (8 * DEPTH) ** -0.25
LN_EPS = 1e-5
N_A = (DEPTH + 1) // 2
N_B = DEPTH // 2
D_FF = 2816
ML_HEADS = 4
ML_DQK = D_MODEL // (2 * ML_HEADS)
ML_DV = D_MODEL // ML_HEADS
ML_QK = ML_HEADS * ML_DQK
ML_PROJ = 2 * ML_QK + 2 * D_MODEL + 4 * ML_HEADS
CHUNK = 64
HEAD_DIM = 64
N_Q_HEADS = D_MODEL // HEAD_DIM
N_KV_HEADS = 4
GROUP = N_Q_HEADS // N_KV_HEADS
WINDOW = 128
BLOCK = 128
AT_PROJ = (N_Q_HEADS + 2 * N_KV_HEADS) * HEAD_DIM
MASK_VALUE = -1e30

kernel_name = "hybrid_mlstm_swa_macaron_deepnorm"


def layer_norm(x, g, b):
    xf = x.astype(jnp.float32)
    mu = jnp.mean(xf, axis=-1, keepdims=True)
    var = jnp.mean(jnp.square(xf - mu), axis=-1, keepdims=True)
    return ((xf - mu) * lax.rsqrt(var + LN_EPS) * g + b).astype(x.dtype)


def swiglu(x, w_in, w_out):
    gu = x @ w_in
    g, u = gu[..., :D_FF], gu[..., D_FF:]
    return (jax.nn.silu(g) * u) @ w_out


def mlstm_chunkwise(q, k, v, i_pre, f_pre):
    B, H, S, dk = q.shape
    dv = v.shape[-1]
    nc = S // CHUNK
    q = q.astype(jnp.float32).reshape(B, H, nc, CHUNK, dk)
    k = k.astype(jnp.float32).reshape(B, H, nc, CHUNK, dk)
    v = v.astype(jnp.float32).reshape(B, H, nc, CHUNK, dv)
    logf = jax.nn.log_sigmoid(f_pre.astype(jnp.float32)).reshape(B, H, nc, CHUNK)
    ig = i_pre.astype(jnp.float32).reshape(B, H, nc, CHUNK)
    g = jnp.cumsum(logf, axis=-1)
    G = g[..., -1]
    w_end = G[..., None] - g + ig
    a = jnp.max(w_end, axis=-1)
    e_end = jnp.exp(w_end - a[..., None])
    K_c = jnp.einsum('bhcl,bhclv,bhclk->bhcvk', e_end, v, k)
    N_c = jnp.einsum('bhcl,bhclk->bhck', e_end, k)

    def step(carry, inp):
        C, n, m = carry
        Gc, ac, Kc, Nc = inp
        m_new = jnp.maximum(Gc + m, ac)
        sp = jnp.exp(Gc + m - m_new)
        sc = jnp.exp(ac - m_new)
        C_new = sp[..., None, None] * C + sc[..., None, None] * Kc
        n_new = sp[..., None] * n + sc[..., None] * Nc
        return (C_new, n_new, m_new), (C, n, m)

    init = (jnp.zeros((B, H, dv, dk), jnp.float32), jnp.zeros((B, H, dk), jnp.float32),
            jnp.zeros((B, H), jnp.float32))
    xs = (jnp.moveaxis(G, 2, 0), jnp.moveaxis(a, 2, 0), jnp.moveaxis(K_c, 2, 0), jnp.moveaxis(N_c, 2, 0))
    _, (C0, n0, m0) = lax.scan(step, init, xs)
    C0 = jnp.moveaxis(C0, 0, 2)
    n0 = jnp.moveaxis(n0, 0, 2)
    m0 = jnp.moveaxis(m0, 0, 2)

    tril = jnp.tril(jnp.ones((CHUNK, CHUNK), dtype=bool))
    dmat = g[..., :, None] - g[..., None, :] + ig[..., None, :]
    dmat = jnp.where(tril, dmat, MASK_VALUE)
    inter_log = g + m0[..., None]
    m_out = jnp.maximum(inter_log, jnp.max(dmat, axis=-1))
    wts = jnp.exp(dmat - m_out[..., None])
    s_inter = jnp.exp(inter_log - m_out)
    s_qk = jnp.einsum('bhcjd,bhcsd->bhcjs', q, k) * wts
    num = jnp.einsum('bhcjs,bhcsv->bhcjv', s_qk, v) + s_inter[..., None] * jnp.einsum('bhcvd,bhcjd->bhcjv', C0, q)
    den = jnp.sum(s_qk, axis=-1) + s_inter * jnp.einsum('bhcd,bhcjd->bhcj', n0, q)
    den = jnp.maximum(jnp.abs(den), jnp.exp(-m_out))
    h = num / den[..., None]
    return h.reshape(B, H, S, dv)


def mlstm_mixer(x, w_in, gate_b, norm_g, w_out):
    B, S, _ = x.shape
    p = x @ w_in
    q = p[..., :ML_QK].reshape(B, S, ML_HEADS, ML_DQK).transpose(0, 2, 1, 3)
    k = p[..., ML_QK:2 * ML_QK].reshape(B, S, ML_HEADS, ML_DQK).transpose(0, 2, 1, 3) * (ML_DQK ** -0.5)
    v = p[..., 2 * ML_QK:2 * ML_QK + D_MODEL].reshape(B, S, ML_HEADS, ML_DV).transpose(0, 2, 1, 3)
    o = p[..., 2 * ML_QK + D_MODEL:2 * ML_QK + 2 * D_MODEL]
    gates = (p[..., 2 * ML_QK + 2 * D_MODEL:] + gate_b).reshape(B, S, 4, ML_HEADS).transpose(2, 0, 3, 1)
    h_f = mlstm_chunkwise(q, k, v, gates[0], gates[1])
    flip = lambda a: jnp.flip(a, axis=2)
    h_b = flip(mlstm_chunkwise(flip(q), flip(k), flip(v), flip(gates[2]), flip(gates[3])))
    h = h_f + h_b
    mu = jnp.mean(h, axis=-1, keepdims=True)
    var = jnp.mean(jnp.square(h - mu), axis=-1, keepdims=True)
    h = (h - mu) * lax.rsqrt(var + LN_EPS)
    h = h.transpose(0, 2, 1, 3).reshape(B, S, D_MODEL) * norm_g
    out = jax.nn.sigmoid(o.astype(jnp.float32)) * h
    return out.astype(x.dtype) @ w_out


def alibi_slopes():
    return jnp.exp2(-8.0 * jnp.arange(1, N_Q_HEADS + 1, dtype=jnp.float32) / N_Q_HEADS)


def window_attention(x, w_in, sink, w_out):
    B, S, _ = x.shape
    nb = S // BLOCK
    p = x @ w_in
    nq = N_Q_HEADS * HEAD_DIM
    nk = N_KV_HEADS * HEAD_DIM
    q = p[..., :nq].reshape(B, nb, BLOCK, N_KV_HEADS, GROUP, HEAD_DIM)
    k = p[..., nq:nq + nk].reshape(B, S, N_KV_HEADS, HEAD_DIM)
    v = p[..., nq + nk:].reshape(B, S, N_KV_HEADS, HEAD_DIM)
    pad = ((0, 0), (BLOCK, BLOCK), (0, 0), (0, 0))
    kp = jnp.pad(k, pad).reshape(B, nb + 2, BLOCK, N_KV_HEADS, HEAD_DIM)
    vp = jnp.pad(v, pad).reshape(B, nb + 2, BLOCK, N_KV_HEADS, HEAD_DIM)
    kw = jnp.concatenate([kp[:, :-2], kp[:, 1:-1], kp[:, 2:]], axis=2)
    vw = jnp.concatenate([vp[:, :-2], vp[:, 1:-1], vp[:, 2:]], axis=2)
    s = jnp.einsum('bnqkgd,bnskd->bnkgqs', q, kw).astype(jnp.float32) * (HEAD_DIM ** -0.5)
    qi = jnp.arange(BLOCK)[:, None]
    kj = jnp.arange(3 * BLOCK)[None, :]
    rel = qi - kj + BLOCK
    dist = jnp.abs(rel).astype(jnp.float32)
    key_pos = jnp.arange(nb)[:, None] * BLOCK - BLOCK + jnp.arange(3 * BLOCK)[None, :]
    valid = (jnp.abs(rel) <= WINDOW)[None] & ((key_pos >= 0) & (key_pos < S))[:, None, :]
    slopes = alibi_slopes().reshape(N_KV_HEADS, GROUP)
    s = s - slopes[:, :, None, None] * dist
    s = jnp.where(valid[None, :, None, None], s, MASK_VALUE)
    sink_l = sink.astype(jnp.float32).reshape(N_KV_HEADS, GROUP)[None, None, :, :, None, None]
    m = jnp.maximum(jnp.max(s, axis=-1, keepdims=True), sink_l)
    pe = jnp.exp(s - m)
    attn = pe / (jnp.sum(pe, axis=-1, keepdims=True) + jnp.exp(sink_l - m))
    o = jnp.einsum('bnkgqs,bnskd->bnqkgd', attn, vw.astype(jnp.float32)).reshape(B, S, nq)
    return o.astype(x.dtype) @ w_out


def setup_inputs(seed: int = 0) -> dict:
    key = jax.random.key(seed)
    ks = jax.random.split(key, 16)
    nrm = jax.random.normal
    x = nrm(ks[0], (BATCH, SEQ, D_MODEL), jnp.float32)
    ffn_w_in = nrm(ks[1], (DEPTH, 2, D_MODEL, 2 * D_FF), jnp.float32) * D_MODEL ** -0.5
    ffn_w_out = nrm(ks[2], (DEPTH, 2, D_FF, D_MODEL), jnp.float32) * (D_FF ** -0.5 * BETA)
    ln_g = 1.0 + 0.02 * nrm(ks[3], (DEPTH, 3, D_MODEL), jnp.float32)
    ln_b = 0.02 * nrm(ks[4], (DEPTH, 3, D_MODEL), jnp.float32)
    ml_w_in = nrm(ks[5], (N_A, D_MODEL, ML_PROJ), jnp.float32) * D_MODEL ** -0.5
    ml_w_in = ml_w_in.at[..., 2 * ML_QK:2 * ML_QK + D_MODEL].multiply(BETA)
    f_off = jnp.linspace(3.0, 6.0, ML_HEADS, dtype=jnp.float32)
    zero = jnp.zeros((ML_HEADS,), jnp.float32)
    gate_off = jnp.stack([zero, f_off, zero, f_off])
    ml_gate_b = (0.1 * nrm(ks[6], (N_A, 4, ML_HEADS), jnp.float32) + gate_off).reshape(N_A, 4 * ML_HEADS)
    ml_norm_g = 1.0 + 0.02 * nrm(ks[7], (N_A, D_MODEL), jnp.float32)
    ml_w_out = nrm(ks[8], (N_A, D_MODEL, D_MODEL), jnp.float32) * (D_MODEL ** -0.5 * BETA)
    at_w_in = nrm(ks[9], (N_B, D_MODEL, AT_PROJ), jnp.float32) * D_MODEL ** -0.5
    at_w_in = at_w_in.at[..., (N_Q_HEADS + N_KV_HEADS) * HEAD_DIM:].multiply(BETA)
    at_sink = 0.5 * nrm(ks[10], (N_B, N_Q_HEADS), jnp.float32)
    at_w_out = nrm(ks[11], (N_B, D_MODEL, D_MODEL), jnp.float32) * (D_MODEL ** -0.5 * BETA)
    return {"x": x, "ffn_w_in": ffn_w_in, "ffn_w_out": ffn_w_out, "ln_g": ln_g, "ln_b": ln_b,
            "ml_w_in": ml_w_in, "ml_gate_b": ml_gate_b, "ml_norm_g": ml_norm_g, "ml_w_out": ml_w_out,
            "at_w_in": at_w_in, "at_sink": at_sink, "at_w_out": at_w_out}


def reference(x, ffn_w_in, ffn_w_out, ln_g, ln_b, ml_w_in, ml_gate_b, ml_norm_g, ml_w_out,
              at_w_in, at_sink, at_w_out):
    for l in range(DEPTH):
        x = layer_norm(ALPHA * x + 0.5 * swiglu(x, ffn_w_in[l, 0], ffn_w_out[l, 0]), ln_g[l, 0], ln_b[l, 0])
        j = l // N_MIXERS
        if l % N_MIXERS == 0:
            y = mlstm_mixer(x, ml_w_in[j], ml_gate_b[j], ml_norm_g[j], ml_w_out[j])
        else:
            y = window_attention(x, at_w_in[j], at_sink[j], at_w_out[j])
        x = layer_norm(ALPHA * x + y, ln_g[l, 1], ln_b[l, 1])
        x = layer_norm(ALPHA * x + 0.5 * swiglu(x, ffn_w_in[l, 1], ffn_w_out[l, 1]), ln_g[l, 2], ln_b[l, 2])
    return x
```

```python
import numpy as np
from contextlib import ExitStack
import concourse.bass as bass
import concourse.mybir as mybir
from concourse.bass_utils import run_bass_kernel_spmd

F32 = mybir.dt.float32
BF16 = mybir.dt.bfloat16
AF = mybir.ActivationFunctionType
ALU = mybir.AluOpType

D = 1024
DC = 8
T = 2048
NT = 16
DFF = 2816
FC = 22
DEPTH = 4
ALPHA = float((2 * DEPTH) ** 0.25)
LN_EPS = 1e-5
NCORES = 8
import os
DBG = int(os.environ.get("KDBG", "0"))


class TR:
    __slots__ = ("key", "ap")

    def __init__(self, key, ap):
        self.key = key
        self.ap = ap

    def __getitem__(self, idx):
        return TR(self.key, self.ap[idx])


class Op:
    __slots__ = ("eng", "fn", "dma", "deps", "needs_inc", "tick", "inc")

    def __init__(self, eng, fn, dma, inc=16):
        self.eng = eng
        self.fn = fn
        self.dma = dma
        self.inc = inc
        self.deps = ()
        self.needs_inc = False
        self.tick = 0


ENGS = ("pe", "act", "dve", "pool", "sp")


class Prog:
    def __init__(self):
        self.ops = []
        self.last_w = {}
        self.readers = {}
        self.last_on = {}

    def add(self, eng, fn, reads=(), writes=(), dma=None, extra_deps=(), inc=16):
        idx = len(self.ops)
        op = Op(eng, fn, dma, inc)
        deps = set(extra_deps)
        rk = [r.key if isinstance(r, TR) else r for r in reads]
        wk = [w.key if isinstance(w, TR) else w for w in writes]
        for k in rk:
            if k in self.last_w:
                deps.add(self.last_w[k])
        for k in wk:
            if k in self.last_w:
                deps.add(self.last_w[k])
            deps.update(self.readers.get(k, ()))
        op.deps = deps
        for k in rk:
            self.readers.setdefault(k, []).append(idx)
        for k in wk:
            self.last_w[k] = idx
            self.readers[k] = []
        self.ops.append(op)
        if dma is None:
            self.last_on[eng] = idx
        return idx

    def fence(self):
        lasts = list(self.last_on.values())
        for e in ("pe", "act", "dve", "pool", "sp"):
            self.add(e, None, extra_deps=[i for i in lasts])

    def emit(self, nc, block, engsem, dmasems):
        ops = self.ops
        for op in ops:
            for d in op.deps:
                dop = ops[d]
                if dop.dma is None:
                    if dop.eng == "pe" and op.eng == "pe" and op.dma is None:
                        continue
                    dop.needs_inc = True
        cnt = {e: 0 for e in ENGS}
        dcnt = {}
        for op in ops:
            if op.dma is not None:
                dcnt[op.dma] = dcnt.get(op.dma, 0) + op.inc
                op.tick = dcnt[op.dma]
            elif op.needs_inc:
                cnt[op.eng] += 1
                op.tick = cnt[op.eng]
        groups = sorted(dcnt.keys(), key=str)
        assert len(groups) <= len(dmasems), (len(groups), len(dmasems))
        gsem = {g: dmasems[i] for i, g in enumerate(groups)}
        per_eng = {e: [] for e in ENGS}
        for op in ops:
            per_eng[op.eng].append(op)

        def run(engname, e):
            waited = {}
            for op in per_eng[engname]:
                need = {}
                for d in op.deps:
                    dop = ops[d]
                    if dop.dma is not None:
                        s = ("d", dop.dma)
                    else:
                        if dop.eng == "pe" and op.eng == "pe" and op.dma is None:
                            continue
                        s = ("e", dop.eng)
                    if dop.tick > need.get(s, 0):
                        need[s] = dop.tick
                for s, v in need.items():
                    if waited.get(s, 0) >= v:
                        continue
                    waited[s] = v
                    sem = gsem[s[1]] if s[0] == "d" else engsem[s[1]]
                    e.wait_ge(sem, v)
                if op.fn is None:
                    if op.needs_inc:
                        e.sem_inc(engsem[engname], 1)
                    continue
                inst = op.fn(e)
                if op.dma is not None:
                    if op.inc == 16:
                        inst.then_inc(gsem[op.dma], 16)
                    else:
                        inst.then_inc(gsem[op.dma])
                elif op.needs_inc:
                    inst.then_inc(engsem[engname], 1)

        @block.tensor
        def _(e):
            run("pe", e)

        @block.scalar
        def _(e):
            run("act", e)

        @block.vector
        def _(e):
            run("dve", e)

        @block.gpsimd
        def _(e):
            run("pool", e)

        @block.sync
        def _(e):
            run("sp", e)


C_LNG = 0
C_LNB = 96
C_DIST = 192
C_MASKF = 704
C_MASKB = 832
C_MTRIF = 960
C_MTRIB = 1088
C_MBLK = 1216
C_MIND0 = 1344
C_MIND1 = 1472
C_IDENT = 1600
C_ONES = 1728
C_SEL = 1856
C_GBIAS = 1858
C_SINK = 1890
C_SLOPE = 1922
NCONST = 1938

BIGDIST = 1.0e6


def _const_table(core, ln_g, ln_b, ml_gate_b, at_sink):
    c = np.zeros((128, NCONST), np.float32)
    odd = core % 2
    p = np.arange(128)
    for l in range(4):
        for i in range(3):
            for dc in range(8):
                c[:, C_LNG + (l * 3 + i) * 8 + dc] = ln_g[l, i, dc * 128:(dc + 1) * 128]
                c[:, C_LNB + (l * 3 + i) * 8 + dc] = ln_b[l, i, dc * 128:(dc + 1) * 128]
    s = p[:, None].astype(np.float64)
    q = p[None, :].astype(np.float64)
    d = q - s + 128
    c[:, C_DIST + 0:C_DIST + 128] = np.where(d <= 128, d, BIGDIST)
    d = np.abs(q - s)
    c[:, C_DIST + 128:C_DIST + 256] = d
    d = s + 128 - q
    c[:, C_DIST + 256:C_DIST + 384] = np.where(d <= 128, d, BIGDIST)
    d = 255 - s - q
    c[:, C_DIST + 384:C_DIST + 512] = np.where(d <= 128, d, BIGDIST)
    same = np.ones((128, 128), bool)
    maskF = (same & (p[:, None] <= p[None, :])).astype(np.float32)
    maskB = (same & (p[:, None] >= p[None, :])).astype(np.float32)
    c[:, C_MASKF:C_MASKF + 128] = maskF
    c[:, C_MASKB:C_MASKB + 128] = maskB
    c[:, C_MTRIF:C_MTRIF + 128] = -maskF
    c[:, C_MTRIB:C_MTRIB + 128] = -maskB
    c[:, C_MBLK:C_MBLK + 128] = -same.astype(np.float32)
    c[:, C_MIND0:C_MIND0 + 128] = -(p[:, None] < 64).astype(np.float32) * np.ones((1, 128), np.float32)
    c[:, C_MIND1:C_MIND1 + 128] = -(p[:, None] >= 64).astype(np.float32) * np.ones((1, 128), np.float32)
    c[:, C_IDENT:C_IDENT + 128] = np.eye(128, dtype=np.float32)
    c[:, C_ONES:C_ONES + 128] = 1.0
    c[:, C_SEL + (1 - odd)] = 1.0
    for j in range(2):
        gb = ml_gate_b[j].reshape(4, 4)
        if odd:
            gb = gb[[2, 3, 0, 1]]
        c[:, C_GBIAS + j * 16:C_GBIAS + (j + 1) * 16] = gb.reshape(1, 16)
        c[:, C_SINK + j * 16:C_SINK + (j + 1) * 16] = at_sink[j].reshape(1, 16)
    slopes = np.exp2(-8.0 * np.arange(1, 17, dtype=np.float32) / 16).astype(np.float32)
    c[:, C_SLOPE:C_SLOPE + 16] = (-8.0 * slopes).reshape(1, 16)
    return c


def alibi_slopes():
    return [float(np.exp2(np.float32(-8.0) * np.float32(h + 1) / np.float32(16))) for h in range(16)]


class Builder:
    def __init__(self, stages, n_w512, n_wo):
        self.stages = stages
        nc = bass.Bass("TRN2", target_bir_lowering=False)
        self.nc = nc
        self.P = Prog()
        self.d_x = nc.dram_tensor("xT", [128, DC * T], F32, kind="ExternalInput").ap()
        self.d_out = nc.dram_tensor("outT", [128, DC * T], F32, kind="ExternalOutput").ap()
        self.d_w512 = nc.dram_tensor("w512", [n_w512, 128, 4096], F32, kind="ExternalInput").ap()
        self.d_wo = nc.dram_tensor("wo", [n_wo, 128, FC * 128], F32, kind="ExternalInput").ap()
        self.d_wg = nc.dram_tensor("wg", [2, 128, DC * 16], F32, kind="ExternalInput").ap()
        self.d_normg = nc.dram_tensor("normg", [2, 128, 1024], F32, kind="ExternalInput").ap()
        self.d_const = nc.dram_tensor("consts", [128, NCONST], F32, kind="ExternalInput").ap()
        self.d_bounce_s = nc.dram_tensor("bounce_s", [128, 260], F32).ap()
        self.d_gath_s = nc.dram_tensor("gath_s", [256, 260], F32).ap()
        self.d_bounce_h = nc.dram_tensor("bounce_h", [128, 768], BF16).ap()
        self.d_gath_h = nc.dram_tensor("gath_h", [256, 768], BF16).ap()
        self.uid = 0

    def key(self, name):
        self.uid += 1
        return (name, self.uid)

    def carve_reset(self):
        self.arena_off = 0

    def carve(self, name, shape, dtype, key=None):
        n = 1
        for s in shape[1:]:
            n *= s
        nbytes = n * (2 if dtype == BF16 else 4)
        nbytes = (nbytes + 31) // 32 * 32
        off = self.arena_off
        assert off + nbytes <= self.ARENA_BYTES, (name, off, nbytes, self.ARENA_BYTES)
        self.arena_off += nbytes
        ap = self.arena[:, off // 4:(off + nbytes) // 4]
        if dtype == BF16:
            ap = ap.bitcast(BF16)[:, 0:n]
        else:
            ap = ap[:, 0:n]
        if len(shape) == 3:
            ap = ap.rearrange("p (a b) -> p a b", a=shape[1])
        elif len(shape) == 4:
            ap = ap.rearrange("p (a b c) -> p a b c", a=shape[1], b=shape[2])
        if shape[0] != 128:
            ap = ap[0:shape[0]]
        return TR(key if key is not None else self.key(name), ap)

    def eps_col(self, eps):
        if eps not in self.eps_cols:
            i = len(self.eps_cols)
            tr = TR(("epsc", i), self.epsbuf[:, i:i + 1])
            self.memset("pool", tr, float(eps))
            self.eps_cols[eps] = tr
        return self.eps_cols[eps]

    def mm(self, out, lhsT, rhs, start=True, stop=True):
        rd = [lhsT, rhs] + ([] if start else [out])
        self.P.add("pe", lambda e: e.matmul(out.ap, lhsT.ap, rhs.ap, start=start, stop=stop),
                   reads=rd, writes=[out])

    def transpose(self, out, in_, ident):
        self.P.add("pe", lambda e: e.transpose(out.ap, in_.ap, ident.ap), reads=[in_, ident], writes=[out])

    def act(self, out, in_, func, bias=None, scale=None, extra_reads=(), eng="act"):
        kw = {}
        rd = [in_] + list(extra_reads)
        if bias is not None:
            kw["bias"] = bias.ap if isinstance(bias, TR) else bias
            if isinstance(bias, TR):
                rd.append(bias)
        if scale is not None:
            kw["scale"] = scale.ap if isinstance(scale, TR) else scale
            if isinstance(scale, TR):
                rd.append(scale)
        self.P.add(eng, lambda e: e.activation(out.ap, in_.ap, func, **kw), reads=rd, writes=[out])

    def tt(self, eng, out, in0, in1, op):
        self.P.add(eng, lambda e: e.tensor_tensor(out.ap, in0.ap, in1.ap, op), reads=[in0, in1], writes=[out])

    def ts(self, eng, out, in0, s1, s2, op0, op1=None):
        rd = [in0]
        a1 = s1.ap if isinstance(s1, TR) else s1
        a2 = s2.ap if isinstance(s2, TR) else s2
        if isinstance(s1, TR):
            rd.append(s1)
        if isinstance(s2, TR):
            rd.append(s2)
        if op1 is None:
            self.P.add(eng, lambda e: e.tensor_scalar(out.ap, in0.ap, a1, None, op0), reads=rd, writes=[out])
        else:
            self.P.add(eng, lambda e: e.tensor_scalar(out.ap, in0.ap, a1, a2, op0, op1), reads=rd, writes=[out])

    def stt(self, eng, out, in0, scalar, in1, op0, op1):
        rd = [in0, in1]
        a = scalar.ap if isinstance(scalar, TR) else scalar
        if isinstance(scalar, TR):
            rd.append(scalar)
        self.P.add(eng, lambda e: e.scalar_tensor_tensor(out.ap, in0.ap, a, in1.ap, op0, op1),
                   reads=rd, writes=[out])

    def copy(self, eng, out, in_):
        if eng == "act":
            self.P.add("act", lambda e: e.activation(out.ap, in_.ap, AF.Copy), reads=[in_], writes=[out])
        else:
            self.P.add(eng, lambda e: e.tensor_copy(out.ap, in_.ap), reads=[in_], writes=[out])

    def memset(self, eng, out, val):
        self.P.add(eng, lambda e: e.memset(out.ap, val), reads=[], writes=[out])

    def dma(self, eng, out, in_, group, reads=(), writes=None):
        o = out.ap if isinstance(out, TR) else out
        i = in_.ap if isinstance(in_, TR) else in_
        rd = list(reads) + ([in_] if isinstance(in_, TR) else [])
        wr = ([out] if isinstance(out, TR) else []) if writes is None else writes
        self.P.add(eng, lambda e: e.dma_start(out=o, in_=i), reads=rd, writes=wr, dma=group)

    def load_w512(self, g, ncols=512):
        s = self.win_rr % len(self.win_slots)
        self.win_rr += 1
        slot = self.win_slots[s]
        if ncols == 512:
            self.dma("pool", slot, self.d_w512[g].rearrange("p (a b) -> p a b", a=DC), group=("win", s))
        else:
            self.dma("pool", slot[:, :, 0:ncols],
                     self.d_w512[g].rearrange("p (a b) -> p a b", a=DC)[:, :, 0:ncols], group=("win", s))
        return slot

    def load_wo(self, g, nparts=128, ncols=FC * 128):
        s = self.wo_rr % len(self.wo_slots)
        self.wo_rr += 1
        slot = self.wo_slots[s]
        self.dma("pool", slot[0:nparts, 0:ncols], self.d_wo[g][0:nparts, 0:ncols], group=("wo", s))
        return slot

    def build(self):
        nc = self.nc
        P = self.P
        with ExitStack() as es:
            def sb(name, shape, dt):
                return es.enter_context(nc.sbuf_tensor(name, shape, dt))

            xs = sb("xres", [128, DC, T], F32)
            cst = sb("cst", [128, NCONST], F32)
            cbf = sb("cbf", [128, 256], BF16)
            self.epsbuf = sb("epsbuf", [128, 8], F32)
            self.eps_cols = {}
            win = [sb("win%d" % i, [128, DC, 512], BF16) for i in range(3)]
            wo = [sb("wo%d" % i, [128, FC * 128], BF16) for i in range(2)]
            self.ARENA_BYTES = 100 * 1024 - 6144
            lnstat = sb("lnstat", [128, 3, 512], F32)
            self.ln_mean = TR("ln_mean", lnstat[:, 0, :])
            self.ln_msq = TR("ln_msq", lnstat[:, 1, :])
            self.ln_rstd = TR("ln_rstd", lnstat[:, 2, :])
            self.pend = []
            arena_t = sb("arena", [128, self.ARENA_BYTES // 4], F32)
            self.arena = arena_t[:]
            psum = [es.enter_context(nc.psum_tensor("ps%d" % i, [128, 512], F32)) for i in range(8)]
            engsem = {e: es.enter_context(nc.semaphore("sem_" + e)) for e in ENGS}
            dmasems = [es.enter_context(nc.semaphore("dsem%d" % i)) for i in range(60)]

            self.x = [[TR(("x", dc, tt), xs[:, dc, tt * 512:(tt + 1) * 512]) for tt in range(4)] for dc in range(DC)]
            self.cst = TR("cst", cst[:])
            self.ident_bf = TR("cbf", cbf[:, 0:128])
            self.ones_bf = TR("cbf", cbf[:, 128:256])
            self.win_slots = [TR(("win", i), win[i][:]) for i in range(3)]
            self.wo_slots = [TR(("wo", i), wo[i][:]) for i in range(2)]
            self.win_rr = 0
            self.wo_rr = 0
            self.ps = [TR(("ps", i), psum[i][:]) for i in range(8)]

            self.dma("sp", self.cst, self.d_const, group="cst")
            dx3 = self.d_x.rearrange("p (a b) -> p a b", a=DC)
            prev = []
            for tt in range(4):
                cur_ = []
                for hf in range(2):
                    o_ = xs[:, hf * 4:(hf + 1) * 4, tt * 512:(tt + 1) * 512]
                    i_ = dx3[:, hf * 4:(hf + 1) * 4, tt * 512:(tt + 1) * 512]
                    cur_.append(P.add("sp" if hf == 0 else "act", lambda e, o_=o_, i_=i_: e.dma_start(out=o_, in_=i_),
                                      writes=[self.x[dc][tt] for dc in range(hf * 4, hf * 4 + 4)],
                                      dma=("xin", tt, hf), extra_deps=prev))
                prev = cur_
            self.copy("dve", TR("cbf", cbf[:, 0:128]), self.cst[:, C_IDENT:C_IDENT + 128])
            self.copy("dve", TR("cbf", cbf[:, 128:256]), self.cst[:, C_ONES:C_ONES + 128])

            for si_, st in enumerate(self.stages):
                if si_ > 0:
                    P.fence()
                self.carve_reset()
                if st[0] == "ffn":
                    self.stage_ffn(st[1], st[2], st[3])
                elif st[0] == "attn":
                    self.stage_attn(st[1], st[2], st[3], st[4])
                elif st[0] == "mlstm":
                    self.stage_mlstm(st[1], st[2], st[3], st[4])

            self.drain()
            outs = []
            do3 = self.d_out.rearrange("p (a b) -> p a b", a=DC)
            for tt in range(4):
                for hf in range(2):
                    k = self.key("outdma")
                    outs.append(k)
                    self.dma("sp", do3[:, hf * 4:(hf + 1) * 4, tt * 512:(tt + 1) * 512],
                             TR(("xo", tt, hf), xs[:, hf * 4:(hf + 1) * 4, tt * 512:(tt + 1) * 512]),
                             group=("xout", tt, hf), reads=[self.x[dc][tt] for dc in range(hf * 4, hf * 4 + 4)], writes=[k])
            P.add("sp", None, reads=outs)

            with nc.Block() as block:
                P.emit(nc, block, engsem, dmasems)
        return nc

    def cast_xb(self, xb, tts):
        n = 0
        for i, tt in enumerate(tts):
            for dc in range(DC):
                eng = "act" if n % 2 == 0 else "dve"
                self.copy(eng, xb[dc][i], self.x[dc][tt])
                n += 1

    def outproj_ln(self, tts, kchunks, wo_group, wo_parts, wo_ncols, lhsT_of, resid_scale, ln_idx, eps,
                   scratch, defer=None, drain=None):
        zb, zsq, meanb, msq, rstd = scratch
        nk = len(kchunks(0))
        ps_y = [self.ps[0], self.ps[1]]
        ps_sum = [self.ps[4], self.ps[6]]
        ps_sq = [self.ps[5], self.ps[7]]
        pending = []

        def flush():
            for (i, c) in pending:
                self.mm(ps_sum[i], self.ones_bf, zb[(c * 2 + i) % len(zb)], start=(c == 0), stop=(c == DC - 1))
                self.mm(ps_sq[i], self.ones_bf, zsq[(c * 2 + i) % len(zsq)], start=(c == 0), stop=(c == DC - 1))
            pending.clear()

        n = 0
        for c in range(DC):
            if c == DC - 1 and drain:
                while drain:
                    drain.pop(0)()
            slot = self.load_wo(wo_group(c), wo_parts, wo_ncols)
            for i, tt in enumerate(tts):
                py = ps_y[n % 2]
                n += 1
                ks = kchunks(i)
                for k in range(nk):
                    self.mm(py, lhsT_of(slot, k), ks[k], start=(k == 0), stop=(k == nk - 1))
                flush()
                for _ in range(4):
                    if drain:
                        drain.pop(0)()
                xc = self.x[c][tt]
                self.stt("dve", xc, xc, resid_scale, py, ALU.mult, ALU.add)
                self.copy("act", zb[(c * 2 + i) % len(zb)], xc)
                self.act(zsq[(c * 2 + i) % len(zsq)], xc, AF.Square)
                pending.append((i, c))
        flush()
        g0 = C_LNG + ln_idx * 8
        b0 = C_LNB + ln_idx * 8
        th = []
        for i, tt in enumerate(tts):
            th.append((i, lambda i=i: self.act(meanb[i], ps_sum[i], AF.Copy, scale=1.0 / D)))
            th.append((i, lambda i=i: self.tt("dve", msq[i], meanb[i], meanb[i], ALU.mult)))
            th.append((i, lambda i=i: self.stt("dve", rstd[i], ps_sq[i], 1.0 / D, msq[i], ALU.mult, ALU.subtract)))
            th.append((i, lambda i=i: self.act(rstd[i], rstd[i], AF.Sqrt, bias=self.eps_col(eps))))
            th.append((i, lambda i=i: self.P.add("dve", lambda e, o=rstd[i]: e.reciprocal(o.ap, o.ap), reads=[rstd[i]], writes=[rstd[i]])))
        for i, tt in enumerate(tts):
            for c in range(DC):
                th.append((i, lambda i=i, tt=tt, c=c: self.tt("dve", self.x[c][tt], self.x[c][tt], meanb[i], ALU.subtract)))
            for c in range(DC):
                th.append((i, lambda i=i, tt=tt, c=c: self.tt("dve", self.x[c][tt], self.x[c][tt], rstd[i], ALU.mult)))
            for c in range(DC):
                th.append((i, lambda tt=tt, c=c: self.act(self.x[c][tt], self.x[c][tt], AF.Identity,
                                                           bias=self.cst[:, b0 + c:b0 + c + 1], scale=self.cst[:, g0 + c:g0 + c + 1])))
        if defer == "all":
            return [f_ for _, f_ in th]
        keep = []
        for i, f_ in th:
            if defer == "last" and i == len(tts) - 1:
                keep.append(f_)
            else:
                f_()
        return keep

    def ln_scratch(self, ntt=2, nz=4):
        zb = [self.carve("zb", [128, 512], BF16) for _ in range(nz)]
        zsq = [self.carve("zsq", [128, 512], BF16) for _ in range(nz)]
        if ntt == 1:
            return (zb, zsq, [self.ln_mean], [self.ln_msq], [self.ln_rstd])
        lm = self.carve("lmean", [128, 512], F32)
        lq = self.carve("lmsq", [128, 512], F32)
        lr = self.carve("lrstd", [128, 512], F32)
        return (zb, zsq, [lm, self.ln_mean], [lq, self.ln_msq], [lr, self.ln_rstd])

    def drain(self, n=None):
        while self.pend and (n is None or n > 0):
            self.pend.pop(0)()
            if n is not None:
                n -= 1

    def stage_ffn(self, g512_base, gwo_base, ln_idx):
        xb = [[self.carve("xb", [128, 512], BF16) for _ in range(2)] for _ in range(DC)]
        h = [[self.carve("h", [128, 512], BF16) for _ in range(2)] for _ in range(FC)]
        sg = [self.carve("sg", [128, 512], F32) for _ in range(2)]
        scratch = self.ln_scratch()
        self.eps_col(4.0 * LN_EPS)
        self.cast_xb(xb, [0, 1])
        n = 0
        for st in range(2):
            tts = [2 * st, 2 * st + 1]
            for jg in range(11):
                slot = self.load_w512(g512_base + jg)
                for i in range(2):
                    for half in range(2):
                        j = 2 * jg + half
                        pg = self.ps[(n % 2) * 2]
                        pu = self.ps[(n % 2) * 2 + 1]
                        n += 1
                        for dc in range(DC):
                            self.mm(pg, slot[:, dc, half * 128:(half + 1) * 128], xb[dc][i],
                                    start=(dc == 0), stop=(dc == DC - 1))
                        for dc in range(DC):
                            self.mm(pu, slot[:, dc, 256 + half * 128:256 + (half + 1) * 128], xb[dc][i],
                                    start=(dc == 0), stop=(dc == DC - 1))
                        s_ = sg[n % 2]
                        self.act(s_, pg, AF.Silu)
                        self.tt("dve", h[j][i], s_, pu, ALU.mult)
                        self.drain(2)
            self.drain()
            if st == 0:
                self.cast_xb(xb, [2, 3])
            self.pend = self.outproj_ln(tts, lambda i: [h[k][i] for k in range(FC)],
                                        lambda c: gwo_base + c, 128, FC * 128,
                                        lambda slot, k: slot[:, k * 128:(k + 1) * 128],
                                        2.0 * ALPHA, ln_idx, 4.0 * LN_EPS, scratch,
                                        defer=("all" if st == 0 else "last"))

    def exchange(self, bounce, gath, tag, srcs, rcv):
        kb = ("dram", tag[0], "b")
        kg = ("dram", tag[0], "g")
        for (tr, c0, ncol) in srcs:
            self.dma("sp", bounce[:, c0:c0 + ncol], tr, group=("bnc", tag), writes=[kb])
        self.P.add("pool", lambda e: e.collective_compute(
            "AllGather", ALU.bypass, replica_groups=[[0, 1], [2, 3], [4, 5], [6, 7]],
            ins=[bounce.opt()], outs=[gath.opt()]), reads=[kb], writes=[kg], dma=("cc", tag), inc=1)
        self.dma("sp", rcv, gath.rearrange("(r p) n -> p r n", p=128), group=("rcv", tag), reads=[kg])

    def select_partner(self, dst, rcv, c0, ncol):
        sel0 = self.cst[:, C_SEL:C_SEL + 1]
        sel1 = self.cst[:, C_SEL + 1:C_SEL + 2]
        self.ts("dve", dst, rcv[:, 0, c0:c0 + ncol], sel0, None, ALU.mult)
        self.stt("dve", dst, rcv[:, 1, c0:c0 + ncol], sel1, dst, ALU.mult, ALU.add)

    def stage_attn(self, g512_base, gwo_base, ln_idx, j):
        P = self.P
        qT_all = self.carve("qT", [128, 8, T], BF16)
        qk = [[self.key("qT") for _ in range(4)] for _ in range(8)]
        kT_all = self.carve("kT", [128, 4, T + 128], BF16)
        kk = [[self.key("kT") for _ in range(5)] for _ in range(4)]
        V_all = self.carve("V", [128, 17, 256], BF16)
        Vb = [TR(self.key("V"), V_all.ap[:, b, :]) for b in range(17)]
        esink = self.carve("esink", [128, 16], F32)
        mark = self.arena_off
        xb = [[self.carve("xb", [128, 512], BF16) for _ in range(2)] for _ in range(DC)]
        rcv = self.carve("rcvh", [128, 2, 768], BF16)

        self.act(esink, self.cst[:, C_SINK + j * 16:C_SINK + (j + 1) * 16], AF.Exp)
        n = 0
        for st in range(2):
            tts = [2 * st, 2 * st + 1]
            if st == 1:
                self.drain()
            self.cast_xb(xb, tts)
            sA = self.load_w512(g512_base + 0)
            sB = self.load_w512(g512_base + 1)
            sC = self.load_w512(g512_base + 2)
            for c in range(8):
                slot = sA if c < 4 else sB
                for i, tt in enumerate(tts):
                    ps = self.ps[n % 4]
                    for dc in range(DC):
                        self.mm(ps, slot[:, dc, (c % 4) * 128:(c % 4 + 1) * 128], xb[dc][i],
                                start=(dc == 0), stop=(dc == DC - 1))
                    dst = TR(qk[c][tt], qT_all.ap[:, c, tt * 512:(tt + 1) * 512])
                    self.copy("act" if n % 2 == 0 else "dve", dst, ps)
                    n += 1
                    self.drain(4)
                if c == 3:
                    sV = self.load_w512(g512_base + 3, 256)
            for k in range(4):
                for i, tt in enumerate(tts):
                    ps = self.ps[n % 4]
                    for dc in range(DC):
                        self.mm(ps, sC[:, dc, k * 128:(k + 1) * 128], xb[dc][i], start=(dc == 0), stop=(dc == DC - 1))
                    dst = TR(kk[k][tt], kT_all.ap[:, k, tt * 512:(tt + 1) * 512])
                    self.copy("act" if n % 2 == 0 else "dve", dst, ps)
                    n += 1
            for i, tt in enumerate(tts):
                for tile in range(4):
                    ps = self.ps[n % 4]
                    for dc in range(DC):
                        self.mm(ps[:, 0:256], xb[dc][i][:, tile * 128:(tile + 1) * 128], sV[:, dc, 0:256],
                                start=(dc == 0), stop=(dc == DC - 1))
                    self.copy("act" if n % 2 == 0 else "dve", Vb[tt * 4 + tile], ps[:, 0:256])
                    n += 1
        if DBG == 1:
            return
        self.drain()
        srcs = [(TR(kk[k][3], kT_all.ap[:, k, T - 128:T]), k * 128, 128) for k in range(4)]
        srcs.append((Vb[15], 512, 256))
        self.exchange(self.d_bounce_h, self.d_gath_h, ("h", j), srcs, rcv)
        for k in range(4):
            self.select_partner(TR(kk[k][4], kT_all.ap[:, k, T:T + 128]), rcv, k * 128, 128)
        self.select_partner(Vb[16], rcv, 512, 256)
        P.fence()
        if DBG == 2:
            return
        self.arena_off = mark
        oT_all = self.carve("oT", [64, 16, 512], BF16)
        qz = self.carve("qz", [128, 8, 2, 128], BF16)
        self.memset("pool", qz, 0.0)
        Pm = [self.carve("Pm", [128, 512], BF16) for _ in range(3)]
        dz_off = self.arena_off
        dsum = [self.carve("dsum", [64, 512], F32) for _ in range(2)]
        end_off = self.arena_off
        self.arena_off = dz_off
        zb = [self.carve("zb", [128, 512], BF16) for _ in range(2)]
        zsq = [self.carve("zsq", [128, 512], BF16) for _ in range(2)]
        assert self.arena_off == end_off
        meanb = [self.ln_mean]
        scratch = (zb, zsq, [self.ln_mean], [self.ln_msq], [self.ln_rstd])
        diag = self.carve("diag", [128, 16, 2, 128], BF16)
        distbf = self.carve("distbf", [128, 4, 128], BF16)
        slopes = alibi_slopes()
        for kd in range(4):
            self.copy("dve", distbf[:, kd, :], self.cst[:, C_DIST + kd * 128:C_DIST + (kd + 1) * 128])
        t32 = meanb[0]
        for h in range(16):
            ta = t32[:, (h % 2) * 256:(h % 2) * 256 + 128]
            tb = t32[:, (h % 2) * 256 + 128:(h % 2) * 256 + 256]
            self.ts("dve", ta, self.cst[:, C_IDENT:C_IDENT + 128], float(np.float32(-8.0 * slopes[h])), None, ALU.mult)
            self.copy("act", diag[:, h, 0, :], ta)
            self.tt("dve", tb, ta, diag[:, h, 0, :], ALU.subtract)
            self.copy("act", diag[:, h, 1, :], tb)
        ones64 = self.ones_bf[:, 0:64]
        att_pend = []
        ps_s_t = [self.ps[0], self.ps[1]]
        ps_o_t = [self.ps[2], self.ps[3], self.ps[4]]
        ps_d_t = [self.ps[5], self.ps[6]]
        for st in range(4):
            ok = [self.key("oT") for _ in range(16)]
            subs = []
            u = 0
            for nb in range(4):
                nq = st * 4 + nb
                for k in range(4):
                    blocks = []
                    if nq > 0:
                        blocks.append((nq - 1, 0))
                    blocks.append((nq, 1))
                    blocks.append((nq + 1, 2) if nq < 15 else (16, 3))
                    for bi, (kb, kind) in enumerate(blocks):
                        subs.append(dict(nq=nq, nb=nb, k=k, kb=kb, kind=kind, first=(bi == 0),
                                         last=(bi == len(blocks) - 1), u=u, newq=(k == 0 and bi == 0)))
                    u += 1
            NS = len(subs)

            def A_Q(t):
                d = subs[t]
                if not d["newq"]:
                    return
                nq = d["nq"]
                for e_ in range(2):
                    src_ap = qT_all.ap[e_ * 64:(e_ + 1) * 64, :, nq * 128:(nq + 1) * 128]
                    dst_ap = qz.ap[e_ * 64:(e_ + 1) * 64, :, e_, :]
                    P.add("act", lambda e, d_=dst_ap, a=src_ap: e.activation(d_, a, AF.Copy),
                          reads=[qk[c][nq // 4] for c in range(8)], writes=[qz])

            def A_S(t):
                d = subs[t]
                k, kb = d["k"], d["kb"]
                ps_s = ps_s_t[t % 2]
                kkey = kk[k][kb // 4] if kb < 16 else kk[k][4]
                for hh in range(4):
                    c_ = 2 * k + hh // 2
                    e_ = hh % 2
                    h = 4 * k + hh
                    o_ap = ps_s.ap[:, hh * 128:(hh + 1) * 128]
                    l_ap = kT_all.ap[:, k, kb * 128:(kb + 1) * 128]
                    r_ap = qz.ap[:, c_, e_, :]
                    P.add("pe", lambda e, o_ap=o_ap, l_ap=l_ap, r_ap=r_ap: e.matmul(o_ap, l_ap, r_ap, start=True, stop=False),
                          reads=[kkey, qz], writes=[ps_s])
                    for hl in range(2):
                        d_ap = diag.ap[:, h, hl, :]
                        b_ap = distbf.ap[:, d["kind"], :]
                        P.add("pe", lambda e, o_ap=o_ap, d_ap=d_ap, b_ap=b_ap, hl=hl: e.matmul(o_ap, d_ap, b_ap, start=False, stop=(hl == 1)),
                              reads=[diag, distbf, ps_s], writes=[ps_s])

            def A_E(t):
                self.act(Pm[t % 3], ps_s_t[t % 2], AF.Exp, scale=0.125)

            def A_PV(t):
                d = subs[t]
                k, kb = d["k"], d["kb"]
                ps_o = ps_o_t[d["u"] % 3]
                ps_d = ps_d_t[d["u"] % 2]
                self.mm(ps_o[0:64, :], Vb[kb][:, k * 64:(k + 1) * 64], Pm[t % 3], start=d["first"], stop=d["last"])
                self.mm(ps_d[0:64, :], ones64, Pm[t % 3], start=d["first"], stop=d["last"])

            def A_N1(t):
                d = subs[t]
                if not d["last"]:
                    return
                ds = dsum[d["u"] % 2]
                ps_d = ps_d_t[d["u"] % 2]
                for hh in range(4):
                    h = 4 * d["k"] + hh
                    self.ts("dve", ds[:, hh * 128:(hh + 1) * 128], ps_d[0:64, hh * 128:(hh + 1) * 128],
                            esink[0:64, h:h + 1], None, ALU.add)

            def A_N2(t):
                d = subs[t]
                if not d["last"]:
                    return
                ds = dsum[d["u"] % 2]
                self.act(ds, ds, AF.Ln)
                self.act(ds, ds, AF.Exp, scale=-1.0)

            def A_N3(t):
                d = subs[t]
                if not d["last"]:
                    return
                ds = dsum[d["u"] % 2]
                ps_o = ps_o_t[d["u"] % 3]
                for hh in range(4):
                    h = 4 * d["k"] + hh
                    dst = TR(ok[h], oT_all.ap[:, h, d["nb"] * 128:(d["nb"] + 1) * 128])
                    self.tt("dve", dst, ps_o[0:64, hh * 128:(hh + 1) * 128], ds[:, hh * 128:(hh + 1) * 128], ALU.mult)

            stages_ = [(0, A_Q), (1, A_S), (2, A_E), (3, A_PV), (4, A_N1), (5, A_N2), (6, A_N3)]
            for _ in range(5):
                if att_pend:
                    att_pend.pop(0)()
            for it in range(NS + 6):
                for l_, fn_ in sorted(stages_, key=lambda x: -x[0]):
                    t = it - l_
                    if 0 <= t < NS:
                        fn_(t)
                if att_pend:
                    att_pend.pop(0)()
            while att_pend:
                att_pend.pop(0)()
            att_pend = self.outproj_ln([st], lambda i: [TR(ok[h], oT_all.ap[:, h, :]) for h in range(16)],
                                       lambda c: gwo_base + c, 64, 16 * 128,
                                       lambda slot, k: slot[0:64, k * 128:(k + 1) * 128],
                                       ALPHA, ln_idx, LN_EPS, scratch, defer="all")
        self.pend = att_pend

    def stage_mlstm(self, g512_base, gwo_base, ln_idx, j):
        P = self.P
        AX = mybir.AxisListType.X
        hT_all = self.carve("hT", [128, 8, T], BF16)
        hk = [[self.key("hT") for _ in range(4)] for _ in range(8)]
        xb = [self.carve("xb", [128, 512], BF16) for _ in range(DC)]
        gtmp = self.carve("gtmp", [128, 512], F32)

        def gview(name, k_):
            return TR(self.key(name), gtmp.ap[:, k_ * 128:(k_ + 1) * 128].rearrange("p (t d h) -> p t d h", t=16, d=2))
        lf = gview("lf", 0)
        etmp = gview("etmp", 1)
        gg = gview("gg", 2)
        bb = gview("bb", 3)
        eb = self.carve("eb", [128, 16, 2, 4], F32)
        emg = self.carve("emg", [128, 16, 2, 4], F32)
        egb = self.carve("egb", [128, 16, 2, 4], F32)
        eG = self.carve("eG", [128, 16, 2, 4], F32)
        normg = self.carve("normg", [128, 256], F32)
        wgs = self.carve("wgs", [128, 8, 16], BF16)
        mark = self.arena_off
        qT = self.carve("qTh", [128, T], BF16)
        kT = self.carve("kTh", [128, T], BF16)
        kTM = self.carve("kTM", [128, 16, 128], BF16)
        vaug = self.carve("vaug", [128, 16, 264], BF16)
        so = self.carve("so", [128, 16, 256], BF16)
        h1 = self.carve("h1", [128, 16, 256], BF16)
        gp = TR(self.key("gp"), h1.ap.rearrange("p t c -> p (t c)")[:, 0:512].bitcast(F32).rearrange("p (t g) -> p t g", t=16))
        Am = [self.carve("Am", [128, 128], BF16) for _ in range(2)]
        vp = [self.carve("vp", [128, 258], BF16) for _ in range(2)]
        vpp = [self.carve("vpp", [128, 258], BF16) for _ in range(2)]
        C32 = [self.carve("C", [128, 260], F32) for _ in range(2)]
        Cb = [self.carve("Cb", [128, 264], BF16) for _ in range(3)]
        rcv = self.carve("rcvs", [128, 2, 260], F32)
        dd = [self.carve("dd", [128, 2], F32) for _ in range(4)]
        hs = [self.carve("hs", [128, 256], F32) for _ in range(2)]
        hs.append(TR(self.key("hsA"), gtmp.ap[:, 0:256]))
        hs.append(TR(self.key("hsB"), gtmp.ap[:, 256:512]))
        st4 = [self.carve("st4", [128, 8], F32) for _ in range(4)]
        st6 = [self.carve("st6", [128, 8], F32) for _ in range(4)]
        hb = [self.carve("hb", [128, 256], BF16) for _ in range(2)]
        one_col = self.eps_col(1.0)
        eps_col = self.eps_col(LN_EPS)
        KS = float(128 ** -0.5)

        if DBG not in (21, 22, 23):
            self.dma("pool", wgs, self.d_wg[j].rearrange("p (a b) -> p a b", a=DC), group="wgs")
        self.memset("pool", vaug, 1.0)
        gp4 = gp.ap.rearrange("p t (k h) -> p t k h", k=4)
        psT = TR(self.ps[7].key, self.ps[7].ap.bitcast(BF16))

        cnt = 0
        xb_sets = [xb, [TR(("win", 2), self.win_slots[2].ap[:, dc, :]) for dc in range(DC)]]
        for h in range(4):
            self.dma("sp", normg, self.d_normg[j][:, h * 256:(h + 1) * 256], group="normg")
            self.win_rr = 0
            sW = self.load_w512(g512_base + 2 * h)
            sO = self.load_w512(g512_base + 2 * h + 1, 256)
            for tt in range(4):
                xb = xb_sets[tt % 2]
                if tt == 2:
                    self.drain()
                for dc in range(DC):
                    self.copy("act" if dc % 2 == 0 else "dve", xb[dc], self.x[dc][tt])
                ps = self.ps[cnt % 4]; cnt += 1
                for dc in range(DC):
                    self.mm(ps, sW[:, dc, 0:128], xb[dc], start=(dc == 0), stop=(dc == DC - 1))
                self.copy("act", qT[:, tt * 512:(tt + 1) * 512], ps)
                ps = self.ps[cnt % 4]; cnt += 1
                for dc in range(DC):
                    self.mm(ps, sW[:, dc, 128:256], xb[dc], start=(dc == 0), stop=(dc == DC - 1))
                self.act(kT[:, tt * 512:(tt + 1) * 512], ps, AF.Copy, scale=KS)
                for tile in range(4):
                    if DBG == 22:
                        continue
                    i = tt * 4 + tile
                    ps = self.ps[cnt % 4]; cnt += 1
                    for dc in range(DC):
                        self.mm(ps[:, 0:128], xb[dc][:, tile * 128:(tile + 1) * 128], sW[:, dc, 128:256],
                                start=(dc == 0), stop=(dc == DC - 1))
                    self.act(kTM[:, i, :], ps[:, 0:128], AF.Copy, scale=KS)
                    ps = self.ps[cnt % 4]; cnt += 1
                    for dc in range(DC):
                        self.mm(ps[:, 0:256], xb[dc][:, tile * 128:(tile + 1) * 128], sW[:, dc, 256:512],
                                start=(dc == 0), stop=(dc == DC - 1))
                    self.copy("dve", vaug[:, i, 0:256], ps[:, 0:256])
                    if DBG == 23:
                        continue
                    ps = self.ps[cnt % 4]; cnt += 1
                    for dc in range(DC):
                        self.mm(ps[:, 0:256], xb[dc][:, tile * 128:(tile + 1) * 128], sO[:, dc, 0:256],
                                start=(dc == 0), stop=(dc == DC - 1))
                    self.act(so[:, i, :], ps[:, 0:256], AF.Sigmoid)
                    self.drain(4)
                    if h == 0 and DBG not in (21, 22, 23):
                        ps = self.ps[cnt % 4]; cnt += 1
                        for dc in range(DC):
                            self.mm(ps[:, 0:16], xb[dc][:, tile * 128:(tile + 1) * 128], wgs[:, dc, :],
                                    start=(dc == 0), stop=(dc == DC - 1))
                        self.tt("dve", gp[:, i, :], ps[:, 0:16], self.cst[:, C_GBIAS + j * 16:C_GBIAS + (j + 1) * 16], ALU.add)
            if DBG in (11, 21, 22, 23):
                return
            if h == 0:
                for dr in range(2):
                    src = TR(gp.key, gp4[:, :, 1 + 2 * dr, :])
                    self.act(etmp[:, :, dr, :], src, AF.Exp, scale=-1.0)
                    self.act(lf[:, :, dr, :], etmp[:, :, dr, :], AF.Ln, bias=one_col)
                lf2 = TR(lf.key, lf.ap.rearrange("p t d h -> p (t d h)"))
                pg = self.ps[4]
                pgF = TR(pg.key, pg.ap[:, 0:128].rearrange("p (t d h) -> p t d h", t=16, d=2))
                pgB = TR(pg.key, pg.ap[:, 128:256].rearrange("p (t d h) -> p t d h", t=16, d=2))
                self.mm(pg[:, 0:128], self.cst[:, C_MTRIF:C_MTRIF + 128], lf2)
                self.mm(pg[:, 128:256], self.cst[:, C_MTRIB:C_MTRIB + 128], lf2)
                self.mm(self.ps[5][:, 0:128], self.cst[:, C_MBLK:C_MBLK + 128], lf2)
                self.copy("dve", gg[:, :, 0, :], pgF[:, :, 0, :])
                self.copy("dve", gg[:, :, 1, :], pgB[:, :, 1, :])
                for dr in range(2):
                    self.tt("dve", bb[:, :, dr, :], TR(gp.key, gp4[:, :, 2 * dr, :]), gg[:, :, dr, :], ALU.subtract)
                self.act(eb, bb, AF.Exp)
                self.act(emg, gg, AF.Exp, scale=-1.0)
                p5 = TR(self.ps[5].key, self.ps[5].ap[:, 0:128].rearrange("p (t d h) -> p t d h", t=16, d=2))
                self.tt("dve", egb, p5, bb, ALU.add)
                self.act(egb, egb, AF.Exp)
                self.act(eG, p5, AF.Exp)
            if DBG == 12:
                return
            LAG = 2
            RING = len(Cb)
            for dirn in range(2):
                mask = self.cst[:, C_MASKF:C_MASKF + 128] if dirn == 0 else self.cst[:, C_MASKB:C_MASKB + 128]
                if dirn == 0:
                    self.memset("dve", C32[0], 0.0)
                    self.memset("dve", Cb[0], 0.0)
                order = list(range(16)) if dirn == 0 else list(range(15, -1, -1))

                psA_t = [self.ps[0][:, 0:128], self.ps[1][:, 0:128]]
                pso_t = [self.ps[2], self.ps[3], self.ps[6]]

                def tsl_(t):
                    i = order[t]
                    return i, slice(i * 128, (i + 1) * 128)

                def V0(t):
                    i, _ = tsl_(t)
                    self.ts("dve", vpp[t % 2][:, 0:257], vaug[:, i, 0:257], egb[:, i, dirn, h:h + 1], None, ALU.mult)

                def L1(t):
                    i, tsl = tsl_(t)
                    self.mm(psA_t[t % 2], kT[:, tsl], qT[:, tsl])
                    self.mm(self.ps[4 + t % 2][:, 0:257], kTM[:, i, :], vpp[t % 2][:, 0:257])

                def L2(t):
                    i, tsl = tsl_(t)
                    ps_U = self.ps[4 + t % 2]
                    self.stt("dve", C32[(t + 1) % 2][:, 0:257], C32[t % 2][:, 0:257], eG[:, i, dirn, h:h + 1],
                             ps_U[:, 0:257], ALU.mult, ALU.add)
                    self.copy("act", Cb[(t + 1) % RING][:, 0:257], C32[(t + 1) % 2][:, 0:257])
                    self.tt("dve", Am[t % 2], psA_t[t % 2], mask, ALU.mult)
                    self.act(vp[t % 2][:, 0:257], vaug[:, i, 0:257], AF.Copy, scale=eb[:, i, dirn, h:h + 1])

                def L3(t):
                    i, tsl = tsl_(t)
                    ps_o = pso_t[t % 3]
                    self.mm(ps_o[:, 0:257], Am[t % 2], vp[t % 2][:, 0:257], start=True, stop=False)
                    self.mm(ps_o[:, 0:257], qT[:, tsl], Cb[t % RING][:, 0:257], start=False, stop=True)
                    self.act(dd[t % 4][:, 0:1], ps_o[:, 256:257], AF.Abs)

                def L4(t):
                    i, tsl = tsl_(t)
                    ps_o = pso_t[t % 3]
                    d_ = dd[t % 4]
                    self.ts("dve", d_[:, 0:1], d_[:, 0:1], emg[:, i, dirn, h:h + 1], None, ALU.max)
                    P.add("dve", lambda e, o=d_: e.reciprocal(o.ap[:, 1:2], o.ap[:, 0:1]), reads=[d_], writes=[d_])
                    if dirn == 0:
                        self.ts("dve", h1[:, i, :], ps_o[:, 0:256], d_[:, 1:2], None, ALU.mult)
                    else:
                        hs_ = hs[t % 4]
                        s6 = st6[t % 4]
                        self.stt("dve", hs_, ps_o[:, 0:256], d_[:, 1:2], h1[:, i, :], ALU.mult, ALU.add)
                        P.add("dve", lambda e, o=s6, a=hs_: e.bn_stats(o.ap[:, 0:6], a.ap), reads=[hs_], writes=[s6])

                def L5(t):
                    s6 = st6[t % 4]
                    s4 = st4[t % 4]
                    P.add("dve", lambda e, o=s4, a=s6: e.bn_aggr(o.ap[:, 0:2], a.ap[:, 0:6]), reads=[s6], writes=[s4])
                    self.act(s4[:, 2:3], s4[:, 1:2], AF.Sqrt, bias=eps_col)

                def L6(t):
                    hs_ = hs[t % 4]
                    s4 = st4[t % 4]
                    P.add("dve", lambda e, o=s4: e.reciprocal(o.ap[:, 3:4], o.ap[:, 2:3]), reads=[s4], writes=[s4])
                    self.stt("dve", s4[:, 4:5], s4[:, 0:1], -1.0, s4[:, 3:4], ALU.mult, ALU.mult)
                    self.act(hs_, hs_, AF.Identity, bias=s4[:, 4:5], scale=s4[:, 3:4])

                def L7(t):
                    hs_ = hs[t % 4]
                    self.tt("pool", hs_, hs_, normg, ALU.mult)

                def L7b(t):
                    i, _ = tsl_(t)
                    self.tt("pool", hb[t % 2], hs[t % 4], so[:, i, :], ALU.mult)

                def L8(t):
                    i, _ = tsl_(t)
                    hb_ = hb[t % 2]
                    for k2 in range(2):
                        self.transpose(psT[:, k2 * 128:(k2 + 1) * 128], hb_[:, k2 * 128:(k2 + 1) * 128], self.ident_bf)
                    for k2 in range(2):
                        dst = TR(hk[2 * h + k2][i // 4], hT_all.ap[:, 2 * h + k2, i * 128:(i + 1) * 128])
                        self.copy("act", dst, psT[:, k2 * 128:(k2 + 1) * 128])

                stages_ = [(0, V0), (1, L1), (2, L2), (3, L3), (4, L4)]
                if dirn == 1:
                    stages_ += [(5, L5), (6, L6), (7, L7), (8, L7b), (9, L8)]
                maxlag = max(l_ for l_, _ in stages_)
                for it in range(16 + maxlag):
                    for l_, fn_ in sorted(stages_, key=lambda x: -x[0]):
                        t = it - l_
                        if 0 <= t < 16:
                            fn_(t)
                if DBG == 13 or (DBG == 15 and dirn == 1):
                    return
                if dirn == 0:
                    self.exchange(self.d_bounce_s, self.d_gath_s, ("s", j, h), [(C32[0], 0, 260)], rcv)
                    self.select_partner(C32[0], rcv, 0, 260)
                    self.copy("act", Cb[0][:, 0:257], C32[0][:, 0:257])
                    if DBG == 14:
                        return
        P.fence()
        self.arena_off = mark
        scratch = self.ln_scratch()
        ml_pend = []
        for st in range(2):
            tts = [2 * st, 2 * st + 1]
            ml_pend = self.outproj_ln(tts, lambda i: [TR(hk[k][tts[i]], hT_all.ap[:, k, tts[i] * 512:(tts[i] + 1) * 512]) for k in range(8)],
                                      lambda c: gwo_base + c, 128, 8 * 128,
                                      lambda slot, k: slot[:, k * 128:(k + 1) * 128],
                                      ALPHA, ln_idx, LN_EPS, scratch, defer=("all" if st == 0 else "last"), drain=ml_pend)
        self.pend = ml_pend


def _lay_w512(w, cols):
    out = np.zeros((128, DC, 512), np.float32)
    out[:, :, :len(cols)] = w[:, cols].reshape(DC, 128, len(cols)).transpose(1, 0, 2)
    return out.reshape(128, DC * 512)


def _lay_wo(w, c, kp, nk):
    out = np.zeros((128, FC * 128), np.float32)
    blk = w[:, c * 128:(c + 1) * 128].reshape(nk, kp, 128).transpose(1, 0, 2).reshape(kp, nk * 128)
    out[:kp, :nk * 128] = blk
    return out


def _prepare(inputs, stages_spec):
    x = np.asarray(inputs["x"], np.float32)
    ffn_w_in = np.asarray(inputs["ffn_w_in"], np.float32)
    ffn_w_out = np.asarray(inputs["ffn_w_out"], np.float32)
    w512 = []
    wo = []
    stages = []
    for spec in stages_spec:
        if spec[0] == "ffn":
            l, i = spec[1], spec[2]
            base512 = len(w512)
            wi = ffn_w_in[l, i]
            for jg in range(11):
                cols = list(range(jg * 256, (jg + 1) * 256)) + list(range(DFF + jg * 256, DFF + (jg + 1) * 256))
                w512.append(_lay_w512(wi, cols))
            basewo = len(wo)
            for c in range(DC):
                wo.append(_lay_wo(ffn_w_out[l, i], c, 128, FC))
            stages.append(("ffn", base512, basewo, l * 3 + (0 if i == 0 else 2)))
        elif spec[0] == "attn":
            j = spec[1]
            w = np.asarray(inputs["at_w_in"], np.float32)[j]
            base512 = len(w512)
            w512.append(_lay_w512(w, list(range(0, 512))))
            w512.append(_lay_w512(w, list(range(512, 1024))))
            kc = []
            for k in range(4):
                kc += list(range(1024 + k * 64, 1024 + (k + 1) * 64)) * 2
            w512.append(_lay_w512(w, kc))
            w512.append(_lay_w512(w, list(range(1280, 1536))))
            basewo = len(wo)
            wout = np.asarray(inputs["at_w_out"], np.float32)[j]
            for c in range(DC):
                wo.append(_lay_wo(wout, c, 64, 16))
            stages.append(("attn", base512, basewo, (2 * j + 1) * 3 + 1, j))
        elif spec[0] == "mlstm":
            j = spec[1]
            w = np.asarray(inputs["ml_w_in"], np.float32)[j]
            base512 = len(w512)
            for h in range(4):
                cols = (list(range(h * 128, (h + 1) * 128)) + list(range(512 + h * 128, 512 + (h + 1) * 128))
                        + list(range(1024 + h * 256, 1024 + (h + 1) * 256)))
                w512.append(_lay_w512(w, cols))
                w512.append(_lay_w512(w, list(range(2048 + h * 256, 2048 + (h + 1) * 256))))
            basewo = len(wo)
            wout = np.asarray(inputs["ml_w_out"], np.float32)[j]
            for c in range(DC):
                wo.append(_lay_wo(wout, c, 128, 8))
            stages.append(("mlstm", base512, basewo, (2 * j) * 3 + 1, j))
        else:
            raise NotImplementedError(spec)
    w512 = np.stack(w512) if w512 else np.zeros((1, 128, 4096), np.float32)
    wo = np.stack(wo) if wo else np.zeros((1, 128, FC * 128), np.float32)
    ml_w_in = np.asarray(inputs["ml_w_in"], np.float32)
    wg_even = np.zeros((2, 128, DC * 16), np.float32)
    wg_odd = np.zeros((2, 128, DC * 16), np.float32)
    for j in range(2):
        g = ml_w_in[j][:, 3072:3088]
        wg_even[j] = g.reshape(DC, 128, 16).transpose(1, 0, 2).reshape(128, DC * 16)
        g2 = g.reshape(1024, 4, 4)[:, [2, 3, 0, 1], :].reshape(1024, 16)
        wg_odd[j] = g2.reshape(DC, 128, 16).transpose(1, 0, 2).reshape(128, DC * 16)
    normg = np.ascontiguousarray(np.broadcast_to(np.asarray(inputs["ml_norm_g"], np.float32)[:, None, :], (2, 128, 1024)))
    in_maps = []
    for core in range(NCORES):
        b, half = core // 2, core % 2
        xs = x[b, half * T:(half + 1) * T]
        if half:
            xs = xs[::-1]
        xT = np.ascontiguousarray(xs.reshape(T, DC, 128).transpose(2, 1, 0)).reshape(128, DC * T)
        consts = _const_table(core, np.asarray(inputs["ln_g"], np.float32), np.asarray(inputs["ln_b"], np.float32),
                              np.asarray(inputs["ml_gate_b"], np.float32), np.asarray(inputs["at_sink"], np.float32))
        in_maps.append({"xT": xT, "w512": w512, "wo": wo, "wg": wg_odd if half else wg_even, "normg": normg,
                        "consts": consts})
    return stages, in_maps, w512.shape[0], wo.shape[0]


def _assemble(results):
    out = np.zeros((4, 2 * T, D), np.float32)
    for core in range(NCORES):
        b, half = core // 2, core % 2
        o = np.asarray(results[core]["outT"]).reshape(128, DC, T).transpose(2, 1, 0).reshape(T, D)
        if half:
            o = o[::-1]
        out[b, half * T:(half + 1) * T] = o
    return out


FULL_SPEC = []
for _l in range(DEPTH):
    FULL_SPEC.append(("ffn", _l, 0))
    FULL_SPEC.append(("mlstm", _l // 2) if _l % 2 == 0 else ("attn", _l // 2))
    FULL_SPEC.append(("ffn", _l, 1))


def run_spec(inputs, spec, trace=False):
    stages, in_maps, n512, nwo = _prepare(inputs, spec)
    b = Builder(stages, n512, nwo)
    nc = b.build()
    res = run_bass_kernel_spmd(nc, in_maps, core_ids=list(range(NCORES)))
    return _assemble(res.results)


def kernel(**inputs):
    return run_spec(inputs, FULL_SPEC)
```

```python
import numpy as np
from contextlib import ExitStack
import concourse.bass as bass
import concourse.mybir as mybir
from concourse.bass_utils import run_bass_kernel_spmd

F32 = mybir.dt.float32
BF16 = mybir.dt.bfloat16
AF = mybir.ActivationFunctionType
ALU = mybir.AluOpType

D = 1024
DC = 8
T = 2048
NT = 16
DFF = 2816
FC = 22
DEPTH = 4
ALPHA = float((2 * DEPTH) ** 0.25)
LN_EPS = 1e-5
NCORES = 8
import os
DBG = int(os.environ.get("KDBG", "0"))


class TR:
    __slots__ = ("key", "ap")

    def __init__(self, key, ap):
        self.key = key
        self.ap = ap

    def __getitem__(self, idx):
        return TR(self.key, self.ap[idx])


class Op:
    __slots__ = ("eng", "fn", "dma", "deps", "needs_inc", "tick", "inc")

    def __init__(self, eng, fn, dma, inc=16):
        self.eng = eng
        self.fn = fn
        self.dma = dma
        self.inc = inc
        self.deps = ()
        self.needs_inc = False
        self.tick = 0


ENGS = ("pe", "act", "dve", "pool", "sp")


class Prog:
    def __init__(self):
        self.ops = []
        self.last_w = {}
        self.readers = {}
        self.last_on = {}

    def add(self, eng, fn, reads=(), writes=(), dma=None, extra_deps=(), inc=16):
        idx = len(self.ops)
        op = Op(eng, fn, dma, inc)
        deps = set(extra_deps)
        rk = [r.key if isinstance(r, TR) else r for r in reads]
        wk = [w.key if isinstance(w, TR) else w for w in writes]
        for k in rk:
            if k in self.last_w:
                deps.add(self.last_w[k])
        for k in wk:
            if k in self.last_w:
                deps.add(self.last_w[k])
            deps.update(self.readers.get(k, ()))
        op.deps = deps
        for k in rk:
            self.readers.setdefault(k, []).append(idx)
        for k in wk:
            self.last_w[k] = idx
            self.readers[k] = []
        self.ops.append(op)
        if dma is None:
            self.last_on[eng] = idx
        return idx

    def fence(self):
        lasts = list(self.last_on.values())
        for e in ("pe", "act", "dve", "pool", "sp"):
            self.add(e, None, extra_deps=[i for i in lasts])

    def emit(self, nc, block, engsem, dmasems):
        ops = self.ops
        for op in ops:
            for d in op.deps:
                dop = ops[d]
                if dop.dma is None:
                    if dop.eng == "pe" and op.eng == "pe" and op.dma is None:
                        continue
                    dop.needs_inc = True
        cnt = {e: 0 for e in ENGS}
        dcnt = {}
        for op in ops:
            if op.dma is not None:
                dcnt[op.dma] = dcnt.get(op.dma, 0) + op.inc
                op.tick = dcnt[op.dma]
            elif op.needs_inc:
                cnt[op.eng] += 1
                op.tick = cnt[op.eng]
        groups = sorted(dcnt.keys(), key=str)
        assert len(groups) <= len(dmasems), (len(groups), len(dmasems))
        gsem = {g: dmasems[i] for i, g in enumerate(groups)}
        per_eng = {e: [] for e in ENGS}
        for op in ops:
            per_eng[op.eng].append(op)

        def run(engname, e):
            waited = {}
            for op in per_eng[engname]:
                need = {}
                for d in op.deps:
                    dop = ops[d]
                    if dop.dma is not None:
                        s = ("d", dop.dma)
                    else:
                        if dop.eng == "pe" and op.eng == "pe" and op.dma is None:
                            continue
                        s = ("e", dop.eng)
                    if dop.tick > need.get(s, 0):
                        need[s] = dop.tick
                for s, v in need.items():
                    if waited.get(s, 0) >= v:
                        continue
                    waited[s] = v
                    sem = gsem[s[1]] if s[0] == "d" else engsem[s[1]]
                    e.wait_ge(sem, v)
                if op.fn is None:
                    if op.needs_inc:
                        e.sem_inc(engsem[engname], 1)
                    continue
                inst = op.fn(e)
                if op.dma is not None:
                    if op.inc == 16:
                        inst.then_inc(gsem[op.dma], 16)
                    else:
                        inst.then_inc(gsem[op.dma])
                elif op.needs_inc:
                    inst.then_inc(engsem[engname], 1)

        @block.tensor
        def _(e):
            run("pe", e)

        @block.scalar
        def _(e):
            run("act", e)

        @block.vector
        def _(e):
            run("dve", e)

        @block.gpsimd
        def _(e):
            run("pool", e)

        @block.sync
        def _(e):
            run("sp", e)


C_LNG = 0
C_LNB = 96
C_DIST = 192
C_MASKF = 704
C_MASKB = 832
C_MTRIF = 960
C_MTRIB = 1088
C_MBLK = 1216
C_MIND0 = 1344
C_MIND1 = 1472
C_IDENT = 1600
C_ONES = 1728
C_SEL = 1856
C_GBIAS = 1858
C_SINK = 1890
C_SLOPE = 1922
NCONST = 1938

BIGDIST = 1.0e6


def _const_table(core, ln_g, ln_b, ml_gate_b, at_sink):
    c = np.zeros((128, NCONST), np.float32)
    odd = core % 2
    p = np.arange(128)
    for l in range(4):
        for i in range(3):
            for dc in range(8):
                c[:, C_LNG + (l * 3 + i) * 8 + dc] = ln_g[l, i, dc * 128:(dc + 1) * 128]
                c[:, C_LNB + (l * 3 + i) * 8 + dc] = ln_b[l, i, dc * 128:(dc + 1) * 128]
    s = p[:, None].astype(np.float64)
    q = p[None, :].astype(np.float64)
    d = q - s + 128
    c[:, C_DIST + 0:C_DIST + 128] = np.where(d <= 128, d, BIGDIST)
    d = np.abs(q - s)
    c[:, C_DIST + 128:C_DIST + 256] = d
    d = s + 128 - q
    c[:, C_DIST + 256:C_DIST + 384] = np.where(d <= 128, d, BIGDIST)
    d = 255 - s - q
    c[:, C_DIST + 384:C_DIST + 512] = np.where(d <= 128, d, BIGDIST)
    same = np.ones((128, 128), bool)
    maskF = (same & (p[:, None] <= p[None, :])).astype(np.float32)
    maskB = (same & (p[:, None] >= p[None, :])).astype(np.float32)
    c[:, C_MASKF:C_MASKF + 128] = maskF
    c[:, C_MASKB:C_MASKB + 128] = maskB
    c[:, C_MTRIF:C_MTRIF + 128] = -maskF
    c[:, C_MTRIB:C_MTRIB + 128] = -maskB
    c[:, C_MBLK:C_MBLK + 128] = -same.astype(np.float32)
    c[:, C_MIND0:C_MIND0 + 128] = -(p[:, None] < 64).astype(np.float32) * np.ones((1, 128), np.float32)
    c[:, C_MIND1:C_MIND1 + 128] = -(p[:, None] >= 64).astype(np.float32) * np.ones((1, 128), np.float32)
    c[:, C_IDENT:C_IDENT + 128] = np.eye(128, dtype=np.float32)
    c[:, C_ONES:C_ONES + 128] = 1.0
    c[:, C_SEL + (1 - odd)] = 1.0
    for j in range(2):
        gb = ml_gate_b[j].reshape(4, 4)
        if odd:
            gb = gb[[2, 3, 0, 1]]
        c[:, C_GBIAS + j * 16:C_GBIAS + (j + 1) * 16] = gb.reshape(1, 16)
        c[:, C_SINK + j * 16:C_SINK + (j + 1) * 16] = at_sink[j].reshape(1, 16)
    slopes = np.exp2(-8.0 * np.arange(1, 17, dtype=np.float32) / 16).astype(np.float32)
    c[:, C_SLOPE:C_SLOPE + 16] = (-8.0 * slopes).reshape(1, 16)
    return c


def alibi_slopes():
    return [float(np.exp2(np.float32(-8.0) * np.float32(h + 1) / np.float32(16))) for h in range(16)]


class Builder:
    def __init__(self, stages, n_w512, n_wo):
        self.stages = stages
        nc = bass.Bass("TRN2", target_bir_lowering=False)
        self.nc = nc
        self.P = Prog()
        self.d_x = nc.dram_tensor("xT", [128, DC * T], F32, kind="ExternalInput").ap()
        self.d_out = nc.dram_tensor("outT", [128, DC * T], F32, kind="ExternalOutput").ap()
        self.d_w512 = nc.dram_tensor("w512", [n_w512, 128, 4096], F32, kind="ExternalInput").ap()
        self.d_wo = nc.dram_tensor("wo", [n_wo, 128, FC * 128], F32, kind="ExternalInput").ap()
        self.d_wg = nc.dram_tensor("wg", [2, 128, DC * 16], F32, kind="ExternalInput").ap()
        self.d_normg = nc.dram_tensor("normg", [2, 128, 1024], F32, kind="ExternalInput").ap()
        self.d_const = nc.dram_tensor("consts", [128, NCONST], F32, kind="ExternalInput").ap()
        self.d_bounce_s = nc.dram_tensor("bounce_s", [128, 260], F32).ap()
        self.d_gath_s = nc.dram_tensor("gath_s", [256, 260], F32).ap()
        self.d_bounce_h = nc.dram_tensor("bounce_h", [128, 768], BF16).ap()
        self.d_gath_h = nc.dram_tensor("gath_h", [256, 768], BF16).ap()
        self.uid = 0

    def key(self, name):
        self.uid += 1
        return (name, self.uid)

    def carve_reset(self):
        self.arena_off = 0

    def carve(self, name, shape, dtype, key=None):
        n = 1
        for s in shape[1:]:
            n *= s
        nbytes = n * (2 if dtype == BF16 else 4)
        nbytes = (nbytes + 31) // 32 * 32
        off = self.arena_off
        assert off + nbytes <= self.ARENA_BYTES, (name, off, nbytes, self.ARENA_BYTES)
        self.arena_off += nbytes
        ap = self.arena[:, off // 4:(off + nbytes) // 4]
        if dtype == BF16:
            ap = ap.bitcast(BF16)[:, 0:n]
        else:
            ap = ap[:, 0:n]
        if len(shape) == 3:
            ap = ap.rearrange("p (a b) -> p a b", a=shape[1])
        elif len(shape) == 4:
            ap = ap.rearrange("p (a b c) -> p a b c", a=shape[1], b=shape[2])
        if shape[0] != 128:
            ap = ap[0:shape[0]]
        return TR(key if key is not None else self.key(name), ap)

    def eps_col(self, eps):
        if eps not in self.eps_cols:
            i = len(self.eps_cols)
            tr = TR(("epsc", i), self.epsbuf[:, i:i + 1])
            self.memset("pool", tr, float(eps))
            self.eps_cols[eps] = tr
        return self.eps_cols[eps]

    def mm(self, out, lhsT, rhs, start=True, stop=True):
        rd = [lhsT, rhs] + ([] if start else [out])
        self.P.add("pe", lambda e: e.matmul(out.ap, lhsT.ap, rhs.ap, start=start, stop=stop),
                   reads=rd, writes=[out])

    def transpose(self, out, in_, ident):
        self.P.add("pe", lambda e: e.transpose(out.ap, in_.ap, ident.ap), reads=[in_, ident], writes=[out])

    def act(self, out, in_, func, bias=None, scale=None, extra_reads=(), eng="act"):
        kw = {}
        rd = [in_] + list(extra_reads)
        if bias is not None:
            kw["bias"] = bias.ap if isinstance(bias, TR) else bias
            if isinstance(bias, TR):
                rd.append(bias)
        if scale is not None:
            kw["scale"] = scale.ap if isinstance(scale, TR) else scale
            if isinstance(scale, TR):
                rd.append(scale)
        self.P.add(eng, lambda e: e.activation(out.ap, in_.ap, func, **kw), reads=rd, writes=[out])

    def tt(self, eng, out, in0, in1, op):
        self.P.add(eng, lambda e: e.tensor_tensor(out.ap, in0.ap, in1.ap, op), reads=[in0, in1], writes=[out])

    def ts(self, eng, out, in0, s1, s2, op0, op1=None):
        rd = [in0]
        a1 = s1.ap if isinstance(s1, TR) else s1
        a2 = s2.ap if isinstance(s2, TR) else s2
        if isinstance(s1, TR):
            rd.append(s1)
        if isinstance(s2, TR):
            rd.append(s2)
        if op1 is None:
            self.P.add(eng, lambda e: e.tensor_scalar(out.ap, in0.ap, a1, None, op0), reads=rd, writes=[out])
        else:
            self.P.add(eng, lambda e: e.tensor_scalar(out.ap, in0.ap, a1, a2, op0, op1), reads=rd, writes=[out])

    def stt(self, eng, out, in0, scalar, in1, op0, op1):
        rd = [in0, in1]
        a = scalar.ap if isinstance(scalar, TR) else scalar
        if isinstance(scalar, TR):
            rd.append(scalar)
        self.P.add(eng, lambda e: e.scalar_tensor_tensor(out.ap, in0.ap, a, in1.ap, op0, op1),
                   reads=rd, writes=[out])

    def copy(self, eng, out, in_):
        if eng == "act":
            self.P.add("act", lambda e: e.activation(out.ap, in_.ap, AF.Copy), reads=[in_], writes=[out])
        else:
            self.P.add(eng, lambda e: e.tensor_copy(out.ap, in_.ap), reads=[in_], writes=[out])

    def memset(self, eng, out, val):
        self.P.add(eng, lambda e: e.memset(out.ap, val), reads=[], writes=[out])

    def dma(self, eng, out, in_, group, reads=(), writes=None):
        o = out.ap if isinstance(out, TR) else out
        i = in_.ap if isinstance(in_, TR) else in_
        rd = list(reads) + ([in_] if isinstance(in_, TR) else [])
        wr = ([out] if isinstance(out, TR) else []) if writes is None else writes
        self.P.add(eng, lambda e: e.dma_start(out=o, in_=i), reads=rd, writes=wr, dma=group)

    def load_w512(self, g, ncols=512):
        s = self.win_rr % len(self.win_slots)
        self.win_rr += 1
        slot = self.win_slots[s]
        if ncols == 512:
            self.dma("pool", slot, self.d_w512[g].rearrange("p (a b) -> p a b", a=DC), group=("win", s))
        else:
            self.dma("pool", slot[:, :, 0:ncols],
                     self.d_w512[g].rearrange("p (a b) -> p a b", a=DC)[:, :, 0:ncols], group=("win", s))
        return slot

    def load_wo(self, g, nparts=128, ncols=FC * 128):
        s = self.wo_rr % len(self.wo_slots)
        self.wo_rr += 1
        slot = self.wo_slots[s]
        self.dma("pool", slot[0:nparts, 0:ncols], self.d_wo[g][0:nparts, 0:ncols], group=("wo", s))
        return slot

    def build(self):
        nc = self.nc
        P = self.P
        with ExitStack() as es:
            def sb(name, shape, dt):
                return es.enter_context(nc.sbuf_tensor(name, shape, dt))

            xs = sb("xres", [128, DC, T], F32)
            cst = sb("cst", [128, NCONST], F32)
            cbf = sb("cbf", [128, 256], BF16)
            self.epsbuf = sb("epsbuf", [128, 8], F32)
            self.eps_cols = {}
            win = [sb("win%d" % i, [128, DC, 512], BF16) for i in range(3)]
            wo = [sb("wo%d" % i, [128, FC * 128], BF16) for i in range(2)]
            self.ARENA_BYTES = 100 * 1024 - 6144
            lnstat = sb("lnstat", [128, 3, 512], F32)
            self.ln_mean = TR("ln_mean", lnstat[:, 0, :])
            self.ln_msq = TR("ln_msq", lnstat[:, 1, :])
            self.ln_rstd = TR("ln_rstd", lnstat[:, 2, :])
            self.pend = []
            arena_t = sb("arena", [128, self.ARENA_BYTES // 4], F32)
            self.arena = arena_t[:]
            psum = [es.enter_context(nc.psum_tensor("ps%d" % i, [128, 512], F32)) for i in range(8)]
            engsem = {e: es.enter_context(nc.semaphore("sem_" + e)) for e in ENGS}
            dmasems = [es.enter_context(nc.semaphore("dsem%d" % i)) for i in range(60)]

            self.x = [[TR(("x", dc, tt), xs[:, dc, tt * 512:(tt + 1) * 512]) for tt in range(4)] for dc in range(DC)]
            self.cst = TR("cst", cst[:])
            self.ident_bf = TR("cbf", cbf[:, 0:128])
            self.ones_bf = TR("cbf", cbf[:, 128:256])
            self.win_slots = [TR(("win", i), win[i][:]) for i in range(3)]
            self.wo_slots = [TR(("wo", i), wo[i][:]) for i in range(2)]
            self.win_rr = 0
            self.wo_rr = 0
            self.ps = [TR(("ps", i), psum[i][:]) for i in range(8)]

            self.dma("sp", self.cst, self.d_const, group="cst")
            dx3 = self.d_x.rearrange("p (a b) -> p a b", a=DC)
            prev = []
            for tt in range(4):
                cur_ = []
                for hf in range(2):
                    o_ = xs[:, hf * 4:(hf + 1) * 4, tt * 512:(tt + 1) * 512]
                    i_ = dx3[:, hf * 4:(hf + 1) * 4, tt * 512:(tt + 1) * 512]
                    cur_.append(P.add("sp", lambda e, o_=o_, i_=i_: e.dma_start(out=o_, in_=i_),
                                      writes=[self.x[dc][tt] for dc in range(hf * 4, hf * 4 + 4)],
                                      dma=("xin", tt, hf), extra_deps=prev))
                prev = cur_
            self.copy("dve", TR("cbf", cbf[:, 0:128]), self.cst[:, C_IDENT:C_IDENT + 128])
            self.copy("dve", TR("cbf", cbf[:, 128:256]), self.cst[:, C_ONES:C_ONES + 128])

            for si_, st in enumerate(self.stages):
                if si_ > 0:
                    P.fence()
                self.carve_reset()
                if st[0] == "ffn":
                    self.stage_ffn(st[1], st[2], st[3])
                elif st[0] == "attn":
                    self.stage_attn(st[1], st[2], st[3], st[4])
                elif st[0] == "mlstm":
                    self.stage_mlstm(st[1], st[2], st[3], st[4])

            self.drain()
            outs = []
            do3 = self.d_out.rearrange("p (a b) -> p a b", a=DC)
            for tt in range(4):
                for hf in range(2):
                    k = self.key("outdma")
                    outs.append(k)
                    self.dma("sp", do3[:, hf * 4:(hf + 1) * 4, tt * 512:(tt + 1) * 512],
                             TR(("xo", tt, hf), xs[:, hf * 4:(hf + 1) * 4, tt * 512:(tt + 1) * 512]),
                             group=("xout", tt, hf), reads=[self.x[dc][tt] for dc in range(hf * 4, hf * 4 + 4)], writes=[k])
            P.add("sp", None, reads=outs)

            with nc.Block() as block:
                P.emit(nc, block, engsem, dmasems)
        return nc

    def cast_xb(self, xb, tts):
        n = 0
        for i, tt in enumerate(tts):
            for dc in range(DC):
                eng = "act" if n % 2 == 0 else "dve"
                self.copy(eng, xb[dc][i], self.x[dc][tt])
                n += 1

    def outproj_ln(self, tts, kchunks, wo_group, wo_parts, wo_ncols, lhsT_of, resid_scale, ln_idx, eps,
                   scratch, defer=None, drain=None):
        zb, zsq, meanb, msq, rstd = scratch
        nk = len(kchunks(0))
        ps_y = [self.ps[0], self.ps[1]]
        ps_sum = [self.ps[4], self.ps[6]]
        ps_sq = [self.ps[5], self.ps[7]]
        pending = []

        def flush():
            for (i, c) in pending:
                self.mm(ps_sum[i], self.ones_bf, zb[(c * 2 + i) % len(zb)], start=(c == 0), stop=(c == DC - 1))
                self.mm(ps_sq[i], self.ones_bf, zsq[(c * 2 + i) % len(zsq)], start=(c == 0), stop=(c == DC - 1))
            pending.clear()

        n = 0
        for c in range(DC):
            if c == DC - 1 and drain:
                while drain:
                    drain.pop(0)()
            slot = self.load_wo(wo_group(c), wo_parts, wo_ncols)
            for i, tt in enumerate(tts):
                py = ps_y[n % 2]
                n += 1
                ks = kchunks(i)
                for k in range(nk):
                    self.mm(py, lhsT_of(slot, k), ks[k], start=(k == 0), stop=(k == nk - 1))
                flush()
                for _ in range(4):
                    if drain:
                        drain.pop(0)()
                xc = self.x[c][tt]
                self.stt("dve", xc, xc, resid_scale, py, ALU.mult, ALU.add)
                self.copy("act", zb[(c * 2 + i) % len(zb)], xc)
                self.act(zsq[(c * 2 + i) % len(zsq)], xc, AF.Square)
                pending.append((i, c))
        flush()
        g0 = C_LNG + ln_idx * 8
        b0 = C_LNB + ln_idx * 8
        th = []
        for i, tt in enumerate(tts):
            th.append((i, lambda i=i: self.act(meanb[i], ps_sum[i], AF.Copy, scale=1.0 / D)))
            th.append((i, lambda i=i: self.tt("dve", msq[i], meanb[i], meanb[i], ALU.mult)))
            th.append((i, lambda i=i: self.stt("dve", rstd[i], ps_sq[i], 1.0 / D, msq[i], ALU.mult, ALU.subtract)))
            th.append((i, lambda i=i: self.act(rstd[i], rstd[i], AF.Sqrt, bias=self.eps_col(eps))))
            th.append((i, lambda i=i: self.P.add("dve", lambda e, o=rstd[i]: e.reciprocal(o.ap, o.ap), reads=[rstd[i]], writes=[rstd[i]])))
        for i, tt in enumerate(tts):
            for c in range(DC):
                th.append((i, lambda i=i, tt=tt, c=c: self.tt("dve", self.x[c][tt], self.x[c][tt], meanb[i], ALU.subtract)))
            for c in range(DC):
                th.append((i, lambda i=i, tt=tt, c=c: self.tt("dve", self.x[c][tt], self.x[c][tt], rstd[i], ALU.mult)))
            for c in range(DC):
                th.append((i, lambda tt=tt, c=c: self.act(self.x[c][tt], self.x[c][tt], AF.Identity,
                                                           bias=self.cst[:, b0 + c:b0 + c + 1], scale=self.cst[:, g0 + c:g0 + c + 1])))
        if defer == "all":
            return [f_ for _, f_ in th]
        keep = []
        for i, f_ in th:
            if defer == "last" and i == len(tts) - 1:
                keep.append(f_)
            else:
                f_()
        return keep

    def ln_scratch(self, ntt=2, nz=4):
        zb = [self.carve("zb", [128, 512], BF16) for _ in range(nz)]
        zsq = [self.carve("zsq", [128, 512], BF16) for _ in range(nz)]
        if ntt == 1:
            return (zb, zsq, [self.ln_mean], [self.ln_msq], [self.ln_rstd])
        lm = self.carve("lmean", [128, 512], F32)
        lq = self.carve("lmsq", [128, 512], F32)
        lr = self.carve("lrstd", [128, 512], F32)
        return (zb, zsq, [lm, self.ln_mean], [lq, self.ln_msq], [lr, self.ln_rstd])

    def drain(self, n=None):
        while self.pend and (n is None or n > 0):
            self.pend.pop(0)()
            if n is not None:
                n -= 1

    def stage_ffn(self, g512_base, gwo_base, ln_idx):
        xb = [[self.carve("xb", [128, 512], BF16) for _ in range(2)] for _ in range(DC)]
        h = [[self.carve("h", [128, 512], BF16) for _ in range(2)] for _ in range(FC)]
        sg = [self.carve("sg", [128, 512], F32) for _ in range(2)]
        scratch = self.ln_scratch()
        self.eps_col(4.0 * LN_EPS)
        self.cast_xb(xb, [0, 1])
        n = 0
        for st in range(2):
            tts = [2 * st, 2 * st + 1]
            for jg in range(11):
                slot = self.load_w512(g512_base + jg)
                for i in range(2):
                    for half in range(2):
                        j = 2 * jg + half
                        pg = self.ps[(n % 2) * 2]
                        pu = self.ps[(n % 2) * 2 + 1]
                        n += 1
                        for dc in range(DC):
                            self.mm(pg, slot[:, dc, half * 128:(half + 1) * 128], xb[dc][i],
                                    start=(dc == 0), stop=(dc == DC - 1))
                        for dc in range(DC):
                            self.mm(pu, slot[:, dc, 256 + half * 128:256 + (half + 1) * 128], xb[dc][i],
                                    start=(dc == 0), stop=(dc == DC - 1))
                        s_ = sg[n % 2]
                        self.act(s_, pg, AF.Silu)
                        self.tt("dve", h[j][i], s_, pu, ALU.mult)
                        self.drain(2)
            self.drain()
            if st == 0:
                self.cast_xb(xb, [2, 3])
            self.pend = self.outproj_ln(tts, lambda i: [h[k][i] for k in range(FC)],
                                        lambda c: gwo_base + c, 128, FC * 128,
                                        lambda slot, k: slot[:, k * 128:(k + 1) * 128],
                                        2.0 * ALPHA, ln_idx, 4.0 * LN_EPS, scratch,
                                        defer=("all" if st == 0 else "last"))

    def exchange(self, bounce, gath, tag, srcs, rcv):
        kb = ("dram", tag[0], "b")
        kg = ("dram", tag[0], "g")
        for (tr, c0, ncol) in srcs:
            self.dma("sp", bounce[:, c0:c0 + ncol], tr, group=("bnc", tag), writes=[kb])
        self.P.add("pool", lambda e: e.collective_compute(
            "AllGather", ALU.bypass, replica_groups=[[0, 1], [2, 3], [4, 5], [6, 7]],
            ins=[bounce.opt()], outs=[gath.opt()]), reads=[kb], writes=[kg], dma=("cc", tag), inc=1)
        self.dma("sp", rcv, gath.rearrange("(r p) n -> p r n", p=128), group=("rcv", tag), reads=[kg])

    def select_partner(self, dst, rcv, c0, ncol):
        sel0 = self.cst[:, C_SEL:C_SEL + 1]
        sel1 = self.cst[:, C_SEL + 1:C_SEL + 2]
        self.ts("dve", dst, rcv[:, 0, c0:c0 + ncol], sel0, None, ALU.mult)
        self.stt("dve", dst, rcv[:, 1, c0:c0 + ncol], sel1, dst, ALU.mult, ALU.add)

    def stage_attn(self, g512_base, gwo_base, ln_idx, j):
        P = self.P
        qT_all = self.carve("qT", [128, 8, T], BF16)
        qk = [[self.key("qT") for _ in range(4)] for _ in range(8)]
        kT_all = self.carve("kT", [128, 4, T + 128], BF16)
        kk = [[self.key("kT") for _ in range(5)] for _ in range(4)]
        V_all = self.carve("V", [128, 17, 256], BF16)
        Vb = [TR(self.key("V"), V_all.ap[:, b, :]) for b in range(17)]
        esink = self.carve("esink", [128, 16], F32)
        mark = self.arena_off
        xb = [[self.carve("xb", [128, 512], BF16) for _ in range(2)] for _ in range(DC)]
        rcv = self.carve("rcvh", [128, 2, 768], BF16)

        self.act(esink, self.cst[:, C_SINK + j * 16:C_SINK + (j + 1) * 16], AF.Exp)
        n = 0
        for st in range(2):
            tts = [2 * st, 2 * st + 1]
            if st == 1:
                self.drain()
            self.cast_xb(xb, tts)
            sA = self.load_w512(g512_base + 0)
            sB = self.load_w512(g512_base + 1)
            sC = self.load_w512(g512_base + 2)
            for c in range(8):
                slot = sA if c < 4 else sB
                for i, tt in enumerate(tts):
                    ps = self.ps[n % 4]
                    for dc in range(DC):
                        self.mm(ps, slot[:, dc, (c % 4) * 128:(c % 4 + 1) * 128], xb[dc][i],
                                start=(dc == 0), stop=(dc == DC - 1))
                    dst = TR(qk[c][tt], qT_all.ap[:, c, tt * 512:(tt + 1) * 512])
                    self.copy("act" if n % 2 == 0 else "dve", dst, ps)
                    n += 1
                    self.drain(4)
                if c == 3:
                    sV = self.load_w512(g512_base + 3, 256)
            for k in range(4):
                for i, tt in enumerate(tts):
                    ps = self.ps[n % 4]
                    for dc in range(DC):
                        self.mm(ps, sC[:, dc, k * 128:(k + 1) * 128], xb[dc][i], start=(dc == 0), stop=(dc == DC - 1))
                    dst = TR(kk[k][tt], kT_all.ap[:, k, tt * 512:(tt + 1) * 512])
                    self.copy("act" if n % 2 == 0 else "dve", dst, ps)
                    n += 1
            for i, tt in enumerate(tts):
                for tile in range(4):
                    ps = self.ps[n % 4]
                    for dc in range(DC):
                        self.mm(ps[:, 0:256], xb[dc][i][:, tile * 128:(tile + 1) * 128], sV[:, dc, 0:256],
                                start=(dc == 0), stop=(dc == DC - 1))
                    self.copy("act" if n % 2 == 0 else "dve", Vb[tt * 4 + tile], ps[:, 0:256])
                    n += 1
        if DBG == 1:
            return
        self.drain()
        srcs = [(TR(kk[k][3], kT_all.ap[:, k, T - 128:T]), k * 128, 128) for k in range(4)]
        srcs.append((Vb[15], 512, 256))
        self.exchange(self.d_bounce_h, self.d_gath_h, ("h", j), srcs, rcv)
        for k in range(4):
            self.select_partner(TR(kk[k][4], kT_all.ap[:, k, T:T + 128]), rcv, k * 128, 128)
        self.select_partner(Vb[16], rcv, 512, 256)
        P.fence()
        if DBG == 2:
            return
        self.arena_off = mark
        oT_all = self.carve("oT", [64, 16, 512], BF16)
        qz = self.carve("qz", [128, 8, 2, 128], BF16)
        self.memset("pool", qz, 0.0)
        Pm = [self.carve("Pm", [128, 512], BF16) for _ in range(3)]
        dz_off = self.arena_off
        dsum = [self.carve("dsum", [64, 512], F32) for _ in range(2)]
        end_off = self.arena_off
        self.arena_off = dz_off
        zb = [self.carve("zb", [128, 512], BF16) for _ in range(2)]
        zsq = [self.carve("zsq", [128, 512], BF16) for _ in range(2)]
        assert self.arena_off == end_off
        meanb = [self.ln_mean]
        scratch = (zb, zsq, [self.ln_mean], [self.ln_msq], [self.ln_rstd])
        diag = self.carve("diag", [128, 16, 2, 128], BF16)
        distbf = self.carve("distbf", [128, 4, 128], BF16)
        slopes = alibi_slopes()
        for kd in range(4):
            self.copy("dve", distbf[:, kd, :], self.cst[:, C_DIST + kd * 128:C_DIST + (kd + 1) * 128])
        t32 = meanb[0]
        for h in range(16):
            ta = t32[:, (h % 2) * 256:(h % 2) * 256 + 128]
            tb = t32[:, (h % 2) * 256 + 128:(h % 2) * 256 + 256]
            self.ts("dve", ta, self.cst[:, C_IDENT:C_IDENT + 128], float(np.float32(-8.0 * slopes[h])), None, ALU.mult)
            self.copy("act", diag[:, h, 0, :], ta)
            self.tt("dve", tb, ta, diag[:, h, 0, :], ALU.subtract)
            self.copy("act", diag[:, h, 1, :], tb)
        ones64 = self.ones_bf[:, 0:64]
        att_pend = []
        ps_s_t = [self.ps[0], self.ps[1]]
        ps_o_t = [self.ps[2], self.ps[3], self.ps[4]]
        ps_d_t = [self.ps[5], self.ps[6]]
        for st in range(4):
            ok = [self.key("oT") for _ in range(16)]
            subs = []
            u = 0
            for nb in range(4):
                nq = st * 4 + nb
                for k in range(4):
                    blocks = []
                    if nq > 0:
                        blocks.append((nq - 1, 0))
                    blocks.append((nq, 1))
                    blocks.append((nq + 1, 2) if nq < 15 else (16, 3))
                    for bi, (kb, kind) in enumerate(blocks):
                        subs.append(dict(nq=nq, nb=nb, k=k, kb=kb, kind=kind, first=(bi == 0),
                                         last=(bi == len(blocks) - 1), u=u, newq=(k == 0 and bi == 0)))
                    u += 1
            NS = len(subs)

            def A_Q(t):
                d = subs[t]
                if not d["newq"]:
                    return
                nq = d["nq"]
                for e_ in range(2):
                    src_ap = qT_all.ap[e_ * 64:(e_ + 1) * 64, :, nq * 128:(nq + 1) * 128]
                    dst_ap = qz.ap[e_ * 64:(e_ + 1) * 64, :, e_, :]
                    P.add("act", lambda e, d_=dst_ap, a=src_ap: e.activation(d_, a, AF.Copy),
                          reads=[qk[c][nq // 4] for c in range(8)], writes=[qz])

            def A_S(t):
                d = subs[t]
                k, kb = d["k"], d["kb"]
                ps_s = ps_s_t[t % 2]
                kkey = kk[k][kb // 4] if kb < 16 else kk[k][4]
                for hh in range(4):
                    c_ = 2 * k + hh // 2
                    e_ = hh % 2
                    h = 4 * k + hh
                    o_ap = ps_s.ap[:, hh * 128:(hh + 1) * 128]
                    l_ap = kT_all.ap[:, k, kb * 128:(kb + 1) * 128]
                    r_ap = qz.ap[:, c_, e_, :]
                    P.add("pe", lambda e, o_ap=o_ap, l_ap=l_ap, r_ap=r_ap: e.matmul(o_ap, l_ap, r_ap, start=True, stop=False),
                          reads=[kkey, qz], writes=[ps_s])
                    for hl in range(2):
                        d_ap = diag.ap[:, h, hl, :]
                        b_ap = distbf.ap[:, d["kind"], :]
                        P.add("pe", lambda e, o_ap=o_ap, d_ap=d_ap, b_ap=b_ap, hl=hl: e.matmul(o_ap, d_ap, b_ap, start=False, stop=(hl == 1)),
                              reads=[diag, distbf, ps_s], writes=[ps_s])

            def A_E(t):
                self.act(Pm[t % 3], ps_s_t[t % 2], AF.Exp, scale=0.125)

            def A_PV(t):
                d = subs[t]
                k, kb = d["k"], d["kb"]
                ps_o = ps_o_t[d["u"] % 3]
                ps_d = ps_d_t[d["u"] % 2]
                self.mm(ps_o[0:64, :], Vb[kb][:, k * 64:(k + 1) * 64], Pm[t % 3], start=d["first"], stop=d["last"])
                self.mm(ps_d[0:64, :], ones64, Pm[t % 3], start=d["first"], stop=d["last"])

            def A_N1(t):
                d = subs[t]
                if not d["last"]:
                    return
                ds = dsum[d["u"] % 2]
                ps_d = ps_d_t[d["u"] % 2]
                for hh in range(4):
                    h = 4 * d["k"] + hh
                    self.ts("dve", ds[:, hh * 128:(hh + 1) * 128], ps_d[0:64, hh * 128:(hh + 1) * 128],
                            esink[0:64, h:h + 1], None, ALU.add)

            def A_N2(t):
                d = subs[t]
                if not d["last"]:
                    return
                ds = dsum[d["u"] % 2]
                self.act(ds, ds, AF.Ln)
                self.act(ds, ds, AF.Exp, scale=-1.0)

            def A_N3(t):
                d = subs[t]
                if not d["last"]:
                    return
                ds = dsum[d["u"] % 2]
                ps_o = ps_o_t[d["u"] % 3]
                for hh in range(4):
                    h = 4 * d["k"] + hh
                    dst = TR(ok[h], oT_all.ap[:, h, d["nb"] * 128:(d["nb"] + 1) * 128])
                    self.tt("dve", dst, ps_o[0:64, hh * 128:(hh + 1) * 128], ds[:, hh * 128:(hh + 1) * 128], ALU.mult)

            stages_ = [(0, A_Q), (1, A_S), (2, A_E), (3, A_PV), (4, A_N1), (5, A_N2), (6, A_N3)]
            for _ in range(5):
                if att_pend:
                    att_pend.pop(0)()
            for it in range(NS + 6):
                for l_, fn_ in sorted(stages_, key=lambda x: -x[0]):
                    t = it - l_
                    if 0 <= t < NS:
                        fn_(t)
                if att_pend:
                    att_pend.pop(0)()
            while att_pend:
                att_pend.pop(0)()
            att_pend = self.outproj_ln([st], lambda i: [TR(ok[h], oT_all.ap[:, h, :]) for h in range(16)],
                                       lambda c: gwo_base + c, 64, 16 * 128,
                                       lambda slot, k: slot[0:64, k * 128:(k + 1) * 128],
                                       ALPHA, ln_idx, LN_EPS, scratch, defer="all")
        self.pend = att_pend

    def stage_mlstm(self, g512_base, gwo_base, ln_idx, j):
        P = self.P
        AX = mybir.AxisListType.X
        hT_all = self.carve("hT", [128, 8, T], BF16)
        hk = [[self.key("hT") for _ in range(4)] for _ in range(8)]
        xb = [self.carve("xb", [128, 512], BF16) for _ in range(DC)]
        gtmp = self.carve("gtmp", [128, 512], F32)

        def gview(name, k_):
            return TR(self.key(name), gtmp.ap[:, k_ * 128:(k_ + 1) * 128].rearrange("p (t d h) -> p t d h", t=16, d=2))
        lf = gview("lf", 0)
        etmp = gview("etmp", 1)
        gg = gview("gg", 2)
        bb = gview("bb", 3)
        eb = self.carve("eb", [128, 16, 2, 4], F32)
        emg = self.carve("emg", [128, 16, 2, 4], F32)
        egb = self.carve("egb", [128, 16, 2, 4], F32)
        eG = self.carve("eG", [128, 16, 2, 4], F32)
        normg = self.carve("normg", [128, 256], F32)
        wgs = self.carve("wgs", [128, 8, 16], BF16)
        mark = self.arena_off
        qT = self.carve("qTh", [128, T], BF16)
        kT = self.carve("kTh", [128, T], BF16)
        kTM = self.carve("kTM", [128, 16, 128], BF16)
        vaug = self.carve("vaug", [128, 16, 264], BF16)
        so = self.carve("so", [128, 16, 256], BF16)
        h1 = self.carve("h1", [128, 16, 256], BF16)
        gp = TR(self.key("gp"), h1.ap.rearrange("p t c -> p (t c)")[:, 0:512].bitcast(F32).rearrange("p (t g) -> p t g", t=16))
        Am = [self.carve("Am", [128, 128], BF16) for _ in range(2)]
        vp = [self.carve("vp", [128, 258], BF16) for _ in range(2)]
        vpp = [self.carve("vpp", [128, 258], BF16) for _ in range(2)]
        C32 = [self.carve("C", [128, 260], F32) for _ in range(2)]
        Cb = [self.carve("Cb", [128, 264], BF16) for _ in range(3)]
        rcv = self.carve("rcvs", [128, 2, 260], F32)
        dd = [self.carve("dd", [128, 2], F32) for _ in range(4)]
        hs = [self.carve("hs", [128, 256], F32) for _ in range(2)]
        hs.append(TR(self.key("hsA"), gtmp.ap[:, 0:256]))
        hs.append(TR(self.key("hsB"), gtmp.ap[:, 256:512]))
        st4 = [self.carve("st4", [128, 8], F32) for _ in range(4)]
        st6 = [self.carve("st6", [128, 8], F32) for _ in range(4)]
        hb = [self.carve("hb", [128, 256], BF16) for _ in range(2)]
        one_col = self.eps_col(1.0)
        eps_col = self.eps_col(LN_EPS)
        KS = float(128 ** -0.5)

        if DBG not in (21, 22, 23):
            self.dma("pool", wgs, self.d_wg[j].rearrange("p (a b) -> p a b", a=DC), group="wgs")
        self.memset("pool", vaug, 1.0)
        gp4 = gp.ap.rearrange("p t (k h) -> p t k h", k=4)
        psT = TR(self.ps[7].key, self.ps[7].ap.bitcast(BF16))

        cnt = 0
        xb_sets = [xb, [TR(("win", 2), self.win_slots[2].ap[:, dc, :]) for dc in range(DC)]]
        for h in range(4):
            self.dma("sp", normg, self.d_normg[j][:, h * 256:(h + 1) * 256], group="normg")
            self.win_rr = 0
            sW = self.load_w512(g512_base + 2 * h)
            sO = self.load_w512(g512_base + 2 * h + 1, 256)
            for tt in range(4):
                xb = xb_sets[tt % 2]
                if tt == 2:
                    self.drain()
                for dc in range(DC):
                    self.copy("act" if dc % 2 == 0 else "dve", xb[dc], self.x[dc][tt])
                ps = self.ps[cnt % 4]; cnt += 1
                for dc in range(DC):
                    self.mm(ps, sW[:, dc, 0:128], xb[dc], start=(dc == 0), stop=(dc == DC - 1))
                self.copy("act", qT[:, tt * 512:(tt + 1) * 512], ps)
                ps = self.ps[cnt % 4]; cnt += 1
                for dc in range(DC):
                    self.mm(ps, sW[:, dc, 128:256], xb[dc], start=(dc == 0), stop=(dc == DC - 1))
                self.act(kT[:, tt * 512:(tt + 1) * 512], ps, AF.Copy, scale=KS)
                for tile in range(4):
                    if DBG == 22:
                        continue
                    i = tt * 4 + tile
                    ps = self.ps[cnt % 4]; cnt += 1
                    for dc in range(DC):
                        self.mm(ps[:, 0:128], xb[dc][:, tile * 128:(tile + 1) * 128], sW[:, dc, 128:256],
                                start=(dc == 0), stop=(dc == DC - 1))
                    self.act(kTM[:, i, :], ps[:, 0:128], AF.Copy, scale=KS)
                    ps = self.ps[cnt % 4]; cnt += 1
                    for dc in range(DC):
                        self.mm(ps[:, 0:256], xb[dc][:, tile * 128:(tile + 1) * 128], sW[:, dc, 256:512],
                                start=(dc == 0), stop=(dc == DC - 1))
                    self.copy("dve", vaug[:, i, 0:256], ps[:, 0:256])
                    if DBG == 23:
                        continue
                    ps = self.ps[cnt % 4]; cnt += 1
                    for dc in range(DC):
                        self.mm(ps[:, 0:256], xb[dc][:, tile * 128:(tile + 1) * 128], sO[:, dc, 0:256],
                                start=(dc == 0), stop=(dc == DC - 1))
                    self.act(so[:, i, :], ps[:, 0:256], AF.Sigmoid)
                    self.drain(4)
                    if h == 0 and DBG not in (21, 22, 23):
                        ps = self.ps[cnt % 4]; cnt += 1
                        for dc in range(DC):
                            self.mm(ps[:, 0:16], xb[dc][:, tile * 128:(tile + 1) * 128], wgs[:, dc, :],
                                    start=(dc == 0), stop=(dc == DC - 1))
                        self.tt("dve", gp[:, i, :], ps[:, 0:16], self.cst[:, C_GBIAS + j * 16:C_GBIAS + (j + 1) * 16], ALU.add)
            if DBG in (11, 21, 22, 23):
                return
            if h == 0:
                for dr in range(2):
                    src = TR(gp.key, gp4[:, :, 1 + 2 * dr, :])
                    self.act(etmp[:, :, dr, :], src, AF.Exp, scale=-1.0)
                    self.act(lf[:, :, dr, :], etmp[:, :, dr, :], AF.Ln, bias=one_col)
                lf2 = TR(lf.key, lf.ap.rearrange("p t d h -> p (t d h)"))
                pg = self.ps[4]
                pgF = TR(pg.key, pg.ap[:, 0:128].rearrange("p (t d h) -> p t d h", t=16, d=2))
                pgB = TR(pg.key, pg.ap[:, 128:256].rearrange("p (t d h) -> p t d h", t=16, d=2))
                self.mm(pg[:, 0:128], self.cst[:, C_MTRIF:C_MTRIF + 128], lf2)
                self.mm(pg[:, 128:256], self.cst[:, C_MTRIB:C_MTRIB + 128], lf2)
                self.mm(self.ps[5][:, 0:128], self.cst[:, C_MBLK:C_MBLK + 128], lf2)
                self.copy("dve", gg[:, :, 0, :], pgF[:, :, 0, :])
                self.copy("dve", gg[:, :, 1, :], pgB[:, :, 1, :])
                for dr in range(2):
                    self.tt("dve", bb[:, :, dr, :], TR(gp.key, gp4[:, :, 2 * dr, :]), gg[:, :, dr, :], ALU.subtract)
                self.act(eb, bb, AF.Exp)
                self.act(emg, gg, AF.Exp, scale=-1.0)
                p5 = TR(self.ps[5].key, self.ps[5].ap[:, 0:128].rearrange("p (t d h) -> p t d h", t=16, d=2))
                self.tt("dve", egb, p5, bb, ALU.add)
                self.act(egb, egb, AF.Exp)
                self.act(eG, p5, AF.Exp)
            if DBG == 12:
                return
            LAG = 2
            RING = len(Cb)
            for dirn in range(2):
                mask = self.cst[:, C_MASKF:C_MASKF + 128] if dirn == 0 else self.cst[:, C_MASKB:C_MASKB + 128]
                if dirn == 0:
                    self.memset("dve", C32[0], 0.0)
                    self.memset("dve", Cb[0], 0.0)
                order = list(range(16)) if dirn == 0 else list(range(15, -1, -1))

                psA_t = [self.ps[0][:, 0:128], self.ps[1][:, 0:128]]
                pso_t = [self.ps[2], self.ps[3], self.ps[6]]

                def tsl_(t):
                    i = order[t]
                    return i, slice(i * 128, (i + 1) * 128)

                def V0(t):
                    i, _ = tsl_(t)
                    self.ts("dve", vpp[t % 2][:, 0:257], vaug[:, i, 0:257], egb[:, i, dirn, h:h + 1], None, ALU.mult)

                def L1(t):
                    i, tsl = tsl_(t)
                    self.mm(psA_t[t % 2], kT[:, tsl], qT[:, tsl])
                    self.mm(self.ps[4 + t % 2][:, 0:257], kTM[:, i, :], vpp[t % 2][:, 0:257])

                def L2(t):
                    i, tsl = tsl_(t)
                    ps_U = self.ps[4 + t % 2]
                    self.stt("dve", C32[(t + 1) % 2][:, 0:257], C32[t % 2][:, 0:257], eG[:, i, dirn, h:h + 1],
                             ps_U[:, 0:257], ALU.mult, ALU.add)
                    self.copy("act", Cb[(t + 1) % RING][:, 0:257], C32[(t + 1) % 2][:, 0:257])
                    self.tt("dve", Am[t % 2], psA_t[t % 2], mask, ALU.mult)
                    self.act(vp[t % 2][:, 0:257], vaug[:, i, 0:257], AF.Copy, scale=eb[:, i, dirn, h:h + 1])

                def L3(t):
                    i, tsl = tsl_(t)
                    ps_o = pso_t[t % 3]
                    self.mm(ps_o[:, 0:257], Am[t % 2], vp[t % 2][:, 0:257], start=True, stop=False)
                    self.mm(ps_o[:, 0:257], qT[:, tsl], Cb[t % RING][:, 0:257], start=False, stop=True)
                    self.act(dd[t % 4][:, 0:1], ps_o[:, 256:257], AF.Abs)

                def L4(t):
                    i, tsl = tsl_(t)
                    ps_o = pso_t[t % 3]
                    d_ = dd[t % 4]
                    self.ts("dve", d_[:, 0:1], d_[:, 0:1], emg[:, i, dirn, h:h + 1], None, ALU.max)
                    P.add("dve", lambda e, o=d_: e.reciprocal(o.ap[:, 1:2], o.ap[:, 0:1]), reads=[d_], writes=[d_])
                    if dirn == 0:
                        self.ts("dve", h1[:, i, :], ps_o[:, 0:256], d_[:, 1:2], None, ALU.mult)
                    else:
                        hs_ = hs[t % 4]
                        s6 = st6[t % 4]
                        self.stt("dve", hs_, ps_o[:, 0:256], d_[:, 1:2], h1[:, i, :], ALU.mult, ALU.add)
                        P.add("dve", lambda e, o=s6, a=hs_: e.bn_stats(o.ap[:, 0:6], a.ap), reads=[hs_], writes=[s6])

                def L5(t):
                    s6 = st6[t % 4]
                    s4 = st4[t % 4]
                    P.add("dve", lambda e, o=s4, a=s6: e.bn_aggr(o.ap[:, 0:2], a.ap[:, 0:6]), reads=[s6], writes=[s4])
                    self.act(s4[:, 2:3], s4[:, 1:2], AF.Sqrt, bias=eps_col)

                def L6(t):
                    hs_ = hs[t % 4]
                    s4 = st4[t % 4]
                    P.add("dve", lambda e, o=s4: e.reciprocal(o.ap[:, 3:4], o.ap[:, 2:3]), reads=[s4], writes=[s4])
                    self.stt("dve", s4[:, 4:5], s4[:, 0:1], -1.0, s4[:, 3:4], ALU.mult, ALU.mult)
                    self.act(hs_, hs_, AF.Identity, bias=s4[:, 4:5], scale=s4[:, 3:4])

                def L7(t):
                    hs_ = hs[t % 4]
                    self.tt("pool", hs_, hs_, normg, ALU.mult)

                def L7b(t):
                    i, _ = tsl_(t)
                    self.tt("pool", hb[t % 2], hs[t % 4], so[:, i, :], ALU.mult)

                def L8(t):
                    i, _ = tsl_(t)
                    hb_ = hb[t % 2]
                    for k2 in range(2):
                        self.transpose(psT[:, k2 * 128:(k2 + 1) * 128], hb_[:, k2 * 128:(k2 + 1) * 128], self.ident_bf)
                    for k2 in range(2):
                        dst = TR(hk[2 * h + k2][i // 4], hT_all.ap[:, 2 * h + k2, i * 128:(i + 1) * 128])
                        self.copy("act", dst, psT[:, k2 * 128:(k2 + 1) * 128])

                stages_ = [(0, V0), (1, L1), (2, L2), (3, L3), (4, L4)]
                if dirn == 1:
                    stages_ += [(5, L5), (6, L6), (7, L7), (8, L7b), (9, L8)]
                maxlag = max(l_ for l_, _ in stages_)
                for it in range(16 + maxlag):
                    for l_, fn_ in sorted(stages_, key=lambda x: -x[0]):
                        t = it - l_
                        if 0 <= t < 16:
                            fn_(t)
                if DBG == 13 or (DBG == 15 and dirn == 1):
                    return
                if dirn == 0:
                    self.exchange(self.d_bounce_s, self.d_gath_s, ("s", j, h), [(C32[0], 0, 260)], rcv)
                    self.select_partner(C32[0], rcv, 0, 260)
                    self.copy("act", Cb[0][:, 0:257], C32[0][:, 0:257])
                    if DBG == 14:
                        return
        P.fence()
        self.arena_off = mark
        scratch = self.ln_scratch()
        ml_pend = []
        for st in range(2):
            tts = [2 * st, 2 * st + 1]
            ml_pend = self.outproj_ln(tts, lambda i: [TR(hk[k][tts[i]], hT_all.ap[:, k, tts[i] * 512:(tts[i] + 1) * 512]) for k in range(8)],
                                      lambda c: gwo_base + c, 128, 8 * 128,
                                      lambda slot, k: slot[:, k * 128:(k + 1) * 128],
                                      ALPHA, ln_idx, LN_EPS, scratch, defer=("all" if st == 0 else "last"), drain=ml_pend)
        self.pend = ml_pend


def _lay_w512(w, cols):
    out = np.zeros((128, DC, 512), np.float32)
    out[:, :, :len(cols)] = w[:, cols].reshape(DC, 128, len(cols)).transpose(1, 0, 2)
    return out.reshape(128, DC * 512)


def _lay_wo(w, c, kp, nk):
    out = np.zeros((128, FC * 128), np.float32)
    blk = w[:, c * 128:(c + 1) * 128].reshape(nk, kp, 128).transpose(1, 0, 2).reshape(kp, nk * 128)
    out[:kp, :nk * 128] = blk
    return out


def _prepare(inputs, stages_spec):
    x = np.asarray(inputs["x"], np.float32)
    ffn_w_in = np.asarray(inputs["ffn_w_in"], np.float32)
    ffn_w_out = np.asarray(inputs["ffn_w_out"], np.float32)
    w512 = []
    wo = []
    stages = []
    for spec in stages_spec:
        if spec[0] == "ffn":
            l, i = spec[1], spec[2]
            base512 = len(w512)
            wi = ffn_w_in[l, i]
            for jg in range(11):
                cols = list(range(jg * 256, (jg + 1) * 256)) + list(range(DFF + jg * 256, DFF + (jg + 1) * 256))
                w512.append(_lay_w512(wi, cols))
            basewo = len(wo)
            for c in range(DC):
                wo.append(_lay_wo(ffn_w_out[l, i], c, 128, FC))
            stages.append(("ffn", base512, basewo, l * 3 + (0 if i == 0 else 2)))
        elif spec[0] == "attn":
            j = spec[1]
            w = np.asarray(inputs["at_w_in"], np.float32)[j]
            base512 = len(w512)
            w512.append(_lay_w512(w, list(range(0, 512))))
            w512.append(_lay_w512(w, list(range(512, 1024))))
            kc = []
            for k in range(4):
                kc += list(range(1024 + k * 64, 1024 + (k + 1) * 64)) * 2
            w512.append(_lay_w512(w, kc))
            w512.append(_lay_w512(w, list(range(1280, 1536))))
            basewo = len(wo)
            wout = np.asarray(inputs["at_w_out"], np.float32)[j]
            for c in range(DC):
                wo.append(_lay_wo(wout, c, 64, 16))
            stages.append(("attn", base512, basewo, (2 * j + 1) * 3 + 1, j))
        elif spec[0] == "mlstm":
            j = spec[1]
            w = np.asarray(inputs["ml_w_in"], np.float32)[j]
            base512 = len(w512)
            for h in range(4):
                cols = (list(range(h * 128, (h + 1) * 128)) + list(range(512 + h * 128, 512 + (h + 1) * 128))
                        + list(range(1024 + h * 256, 1024 + (h + 1) * 256)))
                w512.append(_lay_w512(w, cols))
                w512.append(_lay_w512(w, list(range(2048 + h * 256, 2048 + (h + 1) * 256))))
            basewo = len(wo)
            wout = np.asarray(inputs["ml_w_out"], np.float32)[j]
            for c in range(DC):
                wo.append(_lay_wo(wout, c, 128, 8))
            stages.append(("mlstm", base512, basewo, (2 * j) * 3 + 1, j))
        else:
            raise NotImplementedError(spec)
    w512 = np.stack(w512) if w512 else np.zeros((1, 128, 4096), np.float32)
    wo = np.stack(wo) if wo else np.zeros((1, 128, FC * 128), np.float32)
    ml_w_in = np.asarray(inputs["ml_w_in"], np.float32)
    wg_even = np.zeros((2, 128, DC * 16), np.float32)
    wg_odd = np.zeros((2, 128, DC * 16), np.float32)
    for j in range(2):
        g = ml_w_in[j][:, 3072:3088]
        wg_even[j] = g.reshape(DC, 128, 16).transpose(1, 0, 2).reshape(128, DC * 16)
        g2 = g.reshape(1024, 4, 4)[:, [2, 3, 0, 1], :].reshape(1024, 16)
        wg_odd[j] = g2.reshape(DC, 128, 16).transpose(1, 0, 2).reshape(128, DC * 16)
    normg = np.ascontiguousarray(np.broadcast_to(np.asarray(inputs["ml_norm_g"], np.float32)[:, None, :], (2, 128, 1024)))
    in_maps = []
    for core in range(NCORES):
        b, half = core // 2, core % 2
        xs = x[b, half * T:(half + 1) * T]
        if half:
            xs = xs[::-1]
        xT = np.ascontiguousarray(xs.reshape(T, DC, 128).transpose(2, 1, 0)).reshape(128, DC * T)
        consts = _const_table(core, np.asarray(inputs["ln_g"], np.float32), np.asarray(inputs["ln_b"], np.float32),
                              np.asarray(inputs["ml_gate_b"], np.float32), np.asarray(inputs["at_sink"], np.float32))
        in_maps.append({"xT": xT, "w512": w512, "wo": wo, "wg": wg_odd if half else wg_even, "normg": normg,
                        "consts": consts})
    return stages, in_maps, w512.shape[0], wo.shape[0]


def _assemble(results):
    out = np.zeros((4, 2 * T, D), np.float32)
    for core in range(NCORES):
        b, half = core // 2, core % 2
        o = np.asarray(results[core]["outT"]).reshape(128, DC, T).transpose(2, 1, 0).reshape(T, D)
        if half:
            o = o[::-1]
        out[b, half * T:(half + 1) * T] = o
    return out


FULL_SPEC = []
for _l in range(DEPTH):
    FULL_SPEC.append(("ffn", _l, 0))
    FULL_SPEC.append(("mlstm", _l // 2) if _l % 2 == 0 else ("attn", _l // 2))
    FULL_SPEC.append(("ffn", _l, 1))


def run_spec(inputs, spec, trace=False):
    stages, in_maps, n512, nwo = _prepare(inputs, spec)
    b = Builder(stages, n512, nwo)
    nc = b.build()
    res = run_bass_kernel_spmd(nc, in_maps, core_ids=list(range(NCORES)))
    return _assemble(res.results)


def kernel(**inputs):
    return run_spec(inputs, FULL_SPEC)
```

```python
import numpy as np
from contextlib import ExitStack
import concourse.bass as bass
import concourse.mybir as mybir
from concourse.bass_utils import run_bass_kernel_spmd

F32 = mybir.dt.float32
BF16 = mybir.dt.bfloat16
AF = mybir.ActivationFunctionType
ALU = mybir.AluOpType

D = 1024
DC = 8
T = 2048
NT = 16
DFF = 2816
FC = 22
DEPTH = 4
ALPHA = float((2 * DEPTH) ** 0.25)
LN_EPS = 1e-5
NCORES = 8
import os
DBG = int(os.environ.get("KDBG", "0"))


class TR:
    __slots__ = ("key", "ap")

    def __init__(self, key, ap):
        self.key = key
        self.ap = ap

    def __getitem__(self, idx):
        return TR(self.key, self.ap[idx])


class Op:
    __slots__ = ("eng", "fn", "dma", "deps", "needs_inc", "tick", "inc")

    def __init__(self, eng, fn, dma, inc=16):
        self.eng = eng
        self.fn = fn
        self.dma = dma
        self.inc = inc
        self.deps = ()
        self.needs_inc = False
        self.tick = 0


ENGS = ("pe", "act", "dve", "pool", "sp")


class Prog:
    def __init__(self):
        self.ops = []
        self.last_w = {}
        self.readers = {}
        self.last_on = {}

    def add(self, eng, fn, reads=(), writes=(), dma=None, extra_deps=(), inc=16):
        idx = len(self.ops)
        op = Op(eng, fn, dma, inc)
        deps = set(extra_deps)
        rk = [r.key if isinstance(r, TR) else r for r in reads]
        wk = [w.key if isinstance(w, TR) else w for w in writes]
        for k in rk:
            if k in self.last_w:
                deps.add(self.last_w[k])
        for k in wk:
            if k in self.last_w:
                deps.add(self.last_w[k])
            deps.update(self.readers.get(k, ()))
        op.deps = deps
        for k in rk:
            self.readers.setdefault(k, []).append(idx)
        for k in wk:
            self.last_w[k] = idx
            self.readers[k] = []
        self.ops.append(op)
        if dma is None:
            self.last_on[eng] = idx
        return idx

    def fence(self):
        lasts = list(self.last_on.values())
        for e in ("pe", "act", "dve", "pool", "sp"):
            self.add(e, None, extra_deps=[i for i in lasts])

    def emit(self, nc, block, engsem, dmasems):
        ops = self.ops
        for op in ops:
            for d in op.deps:
                dop = ops[d]
                if dop.dma is None:
                    if dop.eng == "pe" and op.eng == "pe" and op.dma is None:
                        continue
                    dop.needs_inc = True
        cnt = {e: 0 for e in ENGS}
        dcnt = {}
        for op in ops:
            if op.dma is not None:
                dcnt[op.dma] = dcnt.get(op.dma, 0) + op.inc
                op.tick = dcnt[op.dma]
            elif op.needs_inc:
                cnt[op.eng] += 1
                op.tick = cnt[op.eng]
        groups = sorted(dcnt.keys(), key=str)
        assert len(groups) <= len(dmasems), (len(groups), len(dmasems))
        gsem = {g: dmasems[i] for i, g in enumerate(groups)}
        per_eng = {e: [] for e in ENGS}
        for op in ops:
            per_eng[op.eng].append(op)

        def run(engname, e):
            waited = {}
            for op in per_eng[engname]:
                need = {}
                for d in op.deps:
                    dop = ops[d]
                    if dop.dma is not None:
                        s = ("d", dop.dma)
                    else:
                        if dop.eng == "pe" and op.eng == "pe" and op.dma is None:
                            continue
                        s = ("e", dop.eng)
                    if dop.tick > need.get(s, 0):
                        need[s] = dop.tick
                for s, v in need.items():
                    if waited.get(s, 0) >= v:
                        continue
                    waited[s] = v
                    sem = gsem[s[1]] if s[0] == "d" else engsem[s[1]]
                    e.wait_ge(sem, v)
                if op.fn is None:
                    if op.needs_inc:
                        e.sem_inc(engsem[engname], 1)
                    continue
                inst = op.fn(e)
                if op.dma is not None:
                    if op.inc == 16:
                        inst.then_inc(gsem[op.dma], 16)
                    else:
                        inst.then_inc(gsem[op.dma])
                elif op.needs_inc:
                    inst.then_inc(engsem[engname], 1)

        @block.tensor
        def _(e):
            run("pe", e)

        @block.scalar
        def _(e):
            run("act", e)

        @block.vector
        def _(e):
            run("dve", e)

        @block.gpsimd
        def _(e):
            run("pool", e)

        @block.sync
        def _(e):
            run("sp", e)


C_LNG = 0
C_LNB = 96
C_DIST = 192
C_MASKF = 704
C_MASKB = 832
C_MTRIF = 960
C_MTRIB = 1088
C_MBLK = 1216
C_MIND0 = 1344
C_MIND1 = 1472
C_IDENT = 1600
C_ONES = 1728
C_SEL = 1856
C_GBIAS = 1858
C_SINK = 1890
C_SLOPE = 1922
NCONST = 1938

BIGDIST = 1.0e6


def _const_table(core, ln_g, ln_b, ml_gate_b, at_sink):
    c = np.zeros((128, NCONST), np.float32)
    odd = core % 2
    p = np.arange(128)
    for l in range(4):
        for i in range(3):
            for dc in range(8):
                c[:, C_LNG + (l * 3 + i) * 8 + dc] = ln_g[l, i, dc * 128:(dc + 1) * 128]
                c[:, C_LNB + (l * 3 + i) * 8 + dc] = ln_b[l, i, dc * 128:(dc + 1) * 128]
    s = p[:, None].astype(np.float64)
    q = p[None, :].astype(np.float64)
    d = q - s + 128
    c[:, C_DIST + 0:C_DIST + 128] = np.where(d <= 128, d, BIGDIST)
    d = np.abs(q - s)
    c[:, C_DIST + 128:C_DIST + 256] = d
    d = s + 128 - q
    c[:, C_DIST + 256:C_DIST + 384] = np.where(d <= 128, d, BIGDIST)
    d = 255 - s - q
    c[:, C_DIST + 384:C_DIST + 512] = np.where(d <= 128, d, BIGDIST)
    same = np.ones((128, 128), bool)
    maskF = (same & (p[:, None] <= p[None, :])).astype(np.float32)
    maskB = (same & (p[:, None] >= p[None, :])).astype(np.float32)
    c[:, C_MASKF:C_MASKF + 128] = maskF
    c[:, C_MASKB:C_MASKB + 128] = maskB
    c[:, C_MTRIF:C_MTRIF + 128] = -maskF
    c[:, C_MTRIB:C_MTRIB + 128] = -maskB
    c[:, C_MBLK:C_MBLK + 128] = -same.astype(np.float32)
    c[:, C_MIND0:C_MIND0 + 128] = -(p[:, None] < 64).astype(np.float32) * np.ones((1, 128), np.float32)
    c[:, C_MIND1:C_MIND1 + 128] = -(p[:, None] >= 64).astype(np.float32) * np.ones((1, 128), np.float32)
    c[:, C_IDENT:C_IDENT + 128] = np.eye(128, dtype=np.float32)
    c[:, C_ONES:C_ONES + 128] = 1.0
    c[:, C_SEL + (1 - odd)] = 1.0
    for j in range(2):
        gb = ml_gate_b[j].reshape(4, 4)
        if odd:
            gb = gb[[2, 3, 0, 1]]
        c[:, C_GBIAS + j * 16:C_GBIAS + (j + 1) * 16] = gb.reshape(1, 16)
        c[:, C_SINK + j * 16:C_SINK + (j + 1) * 16] = at_sink[j].reshape(1, 16)
    slopes = np.exp2(-8.0 * np.arange(1, 17, dtype=np.float32) / 16).astype(np.float32)
    c[:, C_SLOPE:C_SLOPE + 16] = (-8.0 * slopes).reshape(1, 16)
    return c


def alibi_slopes():
    return [float(np.exp2(np.float32(-8.0) * np.float32(h + 1) / np.float32(16))) for h in range(16)]


class Builder:
    def __init__(self, stages, n_w512, n_wo):
        self.stages = stages
        nc = bass.Bass("TRN2", target_bir_lowering=False)
        self.nc = nc
        self.P = Prog()
        self.d_x = nc.dram_tensor("xT", [128, DC * T], F32, kind="ExternalInput").ap()
        self.d_out = nc.dram_tensor("outT", [128, DC * T], F32, kind="ExternalOutput").ap()
        self.d_w512 = nc.dram_tensor("w512", [n_w512, 128, 4096], F32, kind="ExternalInput").ap()
        self.d_wo = nc.dram_tensor("wo", [n_wo, 128, FC * 128], F32, kind="ExternalInput").ap()
        self.d_wg = nc.dram_tensor("wg", [2, 128, DC * 16], F32, kind="ExternalInput").ap()
        self.d_normg = nc.dram_tensor("normg", [2, 128, 1024], F32, kind="ExternalInput").ap()
        self.d_const = nc.dram_tensor("consts", [128, NCONST], F32, kind="ExternalInput").ap()
        self.d_bounce_s = nc.dram_tensor("bounce_s", [128, 260], F32).ap()
        self.d_gath_s = nc.dram_tensor("gath_s", [256, 260], F32).ap()
        self.d_bounce_h = nc.dram_tensor("bounce_h", [128, 768], BF16).ap()
        self.d_gath_h = nc.dram_tensor("gath_h", [256, 768], BF16).ap()
        self.uid = 0

    def key(self, name):
        self.uid += 1
        return (name, self.uid)

    def carve_reset(self):
        self.arena_off = 0

    def carve(self, name, shape, dtype, key=None):
        n = 1
        for s in shape[1:]:
            n *= s
        nbytes = n * (2 if dtype == BF16 else 4)
        nbytes = (nbytes + 31) // 32 * 32
        off = self.arena_off
        assert off + nbytes <= self.ARENA_BYTES, (name, off, nbytes, self.ARENA_BYTES)
        self.arena_off += nbytes
        ap = self.arena[:, off // 4:(off + nbytes) // 4]
        if dtype == BF16:
            ap = ap.bitcast(BF16)[:, 0:n]
        else:
            ap = ap[:, 0:n]
        if len(shape) == 3:
            ap = ap.rearrange("p (a b) -> p a b", a=shape[1])
        elif len(shape) == 4:
            ap = ap.rearrange("p (a b c) -> p a b c", a=shape[1], b=shape[2])
        if shape[0] != 128:
            ap = ap[0:shape[0]]
        return TR(key if key is not None else self.key(name), ap)

    def eps_col(self, eps):
        if eps not in self.eps_cols:
            i = len(self.eps_cols)
            tr = TR(("epsc", i), self.epsbuf[:, i:i + 1])
            self.memset("pool", tr, float(eps))
            self.eps_cols[eps] = tr
        return self.eps_cols[eps]

    def mm(self, out, lhsT, rhs, start=True, stop=True):
        rd = [lhsT, rhs] + ([] if start else [out])
        self.P.add("pe", lambda e: e.matmul(out.ap, lhsT.ap, rhs.ap, start=start, stop=stop),
                   reads=rd, writes=[out])

    def transpose(self, out, in_, ident):
        self.P.add("pe", lambda e: e.transpose(out.ap, in_.ap, ident.ap), reads=[in_, ident], writes=[out])

    def act(self, out, in_, func, bias=None, scale=None, extra_reads=(), eng="act"):
        kw = {}
        rd = [in_] + list(extra_reads)
        if bias is not None:
            kw["bias"] = bias.ap if isinstance(bias, TR) else bias
            if isinstance(bias, TR):
                rd.append(bias)
        if scale is not None:
            kw["scale"] = scale.ap if isinstance(scale, TR) else scale
            if isinstance(scale, TR):
                rd.append(scale)
        self.P.add(eng, lambda e: e.activation(out.ap, in_.ap, func, **kw), reads=rd, writes=[out])

    def tt(self, eng, out, in0, in1, op):
        self.P.add(eng, lambda e: e.tensor_tensor(out.ap, in0.ap, in1.ap, op), reads=[in0, in1], writes=[out])

    def ts(self, eng, out, in0, s1, s2, op0, op1=None):
        rd = [in0]
        a1 = s1.ap if isinstance(s1, TR) else s1
        a2 = s2.ap if isinstance(s2, TR) else s2
        if isinstance(s1, TR):
            rd.append(s1)
        if isinstance(s2, TR):
            rd.append(s2)
        if op1 is None:
            self.P.add(eng, lambda e: e.tensor_scalar(out.ap, in0.ap, a1, None, op0), reads=rd, writes=[out])
        else:
            self.P.add(eng, lambda e: e.tensor_scalar(out.ap, in0.ap, a1, a2, op0, op1), reads=rd, writes=[out])

    def stt(self, eng, out, in0, scalar, in1, op0, op1):
        rd = [in0, in1]
        a = scalar.ap if isinstance(scalar, TR) else scalar
        if isinstance(scalar, TR):
            rd.append(scalar)
        self.P.add(eng, lambda e: e.scalar_tensor_tensor(out.ap, in0.ap, a, in1.ap, op0, op1),
                   reads=rd, writes=[out])

    def copy(self, eng, out, in_):
        if eng == "act":
            self.P.add("act", lambda e: e.activation(out.ap, in_.ap, AF.Copy), reads=[in_], writes=[out])
        else:
            self.P.add(eng, lambda e: e.tensor_copy(out.ap, in_.ap), reads=[in_], writes=[out])

    def memset(self, eng, out, val):
        self.P.add(eng, lambda e: e.memset(out.ap, val), reads=[], writes=[out])

    def dma(self, eng, out, in_, group, reads=(), writes=None):
        o = out.ap if isinstance(out, TR) else out
        i = in_.ap if isinstance(in_, TR) else in_
        rd = list(reads) + ([in_] if isinstance(in_, TR) else [])
        wr = ([out] if isinstance(out, TR) else []) if writes is None else writes
        self.P.add(eng, lambda e: e.dma_start(out=o, in_=i), reads=rd, writes=wr, dma=group)

    def load_w512(self, g, ncols=512):
        s = self.win_rr % len(self.win_slots)
        self.win_rr += 1
        slot = self.win_slots[s]
        if ncols == 512:
            self.dma("pool", slot, self.d_w512[g].rearrange("p (a b) -> p a b", a=DC), group=("win", s))
        else:
            self.dma("pool", slot[:, :, 0:ncols],
                     self.d_w512[g].rearrange("p (a b) -> p a b", a=DC)[:, :, 0:ncols], group=("win", s))
        return slot

    def load_wo(self, g, nparts=128, ncols=FC * 128):
        s = self.wo_rr % len(self.wo_slots)
        self.wo_rr += 1
        slot = self.wo_slots[s]
        self.dma("pool", slot[0:nparts, 0:ncols], self.d_wo[g][0:nparts, 0:ncols], group=("wo", s))
        return slot

    def build(self):
        nc = self.nc
        P = self.P
        with ExitStack() as es:
            def sb(name, shape, dt):
                return es.enter_context(nc.sbuf_tensor(name, shape, dt))

            xs = sb("xres", [128, DC, T], F32)
            cst = sb("cst", [128, NCONST], F32)
            cbf = sb("cbf", [128, 256], BF16)
            self.epsbuf = sb("epsbuf", [128, 8], F32)
            self.eps_cols = {}
            win = [sb("win%d" % i, [128, DC, 512], BF16) for i in range(3)]
            wo = [sb("wo%d" % i, [128, FC * 128], BF16) for i in range(2)]
            self.ARENA_BYTES = 100 * 1024 - 6144
            lnstat = sb("lnstat", [128, 3, 512], F32)
            self.ln_mean = TR("ln_mean", lnstat[:, 0, :])
            self.ln_msq = TR("ln_msq", lnstat[:, 1, :])
            self.ln_rstd = TR("ln_rstd", lnstat[:, 2, :])
            self.pend = []
            arena_t = sb("arena", [128, self.ARENA_BYTES // 4], F32)
            self.arena = arena_t[:]
            psum = [es.enter_context(nc.psum_tensor("ps%d" % i, [128, 512], F32)) for i in range(8)]
            engsem = {e: es.enter_context(nc.semaphore("sem_" + e)) for e in ENGS}
            dmasems = [es.enter_context(nc.semaphore("dsem%d" % i)) for i in range(60)]

            self.x = [[TR(("x", dc, tt), xs[:, dc, tt * 512:(tt + 1) * 512]) for tt in range(4)] for dc in range(DC)]
            self.cst = TR("cst", cst[:])
            self.ident_bf = TR("cbf", cbf[:, 0:128])
            self.ones_bf = TR("cbf", cbf[:, 128:256])
            self.win_slots = [TR(("win", i), win[i][:]) for i in range(3)]
            self.wo_slots = [TR(("wo", i), wo[i][:]) for i in range(2)]
            self.win_rr = 0
            self.wo_rr = 0
            self.ps = [TR(("ps", i), psum[i][:]) for i in range(8)]

            self.dma("sp", self.cst, self.d_const, group="cst")
            dx3 = self.d_x.rearrange("p (a b) -> p a b", a=DC)
            prev = []
            for tt in range(4):
                cur_ = []
                for hf in range(2):
                    o_ = xs[:, hf * 4:(hf + 1) * 4, tt * 512:(tt + 1) * 512]
                    i_ = dx3[:, hf * 4:(hf + 1) * 4, tt * 512:(tt + 1) * 512]
                    cur_.append(P.add("sp", lambda e, o_=o_, i_=i_: e.dma_start(out=o_, in_=i_),
                                      writes=[self.x[dc][tt] for dc in range(hf * 4, hf * 4 + 4)],
                                      dma=("xin", tt, hf), extra_deps=prev))
                prev = cur_
            self.copy("dve", TR("cbf", cbf[:, 0:128]), self.cst[:, C_IDENT:C_IDENT + 128])
            self.copy("dve", TR("cbf", cbf[:, 128:256]), self.cst[:, C_ONES:C_ONES + 128])

            for si_, st in enumerate(self.stages):
                if si_ > 0:
                    P.fence()
                self.carve_reset()
                if st[0] == "ffn":
                    self.stage_ffn(st[1], st[2], st[3])
                elif st[0] == "attn":
                    self.stage_attn(st[1], st[2], st[3], st[4])
                elif st[0] == "mlstm":
                    self.stage_mlstm(st[1], st[2], st[3], st[4])

            self.drain()
            outs = []
            do3 = self.d_out.rearrange("p (a b) -> p a b", a=DC)
            for tt in range(4):
                for hf in range(2):
                    k = self.key("outdma")
                    outs.append(k)
                    self.dma("sp", do3[:, hf * 4:(hf + 1) * 4, tt * 512:(tt + 1) * 512],
                             TR(("xo", tt, hf), xs[:, hf * 4:(hf + 1) * 4, tt * 512:(tt + 1) * 512]),
                             group=("xout", tt, hf), reads=[self.x[dc][tt] for dc in range(hf * 4, hf * 4 + 4)], writes=[k])
            P.add("sp", None, reads=outs)

            with nc.Block() as block:
                P.emit(nc, block, engsem, dmasems)
        return nc

    def cast_xb(self, xb, tts):
        n = 0
        for i, tt in enumerate(tts):
            for dc in range(DC):
                eng = "act" if n % 2 == 0 else "dve"
                self.copy(eng, xb[dc][i], self.x[dc][tt])
                n += 1

    def outproj_ln(self, tts, kchunks, wo_group, wo_parts, wo_ncols, lhsT_of, resid_scale, ln_idx, eps,
                   scratch, defer=None, drain=None):
        zb, zsq, meanb, msq, rstd = scratch
        nk = len(kchunks(0))
        ps_y = [self.ps[0], self.ps[1]]
        ps_sum = [self.ps[4], self.ps[6]]
        ps_sq = [self.ps[5], self.ps[7]]
        pending = []

        def flush():
            for (i, c) in pending:
                self.mm(ps_sum[i], self.ones_bf, zb[(c * 2 + i) % len(zb)], start=(c == 0), stop=(c == DC - 1))
                self.mm(ps_sq[i], self.ones_bf, zsq[(c * 2 + i) % len(zsq)], start=(c == 0), stop=(c == DC - 1))
            pending.clear()

        n = 0
        for c in range(DC):
            if c == DC - 1 and drain:
                while drain:
                    drain.pop(0)()
            slot = self.load_wo(wo_group(c), wo_parts, wo_ncols)
            for i, tt in enumerate(tts):
                py = ps_y[n % 2]
                n += 1
                ks = kchunks(i)
                for k in range(nk):
                    self.mm(py, lhsT_of(slot, k), ks[k], start=(k == 0), stop=(k == nk - 1))
                flush()
                for _ in range(4):
                    if drain:
                        drain.pop(0)()
                xc = self.x[c][tt]
                self.stt("dve", xc, xc, resid_scale, py, ALU.mult, ALU.add)
                self.copy("act", zb[(c * 2 + i) % len(zb)], xc)
                self.act(zsq[(c * 2 + i) % len(zsq)], xc, AF.Square)
                pending.append((i, c))
        flush()
        g0 = C_LNG + ln_idx * 8
        b0 = C_LNB + ln_idx * 8
        th = []
        for i, tt in enumerate(tts):
            th.append((i, lambda i=i: self.act(meanb[i], ps_sum[i], AF.Copy, scale=1.0 / D)))
            th.append((i, lambda i=i: self.tt("dve", msq[i], meanb[i], meanb[i], ALU.mult)))
            th.append((i, lambda i=i: self.stt("dve", rstd[i], ps_sq[i], 1.0 / D, msq[i], ALU.mult, ALU.subtract)))
            th.append((i, lambda i=i: self.act(rstd[i], rstd[i], AF.Sqrt, bias=self.eps_col(eps))))
            th.append((i, lambda i=i: self.P.add("dve", lambda e, o=rstd[i]: e.reciprocal(o.ap, o.ap), reads=[rstd[i]], writes=[rstd[i]])))
        for i, tt in enumerate(tts):
            for c in range(DC):
                th.append((i, lambda i=i, tt=tt, c=c: self.tt("dve", self.x[c][tt], self.x[c][tt], meanb[i], ALU.subtract)))
            for c in range(DC):
                th.append((i, lambda i=i, tt=tt, c=c: self.tt("dve", self.x[c][tt], self.x[c][tt], rstd[i], ALU.mult)))
            for c in range(DC):
                th.append((i, lambda tt=tt, c=c: self.act(self.x[c][tt], self.x[c][tt], AF.Identity,
                                                           bias=self.cst[:, b0 + c:b0 + c + 1], scale=self.cst[:, g0 + c:g0 + c + 1])))
        if defer == "all":
            return [f_ for _, f_ in th]
        keep = []
        for i, f_ in th:
            if defer == "last" and i == len(tts) - 1:
                keep.append(f_)
            else:
                f_()
        return keep

    def ln_scratch(self, ntt=2, nz=4):
        zb = [self.carve("zb", [128, 512], BF16) for _ in range(nz)]
        zsq = [self.carve("zsq", [128, 512], BF16) for _ in range(nz)]
        if ntt == 1:
            return (zb, zsq, [self.ln_mean], [self.ln_msq], [self.ln_rstd])
        lm = self.carve("lmean", [128, 512], F32)
        lq = self.carve("lmsq", [128, 512], F32)
        lr = self.carve("lrstd", [128, 512], F32)
        return (zb, zsq, [lm, self.ln_mean], [lq, self.ln_msq], [lr, self.ln_rstd])

    def drain(self, n=None):
        while self.pend and (n is None or n > 0):
            self.pend.pop(0)()
            if n is not None:
                n -= 1

    def stage_ffn(self, g512_base, gwo_base, ln_idx):
        xb = [[self.carve("xb", [128, 512], BF16) for _ in range(2)] for _ in range(DC)]
        h = [[self.carve("h", [128, 512], BF16) for _ in range(2)] for _ in range(FC)]
        sg = [self.carve("sg", [128, 512], F32) for _ in range(2)]
        scratch = self.ln_scratch()
        self.eps_col(4.0 * LN_EPS)
        self.cast_xb(xb, [0, 1])
        n = 0
        for st in range(2):
            tts = [2 * st, 2 * st + 1]
            for jg in range(11):
                slot = self.load_w512(g512_base + jg)
                for i in range(2):
                    for half in range(2):
                        j = 2 * jg + half
                        pg = self.ps[(n % 2) * 2]
                        pu = self.ps[(n % 2) * 2 + 1]
                        n += 1
                        for dc in range(DC):
                            self.mm(pg, slot[:, dc, half * 128:(half + 1) * 128], xb[dc][i],
                                    start=(dc == 0), stop=(dc == DC - 1))
                        for dc in range(DC):
                            self.mm(pu, slot[:, dc, 256 + half * 128:256 + (half + 1) * 128], xb[dc][i],
                                    start=(dc == 0), stop=(dc == DC - 1))
                        s_ = sg[n % 2]
                        self.act(s_, pg, AF.Silu)
                        self.tt("dve", h[j][i], s_, pu, ALU.mult)
                        self.drain(2)
            self.drain()
            if st == 0:
                self.cast_xb(xb, [2, 3])
            self.pend = self.outproj_ln(tts, lambda i: [h[k][i] for k in range(FC)],
                                        lambda c: gwo_base + c, 128, FC * 128,
                                        lambda slot, k: slot[:, k * 128:(k + 1) * 128],
                                        2.0 * ALPHA, ln_idx, 4.0 * LN_EPS, scratch,
                                        defer=("all" if st == 0 else "last"))

    def exchange(self, bounce, gath, tag, srcs, rcv):
        kb = ("dram", tag[0], "b")
        kg = ("dram", tag[0], "g")
        for (tr, c0, ncol) in srcs:
            self.dma("sp", bounce[:, c0:c0 + ncol], tr, group=("bnc", tag), writes=[kb])
        self.P.add("pool", lambda e: e.collective_compute(
            "AllGather", ALU.bypass, replica_groups=[[0, 1], [2, 3], [4, 5], [6, 7]],
            ins=[bounce.opt()], outs=[gath.opt()]), reads=[kb], writes=[kg], dma=("cc", tag), inc=1)
        self.dma("sp", rcv, gath.rearrange("(r p) n -> p r n", p=128), group=("rcv", tag), reads=[kg])

    def select_partner(self, dst, rcv, c0, ncol):
        sel0 = self.cst[:, C_SEL:C_SEL + 1]
        sel1 = self.cst[:, C_SEL + 1:C_SEL + 2]
        self.ts("dve", dst, rcv[:, 0, c0:c0 + ncol], sel0, None, ALU.mult)
        self.stt("dve", dst, rcv[:, 1, c0:c0 + ncol], sel1, dst, ALU.mult, ALU.add)

    def stage_attn(self, g512_base, gwo_base, ln_idx, j):
        P = self.P
        qT_all = self.carve("qT", [128, 8, T], BF16)
        qk = [[self.key("qT") for _ in range(4)] for _ in range(8)]
        kT_all = self.carve("kT", [128, 4, T + 128], BF16)
        kk = [[self.key("kT") for _ in range(5)] for _ in range(4)]
        V_all = self.carve("V", [128, 17, 256], BF16)
        Vb = [TR(self.key("V"), V_all.ap[:, b, :]) for b in range(17)]
        esink = self.carve("esink", [128, 16], F32)
        mark = self.arena_off
        xb = [[self.carve("xb", [128, 512], BF16) for _ in range(2)] for _ in range(DC)]
        rcv = self.carve("rcvh", [128, 2, 768], BF16)

        self.act(esink, self.cst[:, C_SINK + j * 16:C_SINK + (j + 1) * 16], AF.Exp)
        n = 0
        for st in range(2):
            tts = [2 * st, 2 * st + 1]
            if st == 1:
                self.drain()
            self.cast_xb(xb, tts)
            sA = self.load_w512(g512_base + 0)
            sB = self.load_w512(g512_base + 1)
            sC = self.load_w512(g512_base + 2)
            for c in range(8):
                slot = sA if c < 4 else sB
                for i, tt in enumerate(tts):
                    ps = self.ps[n % 4]
                    for dc in range(DC):
                        self.mm(ps, slot[:, dc, (c % 4) * 128:(c % 4 + 1) * 128], xb[dc][i],
                                start=(dc == 0), stop=(dc == DC - 1))
                    dst = TR(qk[c][tt], qT_all.ap[:, c, tt * 512:(tt + 1) * 512])
                    self.copy("act" if n % 2 == 0 else "dve", dst, ps)
                    n += 1
                    self.drain(4)
                if c == 3:
                    sV = self.load_w512(g512_base + 3, 256)
            for k in range(4):
                for i, tt in enumerate(tts):
                    ps = self.ps[n % 4]
                    for dc in range(DC):
                        self.mm(ps, sC[:, dc, k * 128:(k + 1) * 128], xb[dc][i], start=(dc == 0), stop=(dc == DC - 1))
                    dst = TR(kk[k][tt], kT_all.ap[:, k, tt * 512:(tt + 1) * 512])
                    self.copy("act" if n % 2 == 0 else "dve", dst, ps)
                    n += 1
            for i, tt in enumerate(tts):
                for tile in range(4):
                    ps = self.ps[n % 4]
                    for dc in range(DC):
                        self.mm(ps[:, 0:256], xb[dc][i][:, tile * 128:(tile + 1) * 128], sV[:, dc, 0:256],
                                start=(dc == 0), stop=(dc == DC - 1))
                    self.copy("act" if n % 2 == 0 else "dve", Vb[tt * 4 + tile], ps[:, 0:256])
                    n += 1
        if DBG == 1:
            return
        self.drain()
        srcs = [(TR(kk[k][3], kT_all.ap[:, k, T - 128:T]), k * 128, 128) for k in range(4)]
        srcs.append((Vb[15], 512, 256))
        self.exchange(self.d_bounce_h, self.d_gath_h, ("h", j), srcs, rcv)
        for k in range(4):
            self.select_partner(TR(kk[k][4], kT_all.ap[:, k, T:T + 128]), rcv, k * 128, 128)
        self.select_partner(Vb[16], rcv, 512, 256)
        P.fence()
        if DBG == 2:
            return
        self.arena_off = mark
        oT_all = self.carve("oT", [64, 16, 512], BF16)
        qz = self.carve("qz", [128, 8, 2, 128], BF16)
        self.memset("pool", qz, 0.0)
        Pm = [self.carve("Pm", [128, 512], BF16) for _ in range(3)]
        dz_off = self.arena_off
        dsum = [self.carve("dsum", [64, 512], F32) for _ in range(2)]
        end_off = self.arena_off
        self.arena_off = dz_off
        zb = [self.carve("zb", [128, 512], BF16) for _ in range(2)]
        zsq = [self.carve("zsq", [128, 512], BF16) for _ in range(2)]
        assert self.arena_off == end_off
        meanb = [self.ln_mean]
        scratch = (zb, zsq, [self.ln_mean], [self.ln_msq], [self.ln_rstd])
        diag = self.carve("diag", [128, 16, 2, 128], BF16)
        distbf = self.carve("distbf", [128, 4, 128], BF16)
        slopes = alibi_slopes()
        for kd in range(4):
            self.copy("dve", distbf[:, kd, :], self.cst[:, C_DIST + kd * 128:C_DIST + (kd + 1) * 128])
        t32 = meanb[0]
        for h in range(16):
            ta = t32[:, (h % 2) * 256:(h % 2) * 256 + 128]
            tb = t32[:, (h % 2) * 256 + 128:(h % 2) * 256 + 256]
            self.ts("dve", ta, self.cst[:, C_IDENT:C_IDENT + 128], float(np.float32(-8.0 * slopes[h])), None, ALU.mult)
            self.copy("act", diag[:, h, 0, :], ta)
            self.tt("dve", tb, ta, diag[:, h, 0, :], ALU.subtract)
            self.copy("act", diag[:, h, 1, :], tb)
        ones64 = self.ones_bf[:, 0:64]
        att_pend = []
        ps_s_t = [self.ps[0], self.ps[1]]
        ps_o_t = [self.ps[2], self.ps[3], self.ps[4]]
        ps_d_t = [self.ps[5], self.ps[6]]
        for st in range(4):
            ok = [self.key("oT") for _ in range(16)]
            subs = []
            u = 0
            for nb in range(4):
                nq = st * 4 + nb
                for k in range(4):
                    blocks = []
                    if nq > 0:
                        blocks.append((nq - 1, 0))
                    blocks.append((nq, 1))
                    blocks.append((nq + 1, 2) if nq < 15 else (16, 3))
                    for bi, (kb, kind) in enumerate(blocks):
                        subs.append(dict(nq=nq, nb=nb, k=k, kb=kb, kind=kind, first=(bi == 0),
                                         last=(bi == len(blocks) - 1), u=u, newq=(k == 0 and bi == 0)))
                    u += 1
            NS = len(subs)

            def A_Q(t):
                d = subs[t]
                if not d["newq"]:
                    return
                nq = d["nq"]
                for e_ in range(2):
                    src_ap = qT_all.ap[e_ * 64:(e_ + 1) * 64, :, nq * 128:(nq + 1) * 128]
                    dst_ap = qz.ap[e_ * 64:(e_ + 1) * 64, :, e_, :]
                    P.add("act", lambda e, d_=dst_ap, a=src_ap: e.activation(d_, a, AF.Copy),
                          reads=[qk[c][nq // 4] for c in range(8)], writes=[qz])

            def A_S(t):
                d = subs[t]
                k, kb = d["k"], d["kb"]
                ps_s = ps_s_t[t % 2]
                kkey = kk[k][kb // 4] if kb < 16 else kk[k][4]
                for hh in range(4):
                    c_ = 2 * k + hh // 2
                    e_ = hh % 2
                    h = 4 * k + hh
                    o_ap = ps_s.ap[:, hh * 128:(hh + 1) * 128]
                    l_ap = kT_all.ap[:, k, kb * 128:(kb + 1) * 128]
                    r_ap = qz.ap[:, c_, e_, :]
                    P.add("pe", lambda e, o_ap=o_ap, l_ap=l_ap, r_ap=r_ap: e.matmul(o_ap, l_ap, r_ap, start=True, stop=False),
                          reads=[kkey, qz], writes=[ps_s])
                    for hl in range(2):
                        d_ap = diag.ap[:, h, hl, :]
                        b_ap = distbf.ap[:, d["kind"], :]
                        P.add("pe", lambda e, o_ap=o_ap, d_ap=d_ap, b_ap=b_ap, hl=hl: e.matmul(o_ap, d_ap, b_ap, start=False, stop=(hl == 1)),
                              reads=[diag, distbf, ps_s], writes=[ps_s])

            def A_E(t):
                self.act(Pm[t % 3], ps_s_t[t % 2], AF.Exp, scale=0.125)

            def A_PV(t):
                d = subs[t]
                k, kb = d["k"], d["kb"]
                ps_o = ps_o_t[d["u"] % 3]
                ps_d = ps_d_t[d["u"] % 2]
                self.mm(ps_o[0:64, :], Vb[kb][:, k * 64:(k + 1) * 64], Pm[t % 3], start=d["first"], stop=d["last"])
                self.mm(ps_d[0:64, :], ones64, Pm[t % 3], start=d["first"], stop=d["last"])

            def A_N1(t):
                d = subs[t]
                if not d["last"]:
                    return
                ds = dsum[d["u"] % 2]
                ps_d = ps_d_t[d["u"] % 2]
                for hh in range(4):
                    h = 4 * d["k"] + hh
                    self.ts("dve", ds[:, hh * 128:(hh + 1) * 128], ps_d[0:64, hh * 128:(hh + 1) * 128],
                            esink[0:64, h:h + 1], None, ALU.add)

            def A_N2(t):
                d = subs[t]
                if not d["last"]:
                    return
                ds = dsum[d["u"] % 2]
                self.act(ds, ds, AF.Ln)
                self.act(ds, ds, AF.Exp, scale=-1.0)

            def A_N3(t):
                d = subs[t]
                if not d["last"]:
                    return
                ds = dsum[d["u"] % 2]
                ps_o = ps_o_t[d["u"] % 3]
                for hh in range(4):
                    h = 4 * d["k"] + hh
                    dst = TR(ok[h], oT_all.ap[:, h, d["nb"] * 128:(d["nb"] + 1) * 128])
                    self.tt("dve", dst, ps_o[0:64, hh * 128:(hh + 1) * 128], ds[:, hh * 128:(hh + 1) * 128], ALU.mult)

            stages_ = [(0, A_Q), (1, A_S), (2, A_E), (3, A_PV), (4, A_N1), (5, A_N2), (6, A_N3)]
            for _ in range(5):
                if att_pend:
                    att_pend.pop(0)()
            for it in range(NS + 6):
                for l_, fn_ in sorted(stages_, key=lambda x: -x[0]):
                    t = it - l_
                    if 0 <= t < NS:
                        fn_(t)
                if att_pend:
                    att_pend.pop(0)()
            while att_pend:
                att_pend.pop(0)()
            att_pend = self.outproj_ln([st], lambda i: [TR(ok[h], oT_all.ap[:, h, :]) for h in range(16)],
                                       lambda c: gwo_base + c, 64, 16 * 128,
                                       lambda slot, k: slot[0:64, k * 128:(k + 1) * 128],
                                       ALPHA, ln_idx, LN_EPS, scratch, defer="all")
        self.pend = att_pend

    def stage_mlstm(self, g512_base, gwo_base, ln_idx, j):
        P = self.P
        AX = mybir.AxisListType.X
        hT_all = self.carve("hT", [128, 8, T], BF16)
        hk = [[self.key("hT") for _ in range(4)] for _ in range(8)]
        xb = [self.carve("xb", [128, 512], BF16) for _ in range(DC)]
        gtmp = self.carve("gtmp", [128, 512], F32)

        def gview(name, k_):
            return TR(self.key(name), gtmp.ap[:, k_ * 128:(k_ + 1) * 128].rearrange("p (t d h) -> p t d h", t=16, d=2))
        lf = gview("lf", 0)
        etmp = gview("etmp", 1)
        gg = gview("gg", 2)
        bb = gview("bb", 3)
        eb = self.carve("eb", [128, 16, 2, 4], F32)
        emg = self.carve("emg", [128, 16, 2, 4], F32)
        egb = self.carve("egb", [128, 16, 2, 4], F32)
        eG = self.carve("eG", [128, 16, 2, 4], F32)
        normg = self.carve("normg", [128, 256], F32)
        wgs = self.carve("wgs", [128, 8, 16], BF16)
        mark = self.arena_off
        qT = self.carve("qTh", [128, T], BF16)
        kT = self.carve("kTh", [128, T], BF16)
        kTM = self.carve("kTM", [128, 16, 128], BF16)
        vaug = self.carve("vaug", [128, 16, 264], BF16)
        so = self.carve("so", [128, 16, 256], BF16)
        h1 = self.carve("h1", [128, 16, 256], BF16)
        gp = TR(self.key("gp"), h1.ap.rearrange("p t c -> p (t c)")[:, 0:512].bitcast(F32).rearrange("p (t g) -> p t g", t=16))
        Am = [self.carve("Am", [128, 128], BF16) for _ in range(2)]
        vp = [self.carve("vp", [128, 258], BF16) for _ in range(2)]
        vpp = [self.carve("vpp", [128, 258], BF16) for _ in range(2)]
        C32 = [self.carve("C", [128, 260], F32) for _ in range(2)]
        Cb = [self.carve("Cb", [128, 264], BF16) for _ in range(3)]
        rcv = self.carve("rcvs", [128, 2, 260], F32)
        dd = [self.carve("dd", [128, 2], F32) for _ in range(4)]
        hs = [self.carve("hs", [128, 256], F32) for _ in range(2)]
        hs.append(TR(self.key("hsA"), gtmp.ap[:, 0:256]))
        hs.append(TR(self.key("hsB"), gtmp.ap[:, 256:512]))
        st4 = [self.carve("st4", [128, 8], F32) for _ in range(4)]
        st6 = [self.carve("st6", [128, 8], F32) for _ in range(4)]
        hb = [self.carve("hb", [128, 256], BF16) for _ in range(2)]
        one_col = self.eps_col(1.0)
        eps_col = self.eps_col(LN_EPS)
        KS = float(128 ** -0.5)

        if DBG not in (21, 22, 23):
            self.dma("pool", wgs, self.d_wg[j].rearrange("p (a b) -> p a b", a=DC), group="wgs")
        self.memset("pool", vaug, 1.0)
        gp4 = gp.ap.rearrange("p t (k h) -> p t k h", k=4)
        psT = TR(self.ps[7].key, self.ps[7].ap.bitcast(BF16))

        cnt = 0
        xb_sets = [xb, [TR(("win", 2), self.win_slots[2].ap[:, dc, :]) for dc in range(DC)]]
        for h in range(4):
            self.dma("sp", normg, self.d_normg[j][:, h * 256:(h + 1) * 256], group="normg")
            if h == 0:
                self.win_rr = 0
                sW = self.load_w512(g512_base + 2 * h)
                sO = self.load_w512(g512_base + 2 * h + 1, 256)
            else:
                sW, sO = sW_next, sO_next

            def cast_tt(tt):
                for dc in range(DC):
                    self.copy("act" if dc % 2 == 0 else "dve", xb_sets[tt % 2][dc], self.x[dc][tt])
            cast_tt(0)
            for tt in range(4):
                xb = xb_sets[tt % 2]
                if tt == 1:
                    self.drain()
                if tt + 1 < 4:
                    cast_tt(tt + 1)
                ps = self.ps[cnt % 4]; cnt += 1
                for dc in range(DC):
                    self.mm(ps, sW[:, dc, 0:128], xb[dc], start=(dc == 0), stop=(dc == DC - 1))
                self.copy("act", qT[:, tt * 512:(tt + 1) * 512], ps)
                ps = self.ps[cnt % 4]; cnt += 1
                for dc in range(DC):
                    self.mm(ps, sW[:, dc, 128:256], xb[dc], start=(dc == 0), stop=(dc == DC - 1))
                self.act(kT[:, tt * 512:(tt + 1) * 512], ps, AF.Copy, scale=KS)
                for tile in range(4):
                    if DBG == 22:
                        continue
                    i = tt * 4 + tile
                    ps = self.ps[cnt % 4]; cnt += 1
                    for dc in range(DC):
                        self.mm(ps[:, 0:128], xb[dc][:, tile * 128:(tile + 1) * 128], sW[:, dc, 128:256],
                                start=(dc == 0), stop=(dc == DC - 1))
                    self.act(kTM[:, i, :], ps[:, 0:128], AF.Copy, scale=KS)
                    ps = self.ps[cnt % 4]; cnt += 1
                    for dc in range(DC):
                        self.mm(ps[:, 0:256], xb[dc][:, tile * 128:(tile + 1) * 128], sW[:, dc, 256:512],
                                start=(dc == 0), stop=(dc == DC - 1))
                    self.copy("dve", vaug[:, i, 0:256], ps[:, 0:256])
                    if DBG == 23:
                        continue
                    ps = self.ps[cnt % 4]; cnt += 1
                    for dc in range(DC):
                        self.mm(ps[:, 0:256], xb[dc][:, tile * 128:(tile + 1) * 128], sO[:, dc, 0:256],
                                start=(dc == 0), stop=(dc == DC - 1))
                    self.act(so[:, i, :], ps[:, 0:256], AF.Sigmoid)
                    self.drain(4)
                    if h == 0 and DBG not in (21, 22, 23):
                        ps = self.ps[cnt % 4]; cnt += 1
                        for dc in range(DC):
                            self.mm(ps[:, 0:16], xb[dc][:, tile * 128:(tile + 1) * 128], wgs[:, dc, :],
                                    start=(dc == 0), stop=(dc == DC - 1))
                        self.tt("dve", gp[:, i, :], ps[:, 0:16], self.cst[:, C_GBIAS + j * 16:C_GBIAS + (j + 1) * 16], ALU.add)
            if h < 3:
                self.win_rr = 0
                sW_next = self.load_w512(g512_base + 2 * (h + 1))
                sO_next = self.load_w512(g512_base + 2 * (h + 1) + 1, 256)
            if h == 0:
                for dr in range(2):
                    src = TR(gp.key, gp4[:, :, 1 + 2 * dr, :])
                    self.act(etmp[:, :, dr, :], src, AF.Exp, scale=-1.0)
                    self.act(lf[:, :, dr, :], etmp[:, :, dr, :], AF.Ln, bias=one_col)
                lf2 = TR(lf.key, lf.ap.rearrange("p t d h -> p (t d h)"))
                pg = self.ps[4]
                pgF = TR(pg.key, pg.ap[:, 0:128].rearrange("p (t d h) -> p t d h", t=16, d=2))
                pgB = TR(pg.key, pg.ap[:, 128:256].rearrange("p (t d h) -> p t d h", t=16, d=2))
                self.mm(pg[:, 0:128], self.cst[:, C_MTRIF:C_MTRIF + 128], lf2)
                self.mm(pg[:, 128:256], self.cst[:, C_MTRIB:C_MTRIB + 128], lf2)
                self.mm(self.ps[5][:, 0:128], self.cst[:, C_MBLK:C_MBLK + 128], lf2)
                self.copy("dve", gg[:, :, 0, :], pgF[:, :, 0, :])
                self.copy("dve", gg[:, :, 1, :], pgB[:, :, 1, :])
                for dr in range(2):
                    self.tt("dve", bb[:, :, dr, :], TR(gp.key, gp4[:, :, 2 * dr, :]), gg[:, :, dr, :], ALU.subtract)
                self.act(eb, bb, AF.Exp)
                self.act(emg, gg, AF.Exp, scale=-1.0)
                p5 = TR(self.ps[5].key, self.ps[5].ap[:, 0:128].rearrange("p (t d h) -> p t d h", t=16, d=2))
                self.tt("dve", egb, p5, bb, ALU.add)
                self.act(egb, egb, AF.Exp)
                self.act(eG, p5, AF.Exp)
            if DBG == 12:
                return
            LAG = 2
            RING = len(Cb)
            for dirn in range(2):
                mask = self.cst[:, C_MASKF:C_MASKF + 128] if dirn == 0 else self.cst[:, C_MASKB:C_MASKB + 128]
                if dirn == 0:
                    self.memset("dve", C32[0], 0.0)
                    self.memset("dve", Cb[0], 0.0)
                order = list(range(16)) if dirn == 0 else list(range(15, -1, -1))

                psA_t = [self.ps[0][:, 0:128], self.ps[1][:, 0:128]]
                pso_t = [self.ps[2], self.ps[3], self.ps[6]]

                def tsl_(t):
                    i = order[t]
                    return i, slice(i * 128, (i + 1) * 128)

                def V0(t):
                    i, _ = tsl_(t)
                    if dirn == 0:
                        self.act(vpp[t % 2][:, 0:257], vaug[:, i, 0:257], AF.Copy, scale=egb[:, i, dirn, h:h + 1])
                    else:
                        self.ts("dve", vpp[t % 2][:, 0:257], vaug[:, i, 0:257], egb[:, i, dirn, h:h + 1], None, ALU.mult)

                def L1(t):
                    i, tsl = tsl_(t)
                    self.mm(psA_t[t % 2], kT[:, tsl], qT[:, tsl])
                    self.mm(self.ps[4 + t % 2][:, 0:257], kTM[:, i, :], vpp[t % 2][:, 0:257])

                def L2(t):
                    i, tsl = tsl_(t)
                    ps_U = self.ps[4 + t % 2]
                    self.stt("dve", C32[(t + 1) % 2][:, 0:257], C32[t % 2][:, 0:257], eG[:, i, dirn, h:h + 1],
                             ps_U[:, 0:257], ALU.mult, ALU.add)
                    self.copy("act", Cb[(t + 1) % RING][:, 0:257], C32[(t + 1) % 2][:, 0:257])
                    self.tt("dve", Am[t % 2], psA_t[t % 2], mask, ALU.mult)
                    self.act(vp[t % 2][:, 0:257], vaug[:, i, 0:257], AF.Copy, scale=eb[:, i, dirn, h:h + 1])

                def L3(t):
                    i, tsl = tsl_(t)
                    ps_o = pso_t[t % 3]
                    self.mm(ps_o[:, 0:257], Am[t % 2], vp[t % 2][:, 0:257], start=True, stop=False)
                    self.mm(ps_o[:, 0:257], qT[:, tsl], Cb[t % RING][:, 0:257], start=False, stop=True)
                    self.act(dd[t % 4][:, 0:1], ps_o[:, 256:257], AF.Abs)

                def L4(t):
                    i, tsl = tsl_(t)
                    ps_o = pso_t[t % 3]
                    d_ = dd[t % 4]
                    self.ts("dve", d_[:, 0:1], d_[:, 0:1], emg[:, i, dirn, h:h + 1], None, ALU.max)
                    P.add("dve", lambda e, o=d_: e.reciprocal(o.ap[:, 1:2], o.ap[:, 0:1]), reads=[d_], writes=[d_])
                    if dirn == 0:
                        self.ts("dve", h1[:, i, :], ps_o[:, 0:256], d_[:, 1:2], None, ALU.mult)
                    else:
                        hs_ = hs[t % 4]
                        s6 = st6[t % 4]
                        self.stt("dve", hs_, ps_o[:, 0:256], d_[:, 1:2], h1[:, i, :], ALU.mult, ALU.add)
                        P.add("dve", lambda e, o=s6, a=hs_: e.bn_stats(o.ap[:, 0:6], a.ap), reads=[hs_], writes=[s6])

                def L5(t):
                    s6 = st6[t % 4]
                    s4 = st4[t % 4]
                    P.add("dve", lambda e, o=s4, a=s6: e.bn_aggr(o.ap[:, 0:2], a.ap[:, 0:6]), reads=[s6], writes=[s4])
                    self.act(s4[:, 2:3], s4[:, 1:2], AF.Sqrt, bias=eps_col)

                def L6(t):
                    hs_ = hs[t % 4]
                    s4 = st4[t % 4]
                    P.add("dve", lambda e, o=s4: e.reciprocal(o.ap[:, 3:4], o.ap[:, 2:3]), reads=[s4], writes=[s4])
                    self.stt("dve", s4[:, 4:5], s4[:, 0:1], -1.0, s4[:, 3:4], ALU.mult, ALU.mult)
                    self.act(hs_, hs_, AF.Identity, bias=s4[:, 4:5], scale=s4[:, 3:4])

                def L7(t):
                    hs_ = hs[t % 4]
                    self.tt("pool", hs_, hs_, normg, ALU.mult)

                def L7b(t):
                    i, _ = tsl_(t)
                    self.tt("pool", hb[t % 2], hs[t % 4], so[:, i, :], ALU.mult)

                def L8(t):
                    i, _ = tsl_(t)
                    hb_ = hb[t % 2]
                    for k2 in range(2):
                        self.transpose(psT[:, k2 * 128:(k2 + 1) * 128], hb_[:, k2 * 128:(k2 + 1) * 128], self.ident_bf)
                    for k2 in range(2):
                        dst = TR(hk[2 * h + k2][i // 4], hT_all.ap[:, 2 * h + k2, i * 128:(i + 1) * 128])
                        self.copy("act", dst, psT[:, k2 * 128:(k2 + 1) * 128])

                stages_ = [(0, V0), (1, L1), (2, L2), (3, L3), (4, L4)]
                if dirn == 1:
                    stages_ += [(5, L5), (6, L6), (7, L7), (8, L7b), (9, L8)]
                maxlag = max(l_ for l_, _ in stages_)
                for it in range(16 + maxlag):
                    for l_, fn_ in sorted(stages_, key=lambda x: -x[0]):
                        t = it - l_
                        if 0 <= t < 16:
                            fn_(t)
                if DBG == 13 or (DBG == 15 and dirn == 1):
                    return
                if dirn == 0:
                    self.exchange(self.d_bounce_s, self.d_gath_s, ("s", j, h), [(C32[0], 0, 260)], rcv)
                    self.select_partner(C32[0], rcv, 0, 260)
                    self.copy("act", Cb[0][:, 0:257], C32[0][:, 0:257])
                    if DBG == 14:
                        return
        P.fence()
        self.arena_off = mark
        scratch = self.ln_scratch()
        ml_pend = []
        for st in range(2):
            tts = [2 * st, 2 * st + 1]
            ml_pend = self.outproj_ln(tts, lambda i: [TR(hk[k][tts[i]], hT_all.ap[:, k, tts[i] * 512:(tts[i] + 1) * 512]) for k in range(8)],
                                      lambda c: gwo_base + c, 128, 8 * 128,
                                      lambda slot, k: slot[:, k * 128:(k + 1) * 128],
                                      ALPHA, ln_idx, LN_EPS, scratch, defer=("all" if st == 0 else "last"), drain=ml_pend)
        self.pend = ml_pend


def _lay_w512(w, cols):
    out = np.zeros((128, DC, 512), np.float32)
    out[:, :, :len(cols)] = w[:, cols].reshape(DC, 128, len(cols)).transpose(1, 0, 2)
    return out.reshape(128, DC * 512)


def _lay_wo(w, c, kp, nk):
    out = np.zeros((128, FC * 128), np.float32)
    blk = w[:, c * 128:(c + 1) * 128].reshape(nk, kp, 128).transpose(1, 0, 2).reshape(kp, nk * 128)
    out[:kp, :nk * 128] = blk
    return out


def _prepare(inputs, stages_spec):
    x = np.asarray(inputs["x"], np.float32)
    ffn_w_in = np.asarray(inputs["ffn_w_in"], np.float32)
    ffn_w_out = np.asarray(inputs["ffn_w_out"], np.float32)
    w512 = []
    wo = []
    stages = []
    for spec in stages_spec:
        if spec[0] == "ffn":
            l, i = spec[1], spec[2]
            base512 = len(w512)
            wi = ffn_w_in[l, i]
            for jg in range(11):
                cols = list(range(jg * 256, (jg + 1) * 256)) + list(range(DFF + jg * 256, DFF + (jg + 1) * 256))
                w512.append(_lay_w512(wi, cols))
            basewo = len(wo)
            for c in range(DC):
                wo.append(_lay_wo(ffn_w_out[l, i], c, 128, FC))
            stages.append(("ffn", base512, basewo, l * 3 + (0 if i == 0 else 2)))
        elif spec[0] == "attn":
            j = spec[1]
            w = np.asarray(inputs["at_w_in"], np.float32)[j]
            base512 = len(w512)
            w512.append(_lay_w512(w, list(range(0, 512))))
            w512.append(_lay_w512(w, list(range(512, 1024))))
            kc = []
            for k in range(4):
                kc += list(range(1024 + k * 64, 1024 + (k + 1) * 64)) * 2
            w512.append(_lay_w512(w, kc))
            w512.append(_lay_w512(w, list(range(1280, 1536))))
            basewo = len(wo)
            wout = np.asarray(inputs["at_w_out"], np.float32)[j]
            for c in range(DC):
                wo.append(_lay_wo(wout, c, 64, 16))
            stages.append(("attn", base512, basewo, (2 * j + 1) * 3 + 1, j))
        elif spec[0] == "mlstm":
            j = spec[1]
            w = np.asarray(inputs["ml_w_in"], np.float32)[j]
            base512 = len(w512)
            for h in range(4):
                cols = (list(range(h * 128, (h + 1) * 128)) + list(range(512 + h * 128, 512 + (h + 1) * 128))
                        + list(range(1024 + h * 256, 1024 + (h + 1) * 256)))
                w512.append(_lay_w512(w, cols))
                w512.append(_lay_w512(w, list(range(2048 + h * 256, 2048 + (h + 1) * 256))))
            basewo = len(wo)
            wout = np.asarray(inputs["ml_w_out"], np.float32)[j]
            for c in range(DC):
                wo.append(_lay_wo(wout, c, 128, 8))
            stages.append(("mlstm", base512, basewo, (2 * j) * 3 + 1, j))
        else:
            raise NotImplementedError(spec)
    w512 = np.stack(w512) if w512 else np.zeros((1, 128, 4096), np.float32)
    wo = np.stack(wo) if wo else np.zeros((1, 128, FC * 128), np.float32)
    ml_w_in = np.asarray(inputs["ml_w_in"], np.float32)
    wg_even = np.zeros((2, 128, DC * 16), np.float32)
    wg_odd = np.zeros((2, 128, DC * 16), np.float32)
    for j in range(2):
        g = ml_w_in[j][:, 3072:3088]
        wg_even[j] = g.reshape(DC, 128, 16).transpose(1, 0, 2).reshape(128, DC * 16)
        g2 = g.reshape(1024, 4, 4)[:, [2, 3, 0, 1], :].reshape(1024, 16)
        wg_odd[j] = g2.reshape(DC, 128, 16).transpose(1, 0, 2).reshape(128, DC * 16)
    normg = np.ascontiguousarray(np.broadcast_to(np.asarray(inputs["ml_norm_g"], np.float32)[:, None, :], (2, 128, 1024)))
    in_maps = []
    for core in range(NCORES):
        b, half = core // 2, core % 2
        xs = x[b, half * T:(half + 1) * T]
        if half:
            xs = xs[::-1]
        xT = np.ascontiguousarray(xs.reshape(T, DC, 128).transpose(2, 1, 0)).reshape(128, DC * T)
        consts = _const_table(core, np.asarray(inputs["ln_g"], np.float32), np.asarray(inputs["ln_b"], np.float32),
                              np.asarray(inputs["ml_gate_b"], np.float32), np.asarray(inputs["at_sink"], np.float32))
        in_maps.append({"xT": xT, "w512": w512, "wo": wo, "wg": wg_odd if half else wg_even, "normg": normg,
                        "consts": consts})
    return stages, in_maps, w512.shape[0], wo.shape[0]


def _assemble(results):
    out = np.zeros((4, 2 * T, D), np.float32)
    for core in range(NCORES):
        b, half = core // 2, core % 2
        o = np.asarray(results[core]["outT"]).reshape(128, DC, T).transpose(2, 1, 0).reshape(T, D)
        if half:
            o = o[::-1]
        out[b, half * T:(half + 1) * T] = o
    return out


FULL_SPEC = []
for _l in range(DEPTH):
    FULL_SPEC.append(("ffn", _l, 0))
    FULL_SPEC.append(("mlstm", _l // 2) if _l % 2 == 0 else ("attn", _l // 2))
    FULL_SPEC.append(("ffn", _l, 1))


def run_spec(inputs, spec, trace=False):
    stages, in_maps, n512, nwo = _prepare(inputs, spec)
    b = Builder(stages, n512, nwo)
    nc = b.build()
    res = run_bass_kernel_spmd(nc, in_maps, core_ids=list(range(NCORES)))
    return _assemble(res.results)


def kernel(**inputs):
    return run_spec(inputs, FULL_SPEC)
```

```python
import numpy as np
from contextlib import ExitStack
import concourse.bass as bass
import concourse.mybir as mybir
from concourse.bass_utils import run_bass_kernel_spmd

F32 = mybir.dt.float32
BF16 = mybir.dt.bfloat16
AF = mybir.ActivationFunctionType
ALU = mybir.AluOpType

D = 1024
DC = 8
T = 2048
NT = 16
DFF = 2816
FC = 22
DEPTH = 4
ALPHA = float((2 * DEPTH) ** 0.25)
LN_EPS = 1e-5
NCORES = 8
import os
DBG = int(os.environ.get("KDBG", "0"))


class TR:
    __slots__ = ("key", "ap")

    def __init__(self, key, ap):
        self.key = key
        self.ap = ap

    def __getitem__(self, idx):
        return TR(self.key, self.ap[idx])


class Op:
    __slots__ = ("eng", "fn", "dma", "deps", "needs_inc", "tick", "inc")

    def __init__(self, eng, fn, dma, inc=16):
        self.eng = eng
        self.fn = fn
        self.dma = dma
        self.inc = inc
        self.deps = ()
        self.needs_inc = False
        self.tick = 0


ENGS = ("pe", "act", "dve", "pool", "sp")


class Prog:
    def __init__(self):
        self.ops = []
        self.last_w = {}
        self.readers = {}
        self.last_on = {}

    def add(self, eng, fn, reads=(), writes=(), dma=None, extra_deps=(), inc=16):
        idx = len(self.ops)
        op = Op(eng, fn, dma, inc)
        deps = set(extra_deps)
        rk = [r.key if isinstance(r, TR) else r for r in reads]
        wk = [w.key if isinstance(w, TR) else w for w in writes]
        for k in rk:
            if k in self.last_w:
                deps.add(self.last_w[k])
        for k in wk:
            if k in self.last_w:
                deps.add(self.last_w[k])
            deps.update(self.readers.get(k, ()))
        op.deps = deps
        for k in rk:
            self.readers.setdefault(k, []).append(idx)
        for k in wk:
            self.last_w[k] = idx
            self.readers[k] = []
        self.ops.append(op)
        if dma is None:
            self.last_on[eng] = idx
        return idx

    def fence(self):
        lasts = list(self.last_on.values())
        for e in ("pe", "act", "dve", "pool", "sp"):
            self.add(e, None, extra_deps=[i for i in lasts])

    def emit(self, nc, block, engsem, dmasems):
        ops = self.ops
        for op in ops:
            for d in op.deps:
                dop = ops[d]
                if dop.dma is None:
                    if dop.eng == "pe" and op.eng == "pe" and op.dma is None:
                        continue
                    dop.needs_inc = True
        cnt = {e: 0 for e in ENGS}
        dcnt = {}
        for op in ops:
            if op.dma is not None:
                dcnt[op.dma] = dcnt.get(op.dma, 0) + op.inc
                op.tick = dcnt[op.dma]
            elif op.needs_inc:
                cnt[op.eng] += 1
                op.tick = cnt[op.eng]
        groups = sorted(dcnt.keys(), key=str)
        assert len(groups) <= len(dmasems), (len(groups), len(dmasems))
        gsem = {g: dmasems[i] for i, g in enumerate(groups)}
        per_eng = {e: [] for e in ENGS}
        for op in ops:
            per_eng[op.eng].append(op)

        def run(engname, e):
            waited = {}
            for op in per_eng[engname]:
                need = {}
                for d in op.deps:
                    dop = ops[d]
                    if dop.dma is not None:
                        s = ("d", dop.dma)
                    else:
                        if dop.eng == "pe" and op.eng == "pe" and op.dma is None:
                            continue
                        s = ("e", dop.eng)
                    if dop.tick > need.get(s, 0):
                        need[s] = dop.tick
                for s, v in need.items():
                    if waited.get(s, 0) >= v:
                        continue
                    waited[s] = v
                    sem = gsem[s[1]] if s[0] == "d" else engsem[s[1]]
                    e.wait_ge(sem, v)
                if op.fn is None:
                    if op.needs_inc:
                        e.sem_inc(engsem[engname], 1)
                    continue
                inst = op.fn(e)
                if op.dma is not None:
                    if op.inc == 16:
                        inst.then_inc(gsem[op.dma], 16)
                    else:
                        inst.then_inc(gsem[op.dma])
                elif op.needs_inc:
                    inst.then_inc(engsem[engname], 1)

        @block.tensor
        def _(e):
            run("pe", e)

        @block.scalar
        def _(e):
            run("act", e)

        @block.vector
        def _(e):
            run("dve", e)

        @block.gpsimd
        def _(e):
            run("pool", e)

        @block.sync
        def _(e):
            run("sp", e)


C_LNG = 0
C_LNB = 96
C_DIST = 192
C_MASKF = 704
C_MASKB = 832
C_MTRIF = 960
C_MTRIB = 1088
C_MBLK = 1216
C_MIND0 = 1344
C_MIND1 = 1472
C_IDENT = 1600
C_ONES = 1728
C_SEL = 1856
C_GBIAS = 1858
C_SINK = 1890
C_SLOPE = 1922
NCONST = 1938

BIGDIST = 1.0e6


def _const_table(core, ln_g, ln_b, ml_gate_b, at_sink):
    c = np.zeros((128, NCONST), np.float32)
    odd = core % 2
    p = np.arange(128)
    for l in range(4):
        for i in range(3):
            for dc in range(8):
                c[:, C_LNG + (l * 3 + i) * 8 + dc] = ln_g[l, i, dc * 128:(dc + 1) * 128]
                c[:, C_LNB + (l * 3 + i) * 8 + dc] = ln_b[l, i, dc * 128:(dc + 1) * 128]
    s = p[:, None].astype(np.float64)
    q = p[None, :].astype(np.float64)
    d = q - s + 128
    c[:, C_DIST + 0:C_DIST + 128] = np.where(d <= 128, d, BIGDIST)
    d = np.abs(q - s)
    c[:, C_DIST + 128:C_DIST + 256] = d
    d = s + 128 - q
    c[:, C_DIST + 256:C_DIST + 384] = np.where(d <= 128, d, BIGDIST)
    d = 255 - s - q
    c[:, C_DIST + 384:C_DIST + 512] = np.where(d <= 128, d, BIGDIST)
    same = np.ones((128, 128), bool)
    maskF = (same & (p[:, None] <= p[None, :])).astype(np.float32)
    maskB = (same & (p[:, None] >= p[None, :])).astype(np.float32)
    c[:, C_MASKF:C_MASKF + 128] = maskF
    c[:, C_MASKB:C_MASKB + 128] = maskB
    c[:, C_MTRIF:C_MTRIF + 128] = -maskF
    c[:, C_MTRIB:C_MTRIB + 128] = -maskB
    c[:, C_MBLK:C_MBLK + 128] = -same.astype(np.float32)
    c[:, C_MIND0:C_MIND0 + 128] = -(p[:, None] < 64).astype(np.float32) * np.ones((1, 128), np.float32)
    c[:, C_MIND1:C_MIND1 + 128] = -(p[:, None] >= 64).astype(np.float32) * np.ones((1, 128), np.float32)
    c[:, C_IDENT:C_IDENT + 128] = np.eye(128, dtype=np.float32)
    c[:, C_ONES:C_ONES + 128] = 1.0
    c[:, C_SEL + (1 - odd)] = 1.0
    for j in range(2):
        gb = ml_gate_b[j].reshape(4, 4)
        if odd:
            gb = gb[[2, 3, 0, 1]]
        c[:, C_GBIAS + j * 16:C_GBIAS + (j + 1) * 16] = gb.reshape(1, 16)
        c[:, C_SINK + j * 16:C_SINK + (j + 1) * 16] = at_sink[j].reshape(1, 16)
    slopes = np.exp2(-8.0 * np.arange(1, 17, dtype=np.float32) / 16).astype(np.float32)
    c[:, C_SLOPE:C_SLOPE + 16] = (-8.0 * slopes).reshape(1, 16)
    return c


def alibi_slopes():
    return [float(np.exp2(np.float32(-8.0) * np.float32(h + 1) / np.float32(16))) for h in range(16)]


class Builder:
    def __init__(self, stages, n_w512, n_wo):
        self.stages = stages
        nc = bass.Bass("TRN2", target_bir_lowering=False)
        self.nc = nc
        self.P = Prog()
        self.d_x = nc.dram_tensor("xT", [128, DC * T], F32, kind="ExternalInput").ap()
        self.d_out = nc.dram_tensor("outT", [128, DC * T], F32, kind="ExternalOutput").ap()
        self.d_w512 = nc.dram_tensor("w512", [n_w512, 128, 4096], F32, kind="ExternalInput").ap()
        self.d_wo = nc.dram_tensor("wo", [n_wo, 128, FC * 128], F32, kind="ExternalInput").ap()
        self.d_wg = nc.dram_tensor("wg", [2, 128, DC * 16], F32, kind="ExternalInput").ap()
        self.d_normg = nc.dram_tensor("normg", [2, 128, 1024], F32, kind="ExternalInput").ap()
        self.d_const = nc.dram_tensor("consts", [128, NCONST], F32, kind="ExternalInput").ap()
        self.d_bounce_s = nc.dram_tensor("bounce_s", [128, 260], F32).ap()
        self.d_gath_s = nc.dram_tensor("gath_s", [256, 260], F32).ap()
        self.d_bounce_h = nc.dram_tensor("bounce_h", [128, 768], BF16).ap()
        self.d_gath_h = nc.dram_tensor("gath_h", [256, 768], BF16).ap()
        self.uid = 0

    def key(self, name):
        self.uid += 1
        return (name, self.uid)

    def carve_reset(self):
        self.arena_off = 0

    def carve(self, name, shape, dtype, key=None):
        n = 1
        for s in shape[1:]:
            n *= s
        nbytes = n * (2 if dtype == BF16 else 4)
        nbytes = (nbytes + 31) // 32 * 32
        off = self.arena_off
        assert off + nbytes <= self.ARENA_BYTES, (name, off, nbytes, self.ARENA_BYTES)
        self.arena_off += nbytes
        ap = self.arena[:, off // 4:(off + nbytes) // 4]
        if dtype == BF16:
            ap = ap.bitcast(BF16)[:, 0:n]
        else:
            ap = ap[:, 0:n]
        if len(shape) == 3:
            ap = ap.rearrange("p (a b) -> p a b", a=shape[1])
        elif len(shape) == 4:
            ap = ap.rearrange("p (a b c) -> p a b c", a=shape[1], b=shape[2])
        if shape[0] != 128:
            ap = ap[0:shape[0]]
        return TR(key if key is not None else self.key(name), ap)

    def eps_col(self, eps):
        if eps not in self.eps_cols:
            i = len(self.eps_cols)
            tr = TR(("epsc", i), self.epsbuf[:, i:i + 1])
            self.memset("pool", tr, float(eps))
            self.eps_cols[eps] = tr
        return self.eps_cols[eps]

    def mm(self, out, lhsT, rhs, start=True, stop=True):
        rd = [lhsT, rhs] + ([] if start else [out])
        self.P.add("pe", lambda e: e.matmul(out.ap, lhsT.ap, rhs.ap, start=start, stop=stop),
                   reads=rd, writes=[out])

    def transpose(self, out, in_, ident):
        self.P.add("pe", lambda e: e.transpose(out.ap, in_.ap, ident.ap), reads=[in_, ident], writes=[out])

    def act(self, out, in_, func, bias=None, scale=None, extra_reads=(), eng="act"):
        kw = {}
        rd = [in_] + list(extra_reads)
        if bias is not None:
            kw["bias"] = bias.ap if isinstance(bias, TR) else bias
            if isinstance(bias, TR):
                rd.append(bias)
        if scale is not None:
            kw["scale"] = scale.ap if isinstance(scale, TR) else scale
            if isinstance(scale, TR):
                rd.append(scale)
        self.P.add(eng, lambda e: e.activation(out.ap, in_.ap, func, **kw), reads=rd, writes=[out])

    def tt(self, eng, out, in0, in1, op):
        self.P.add(eng, lambda e: e.tensor_tensor(out.ap, in0.ap, in1.ap, op), reads=[in0, in1], writes=[out])

    def ts(self, eng, out, in0, s1, s2, op0, op1=None):
        rd = [in0]
        a1 = s1.ap if isinstance(s1, TR) else s1
        a2 = s2.ap if isinstance(s2, TR) else s2
        if isinstance(s1, TR):
            rd.append(s1)
        if isinstance(s2, TR):
            rd.append(s2)
        if op1 is None:
            self.P.add(eng, lambda e: e.tensor_scalar(out.ap, in0.ap, a1, None, op0), reads=rd, writes=[out])
        else:
            self.P.add(eng, lambda e: e.tensor_scalar(out.ap, in0.ap, a1, a2, op0, op1), reads=rd, writes=[out])

    def stt(self, eng, out, in0, scalar, in1, op0, op1):
        rd = [in0, in1]
        a = scalar.ap if isinstance(scalar, TR) else scalar
        if isinstance(scalar, TR):
            rd.append(scalar)
        self.P.add(eng, lambda e: e.scalar_tensor_tensor(out.ap, in0.ap, a, in1.ap, op0, op1),
                   reads=rd, writes=[out])

    def copy(self, eng, out, in_):
        if eng == "act":
            self.P.add("act", lambda e: e.activation(out.ap, in_.ap, AF.Copy), reads=[in_], writes=[out])
        else:
            self.P.add(eng, lambda e: e.tensor_copy(out.ap, in_.ap), reads=[in_], writes=[out])

    def memset(self, eng, out, val):
        self.P.add(eng, lambda e: e.memset(out.ap, val), reads=[], writes=[out])

    def dma(self, eng, out, in_, group, reads=(), writes=None):
        o = out.ap if isinstance(out, TR) else out
        i = in_.ap if isinstance(in_, TR) else in_
        rd = list(reads) + ([in_] if isinstance(in_, TR) else [])
        wr = ([out] if isinstance(out, TR) else []) if writes is None else writes
        self.P.add(eng, lambda e: e.dma_start(out=o, in_=i), reads=rd, writes=wr, dma=group)

    def load_w512(self, g, ncols=512):
        s = self.win_rr % len(self.win_slots)
        self.win_rr += 1
        slot = self.win_slots[s]
        if ncols == 512:
            self.dma("pool", slot, self.d_w512[g].rearrange("p (a b) -> p a b", a=DC), group=("win", s))
        else:
            self.dma("pool", slot[:, :, 0:ncols],
                     self.d_w512[g].rearrange("p (a b) -> p a b", a=DC)[:, :, 0:ncols], group=("win", s))
        return slot

    def load_wo(self, g, nparts=128, ncols=FC * 128):
        s = self.wo_rr % len(self.wo_slots)
        self.wo_rr += 1
        slot = self.wo_slots[s]
        self.dma("pool", slot[0:nparts, 0:ncols], self.d_wo[g][0:nparts, 0:ncols], group=("wo", s))
        return slot

    def build(self):
        nc = self.nc
        P = self.P
        with ExitStack() as es:
            def sb(name, shape, dt):
                return es.enter_context(nc.sbuf_tensor(name, shape, dt))

            xs = sb("xres", [128, DC, T], F32)
            cst = sb("cst", [128, NCONST], F32)
            cbf = sb("cbf", [128, 256], BF16)
            self.epsbuf = sb("epsbuf", [128, 8], F32)
            self.eps_cols = {}
            win = [sb("win%d" % i, [128, DC, 512], BF16) for i in range(3)]
            wo = [sb("wo%d" % i, [128, FC * 128], BF16) for i in range(2)]
            self.ARENA_BYTES = 100 * 1024 - 6144
            lnstat = sb("lnstat", [128, 3, 512], F32)
            self.ln_mean = TR("ln_mean", lnstat[:, 0, :])
            self.ln_msq = TR("ln_msq", lnstat[:, 1, :])
            self.ln_rstd = TR("ln_rstd", lnstat[:, 2, :])
            self.pend = []
            arena_t = sb("arena", [128, self.ARENA_BYTES // 4], F32)
            self.arena = arena_t[:]
            psum = [es.enter_context(nc.psum_tensor("ps%d" % i, [128, 512], F32)) for i in range(8)]
            engsem = {e: es.enter_context(nc.semaphore("sem_" + e)) for e in ENGS}
            dmasems = [es.enter_context(nc.semaphore("dsem%d" % i)) for i in range(60)]

            self.x = [[TR(("x", dc, tt), xs[:, dc, tt * 512:(tt + 1) * 512]) for tt in range(4)] for dc in range(DC)]
            self.cst = TR("cst", cst[:])
            self.ident_bf = TR("cbf", cbf[:, 0:128])
            self.ones_bf = TR("cbf", cbf[:, 128:256])
            self.win_slots = [TR(("win", i), win[i][:]) for i in range(3)]
            self.wo_slots = [TR(("wo", i), wo[i][:]) for i in range(2)]
            self.win_rr = 0
            self.wo_rr = 0
            self.ps = [TR(("ps", i), psum[i][:]) for i in range(8)]

            self.dma("sp", self.cst, self.d_const, group="cst")
            dx3 = self.d_x.rearrange("p (a b) -> p a b", a=DC)
            prev = []
            for tt in range(4):
                cur_ = []
                for hf in range(2):
                    o_ = xs[:, hf * 4:(hf + 1) * 4, tt * 512:(tt + 1) * 512]
                    i_ = dx3[:, hf * 4:(hf + 1) * 4, tt * 512:(tt + 1) * 512]
                    cur_.append(P.add("sp", lambda e, o_=o_, i_=i_: e.dma_start(out=o_, in_=i_),
                                      writes=[self.x[dc][tt] for dc in range(hf * 4, hf * 4 + 4)],
                                      dma=("xin", tt, hf), extra_deps=prev))
                prev = cur_
            self.copy("dve", TR("cbf", cbf[:, 0:128]), self.cst[:, C_IDENT:C_IDENT + 128])
            self.copy("dve", TR("cbf", cbf[:, 128:256]), self.cst[:, C_ONES:C_ONES + 128])

            for si_, st in enumerate(self.stages):
                if si_ > 0:
                    P.fence()
                self.carve_reset()
                if st[0] == "ffn":
                    self.stage_ffn(st[1], st[2], st[3])
                elif st[0] == "attn":
                    self.stage_attn(st[1], st[2], st[3], st[4])
                elif st[0] == "mlstm":
                    self.stage_mlstm(st[1], st[2], st[3], st[4])

            self.drain()
            outs = []
            do3 = self.d_out.rearrange("p (a b) -> p a b", a=DC)
            for tt in range(4):
                for hf in range(2):
                    k = self.key("outdma")
                    outs.append(k)
                    self.dma("sp", do3[:, hf * 4:(hf + 1) * 4, tt * 512:(tt + 1) * 512],
                             TR(("xo", tt, hf), xs[:, hf * 4:(hf + 1) * 4, tt * 512:(tt + 1) * 512]),
                             group=("xout", tt, hf), reads=[self.x[dc][tt] for dc in range(hf * 4, hf * 4 + 4)], writes=[k])
            P.add("sp", None, reads=outs)

            with nc.Block() as block:
                P.emit(nc, block, engsem, dmasems)
        return nc

    def cast_xb(self, xb, tts):
        n = 0
        for i, tt in enumerate(tts):
            for dc in range(DC):
                eng = "act" if n % 2 == 0 else "dve"
                self.copy(eng, xb[dc][i], self.x[dc][tt])
                n += 1

    def outproj_ln(self, tts, kchunks, wo_group, wo_parts, wo_ncols, lhsT_of, resid_scale, ln_idx, eps,
                   scratch, defer=None, drain=None):
        zb, zsq, meanb, msq, rstd = scratch
        nk = len(kchunks(0))
        ps_y = [self.ps[0], self.ps[1]]
        ps_sum = [self.ps[4], self.ps[6]]
        ps_sq = [self.ps[5], self.ps[7]]
        pending = []

        def flush():
            for (i, c) in pending:
                self.mm(ps_sum[i], self.ones_bf, zb[(c * 2 + i) % len(zb)], start=(c == 0), stop=(c == DC - 1))
                self.mm(ps_sq[i], self.ones_bf, zsq[(c * 2 + i) % len(zsq)], start=(c == 0), stop=(c == DC - 1))
            pending.clear()

        n = 0
        for c in range(DC):
            if c == DC - 1 and drain:
                while drain:
                    drain.pop(0)()
            slot = self.load_wo(wo_group(c), wo_parts, wo_ncols)
            for i, tt in enumerate(tts):
                py = ps_y[n % 2]
                n += 1
                ks = kchunks(i)
                for k in range(nk):
                    self.mm(py, lhsT_of(slot, k), ks[k], start=(k == 0), stop=(k == nk - 1))
                flush()
                for _ in range(4):
                    if drain:
                        drain.pop(0)()
                xc = self.x[c][tt]
                self.stt("dve", xc, xc, resid_scale, py, ALU.mult, ALU.add)
                self.copy("act", zb[(c * 2 + i) % len(zb)], xc)
                self.act(zsq[(c * 2 + i) % len(zsq)], xc, AF.Square)
                pending.append((i, c))
        flush()
        g0 = C_LNG + ln_idx * 8
        b0 = C_LNB + ln_idx * 8
        th = []
        for i, tt in enumerate(tts):
            th.append((i, lambda i=i: self.act(meanb[i], ps_sum[i], AF.Copy, scale=1.0 / D)))
            th.append((i, lambda i=i: self.tt("dve", msq[i], meanb[i], meanb[i], ALU.mult)))
            th.append((i, lambda i=i: self.stt("dve", rstd[i], ps_sq[i], 1.0 / D, msq[i], ALU.mult, ALU.subtract)))
            th.append((i, lambda i=i: self.act(rstd[i], rstd[i], AF.Sqrt, bias=self.eps_col(eps))))
            th.append((i, lambda i=i: self.P.add("dve", lambda e, o=rstd[i]: e.reciprocal(o.ap, o.ap), reads=[rstd[i]], writes=[rstd[i]])))
        for i, tt in enumerate(tts):
            for c in range(DC):
                th.append((i, lambda i=i, tt=tt, c=c: self.tt("dve", self.x[c][tt], self.x[c][tt], meanb[i], ALU.subtract)))
            for c in range(DC):
                th.append((i, lambda i=i, tt=tt, c=c: self.tt("dve", self.x[c][tt], self.x[c][tt], rstd[i], ALU.mult)))
            for c in range(DC):
                th.append((i, lambda tt=tt, c=c: self.act(self.x[c][tt], self.x[c][tt], AF.Identity,
                                                           bias=self.cst[:, b0 + c:b0 + c + 1], scale=self.cst[:, g0 + c:g0 + c + 1])))
        if defer == "all":
            return [f_ for _, f_ in th]
        keep = []
        for i, f_ in th:
            if defer == "last" and i == len(tts) - 1:
                keep.append(f_)
            else:
                f_()
        return keep

    def ln_scratch(self, ntt=2, nz=4):
        zb = [self.carve("zb", [128, 512], BF16) for _ in range(nz)]
        zsq = [self.carve("zsq", [128, 512], BF16) for _ in range(nz)]
        if ntt == 1:
            return (zb, zsq, [self.ln_mean], [self.ln_msq], [self.ln_rstd])
        lm = self.carve("lmean", [128, 512], F32)
        lq = self.carve("lmsq", [128, 512], F32)
        lr = self.carve("lrstd", [128, 512], F32)
        return (zb, zsq, [lm, self.ln_mean], [lq, self.ln_msq], [lr, self.ln_rstd])

    def drain(self, n=None):
        while self.pend and (n is None or n > 0):
            self.pend.pop(0)()
            if n is not None:
                n -= 1

    def stage_ffn(self, g512_base, gwo_base, ln_idx):
        xb = [[self.carve("xb", [128, 512], BF16) for _ in range(2)] for _ in range(DC)]
        h = [[self.carve("h", [128, 512], BF16) for _ in range(2)] for _ in range(FC)]
        sg = [self.carve("sg", [128, 512], F32) for _ in range(2)]
        scratch = self.ln_scratch()
        self.eps_col(4.0 * LN_EPS)
        self.cast_xb(xb, [0, 1])
        n = 0
        for st in range(2):
            tts = [2 * st, 2 * st + 1]
            for jg in range(11):
                slot = self.load_w512(g512_base + jg)
                for i in range(2):
                    for half in range(2):
                        j = 2 * jg + half
                        pg = self.ps[(n % 2) * 2]
                        pu = self.ps[(n % 2) * 2 + 1]
                        n += 1
                        for dc in range(DC):
                            self.mm(pg, slot[:, dc, half * 128:(half + 1) * 128], xb[dc][i],
                                    start=(dc == 0), stop=(dc == DC - 1))
                        for dc in range(DC):
                            self.mm(pu, slot[:, dc, 256 + half * 128:256 + (half + 1) * 128], xb[dc][i],
                                    start=(dc == 0), stop=(dc == DC - 1))
                        s_ = sg[n % 2]
                        self.act(s_, pg, AF.Silu)
                        self.tt("dve", h[j][i], s_, pu, ALU.mult)
                        self.drain(2)
            self.drain()
            if st == 0:
                self.cast_xb(xb, [2, 3])
            self.pend = self.outproj_ln(tts, lambda i: [h[k][i] for k in range(FC)],
                                        lambda c: gwo_base + c, 128, FC * 128,
                                        lambda slot, k: slot[:, k * 128:(k + 1) * 128],
                                        2.0 * ALPHA, ln_idx, 4.0 * LN_EPS, scratch,
                                        defer=("all" if st == 0 else "last"))

    def exchange(self, bounce, gath, tag, srcs, rcv):
        kb = ("dram", tag[0], "b")
        kg = ("dram", tag[0], "g")
        for (tr, c0, ncol) in srcs:
            self.dma("sp", bounce[:, c0:c0 + ncol], tr, group=("bnc", tag), writes=[kb])
        self.P.add("pool", lambda e: e.collective_compute(
            "AllGather", ALU.bypass, replica_groups=[[0, 1], [2, 3], [4, 5], [6, 7]],
            ins=[bounce.opt()], outs=[gath.opt()]), reads=[kb], writes=[kg], dma=("cc", tag), inc=1)
        self.dma("sp", rcv, gath.rearrange("(r p) n -> p r n", p=128), group=("rcv", tag), reads=[kg])

    def select_partner(self, dst, rcv, c0, ncol):
        sel0 = self.cst[:, C_SEL:C_SEL + 1]
        sel1 = self.cst[:, C_SEL + 1:C_SEL + 2]
        self.ts("dve", dst, rcv[:, 0, c0:c0 + ncol], sel0, None, ALU.mult)
        self.stt("dve", dst, rcv[:, 1, c0:c0 + ncol], sel1, dst, ALU.mult, ALU.add)

    def stage_attn(self, g512_base, gwo_base, ln_idx, j):
        P = self.P
        qT_all = self.carve("qT", [128, 8, T], BF16)
        qk = [[self.key("qT") for _ in range(4)] for _ in range(8)]
        kT_all = self.carve("kT", [128, 4, T + 128], BF16)
        kk = [[self.key("kT") for _ in range(5)] for _ in range(4)]
        V_all = self.carve("V", [128, 17, 256], BF16)
        Vb = [TR(self.key("V"), V_all.ap[:, b, :]) for b in range(17)]
        esink = self.carve("esink", [128, 16], F32)
        mark = self.arena_off
        xb = [[self.carve("xb", [128, 512], BF16) for _ in range(2)] for _ in range(DC)]
        rcv = self.carve("rcvh", [128, 2, 768], BF16)

        self.act(esink, self.cst[:, C_SINK + j * 16:C_SINK + (j + 1) * 16], AF.Exp)
        n = 0
        for st in range(2):
            tts = [2 * st, 2 * st + 1]
            if st == 1:
                self.drain()
            self.cast_xb(xb, tts)
            sA = self.load_w512(g512_base + 0)
            sB = self.load_w512(g512_base + 1)
            sC = self.load_w512(g512_base + 2)
            for c in range(8):
                slot = sA if c < 4 else sB
                for i, tt in enumerate(tts):
                    ps = self.ps[n % 4]
                    for dc in range(DC):
                        self.mm(ps, slot[:, dc, (c % 4) * 128:(c % 4 + 1) * 128], xb[dc][i],
                                start=(dc == 0), stop=(dc == DC - 1))
                    dst = TR(qk[c][tt], qT_all.ap[:, c, tt * 512:(tt + 1) * 512])
                    self.copy("act" if n % 2 == 0 else "dve", dst, ps)
                    n += 1
                    self.drain(4)
                if c == 3:
                    sV = self.load_w512(g512_base + 3, 256)
            for k in range(4):
                for i, tt in enumerate(tts):
                    ps = self.ps[n % 4]
                    for dc in range(DC):
                        self.mm(ps, sC[:, dc, k * 128:(k + 1) * 128], xb[dc][i], start=(dc == 0), stop=(dc == DC - 1))
                    dst = TR(kk[k][tt], kT_all.ap[:, k, tt * 512:(tt + 1) * 512])
                    self.copy("act" if n % 2 == 0 else "dve", dst, ps)
                    n += 1
            for i, tt in enumerate(tts):
                for tile in range(4):
                    ps = self.ps[n % 4]
                    for dc in range(DC):
                        self.mm(ps[:, 0:256], xb[dc][i][:, tile * 128:(tile + 1) * 128], sV[:, dc, 0:256],
                                start=(dc == 0), stop=(dc == DC - 1))
                    self.copy("act" if n % 2 == 0 else "dve", Vb[tt * 4 + tile], ps[:, 0:256])
                    n += 1
        if DBG == 1:
            return
        self.drain()
        srcs = [(TR(kk[k][3], kT_all.ap[:, k, T - 128:T]), k * 128, 128) for k in range(4)]
        srcs.append((Vb[15], 512, 256))
        self.exchange(self.d_bounce_h, self.d_gath_h, ("h", j), srcs, rcv)
        for k in range(4):
            self.select_partner(TR(kk[k][4], kT_all.ap[:, k, T:T + 128]), rcv, k * 128, 128)
        self.select_partner(Vb[16], rcv, 512, 256)
        P.fence()
        if DBG == 2:
            return
        self.arena_off = mark
        oT_all = self.carve("oT", [64, 16, 512], BF16)
        qz = self.carve("qz", [128, 8, 2, 128], BF16)
        self.memset("pool", qz, 0.0)
        Pm = [self.carve("Pm", [128, 512], BF16) for _ in range(3)]
        dz_off = self.arena_off
        dsum = [self.carve("dsum", [64, 512], F32) for _ in range(2)]
        end_off = self.arena_off
        self.arena_off = dz_off
        zb = [self.carve("zb", [128, 512], BF16) for _ in range(2)]
        zsq = [self.carve("zsq", [128, 512], BF16) for _ in range(2)]
        assert self.arena_off == end_off
        meanb = [self.ln_mean]
        scratch = (zb, zsq, [self.ln_mean], [self.ln_msq], [self.ln_rstd])
        diag = self.carve("diag", [128, 2, 16, 128], BF16)
        distbf = self.carve("distbf", [128, 4, 128], BF16)
        slopes = alibi_slopes()
        for kd in range(4):
            self.copy("dve", distbf[:, kd, :], self.cst[:, C_DIST + kd * 128:C_DIST + (kd + 1) * 128])
        t32 = meanb[0]
        for h in range(16):
            ta = t32[:, (h % 2) * 256:(h % 2) * 256 + 128]
            tb = t32[:, (h % 2) * 256 + 128:(h % 2) * 256 + 256]
            self.ts("dve", ta, self.cst[:, C_IDENT:C_IDENT + 128], float(np.float32(-8.0 * slopes[h])), None, ALU.mult)
            self.copy("act", diag[:, 0, h, :], ta)
            self.tt("dve", tb, ta, diag[:, 0, h, :], ALU.subtract)
            self.copy("act", diag[:, 1, h, :], tb)
        ones64 = self.ones_bf[:, 0:64]
        att_pend = []
        ps_s_t = [self.ps[0], self.ps[1]]
        ps_o_t = [self.ps[2], self.ps[3], self.ps[4]]
        ps_d_t = [self.ps[5], self.ps[6]]
        for st in range(4):
            ok = [self.key("oT") for _ in range(16)]
            subs = []
            u = 0
            for nb in range(4):
                nq = st * 4 + nb
                for k in range(4):
                    blocks = []
                    if nq > 0:
                        blocks.append((nq - 1, 0))
                    blocks.append((nq, 1))
                    blocks.append((nq + 1, 2) if nq < 15 else (16, 3))
                    for bi, (kb, kind) in enumerate(blocks):
                        subs.append(dict(nq=nq, nb=nb, k=k, kb=kb, kind=kind, first=(bi == 0),
                                         last=(bi == len(blocks) - 1), u=u, newq=(k == 0 and bi == 0)))
                    u += 1
            NS = len(subs)

            def A_Q(t):
                d = subs[t]
                if not d["newq"]:
                    return
                nq = d["nq"]
                for e_ in range(2):
                    src_ap = qT_all.ap[e_ * 64:(e_ + 1) * 64, :, nq * 128:(nq + 1) * 128]
                    dst_ap = qz.ap[e_ * 64:(e_ + 1) * 64, :, e_, :]
                    P.add("act", lambda e, d_=dst_ap, a=src_ap: e.activation(d_, a, AF.Copy),
                          reads=[qk[c][nq // 4] for c in range(8)], writes=[qz])

            def A_S(t):
                d = subs[t]
                k, kb = d["k"], d["kb"]
                ps_s = ps_s_t[t % 2]
                kkey = kk[k][kb // 4] if kb < 16 else kk[k][4]
                o_ap = ps_s.ap
                l_ap = kT_all.ap[:, k, kb * 128:(kb + 1) * 128]
                r_ap = qz.ap[:, 2 * k:2 * k + 2, :, :].rearrange("p c e q -> p (c e q)")
                P.add("pe", lambda e, o_ap=o_ap, l_ap=l_ap, r_ap=r_ap: e.matmul(o_ap, l_ap, r_ap, start=True, stop=False),
                      reads=[kkey, qz], writes=[ps_s])
                kindT = {0: 2, 1: 1, 2: 0, 3: 3}[d["kind"]]
                for hl in range(2):
                    d_ap = distbf.ap[:, kindT, :]
                    b_ap = diag.ap[:, hl, 4 * k:4 * k + 4, :].rearrange("p h q -> p (h q)")
                    P.add("pe", lambda e, o_ap=o_ap, d_ap=d_ap, b_ap=b_ap, hl=hl: e.matmul(o_ap, d_ap, b_ap, start=False, stop=(hl == 1)),
                          reads=[diag, distbf, ps_s], writes=[ps_s])

            def A_E(t):
                self.act(Pm[t % 3], ps_s_t[t % 2], AF.Exp, scale=0.125)

            def A_PV(t):
                d = subs[t]
                k, kb = d["k"], d["kb"]
                ps_o = ps_o_t[d["u"] % 3]
                ps_d = ps_d_t[d["u"] % 2]
                self.mm(ps_o[0:64, :], Vb[kb][:, k * 64:(k + 1) * 64], Pm[t % 3], start=d["first"], stop=d["last"])
                self.mm(ps_d[0:64, :], ones64, Pm[t % 3], start=d["first"], stop=d["last"])

            def A_N1(t):
                d = subs[t]
                if not d["last"]:
                    return
                ds = dsum[d["u"] % 2]
                ps_d = ps_d_t[d["u"] % 2]
                for hh in range(4):
                    h = 4 * d["k"] + hh
                    self.ts("dve", ds[:, hh * 128:(hh + 1) * 128], ps_d[0:64, hh * 128:(hh + 1) * 128],
                            esink[0:64, h:h + 1], None, ALU.add)

            def A_N2(t):
                d = subs[t]
                if not d["last"]:
                    return
                ds = dsum[d["u"] % 2]
                self.act(ds, ds, AF.Ln)
                self.act(ds, ds, AF.Exp, scale=-1.0)

            def A_N3(t):
                d = subs[t]
                if not d["last"]:
                    return
                ds = dsum[d["u"] % 2]
                ps_o = ps_o_t[d["u"] % 3]
                for hh in range(4):
                    h = 4 * d["k"] + hh
                    dst = TR(ok[h], oT_all.ap[:, h, d["nb"] * 128:(d["nb"] + 1) * 128])
                    self.tt("dve", dst, ps_o[0:64, hh * 128:(hh + 1) * 128], ds[:, hh * 128:(hh + 1) * 128], ALU.mult)

            stages_ = [(0, A_Q), (1, A_S), (2, A_E), (3, A_PV), (4, A_N1), (5, A_N2), (6, A_N3)]
            for _ in range(5):
                if att_pend:
                    att_pend.pop(0)()
            for it in range(NS + 6):
                for l_, fn_ in sorted(stages_, key=lambda x: -x[0]):
                    t = it - l_
                    if 0 <= t < NS:
                        fn_(t)
                if att_pend:
                    att_pend.pop(0)()
            while att_pend:
                att_pend.pop(0)()
            att_pend = self.outproj_ln([st], lambda i: [TR(ok[h], oT_all.ap[:, h, :]) for h in range(16)],
                                       lambda c: gwo_base + c, 64, 16 * 128,
                                       lambda slot, k: slot[0:64, k * 128:(k + 1) * 128],
                                       ALPHA, ln_idx, LN_EPS, scratch, defer="all")
        self.pend = att_pend

    def stage_mlstm(self, g512_base, gwo_base, ln_idx, j):
        P = self.P
        AX = mybir.AxisListType.X
        hT_all = self.carve("hT", [128, 8, T], BF16)
        hk = [[self.key("hT") for _ in range(4)] for _ in range(8)]
        xb = [self.carve("xb", [128, 512], BF16) for _ in range(DC)]
        gtmp = self.carve("gtmp", [128, 512], F32)

        def gview(name, k_):
            return TR(self.key(name), gtmp.ap[:, k_ * 128:(k_ + 1) * 128].rearrange("p (t d h) -> p t d h", t=16, d=2))
        lf = gview("lf", 0)
        etmp = gview("etmp", 1)
        gg = gview("gg", 2)
        bb = gview("bb", 3)
        eb = self.carve("eb", [128, 16, 2, 4], F32)
        emg = self.carve("emg", [128, 16, 2, 4], F32)
        egb = self.carve("egb", [128, 16, 2, 4], F32)
        eG = self.carve("eG", [128, 16, 2, 4], F32)
        normg = self.carve("normg", [128, 256], F32)
        wgs = self.carve("wgs", [128, 8, 16], BF16)
        mark = self.arena_off
        qT = self.carve("qTh", [128, T], BF16)
        kT = self.carve("kTh", [128, T], BF16)
        kTM = self.carve("kTM", [128, 16, 128], BF16)
        vaug = self.carve("vaug", [128, 16, 264], BF16)
        so = self.carve("so", [128, 16, 256], BF16)
        h1 = self.carve("h1", [128, 16, 256], BF16)
        gp = TR(self.key("gp"), h1.ap.rearrange("p t c -> p (t c)")[:, 0:512].bitcast(F32).rearrange("p (t g) -> p t g", t=16))
        Am = [self.carve("Am", [128, 128], BF16) for _ in range(2)]
        vp = [self.carve("vp", [128, 258], BF16) for _ in range(2)]
        vpp = [self.carve("vpp", [128, 258], BF16) for _ in range(2)]
        C32 = [self.carve("C", [128, 260], F32) for _ in range(2)]
        Cb = [self.carve("Cb", [128, 264], BF16) for _ in range(3)]
        rcv = self.carve("rcvs", [128, 2, 260], F32)
        dd = [self.carve("dd", [128, 2], F32) for _ in range(4)]
        hs = [self.carve("hs", [128, 256], F32) for _ in range(2)]
        hs.append(TR(self.key("hsA"), gtmp.ap[:, 0:256]))
        hs.append(TR(self.key("hsB"), gtmp.ap[:, 256:512]))
        st4 = [self.carve("st4", [128, 8], F32) for _ in range(4)]
        st6 = [self.carve("st6", [128, 8], F32) for _ in range(4)]
        hb = [self.carve("hb", [128, 256], BF16) for _ in range(2)]
        one_col = self.eps_col(1.0)
        eps_col = self.eps_col(LN_EPS)
        KS = float(128 ** -0.5)

        if DBG not in (21, 22, 23):
            self.dma("pool", wgs, self.d_wg[j].rearrange("p (a b) -> p a b", a=DC), group="wgs")
        self.memset("pool", vaug, 1.0)
        gp4 = gp.ap.rearrange("p t (k h) -> p t k h", k=4)
        psT = TR(self.ps[7].key, self.ps[7].ap.bitcast(BF16))

        cnt = 0
        xb_sets = [xb, [TR(("win", 2), self.win_slots[2].ap[:, dc, :]) for dc in range(DC)]]
        for h in range(4):
            self.dma("sp", normg, self.d_normg[j][:, h * 256:(h + 1) * 256], group="normg")
            if h == 0:
                self.win_rr = 0
                sW = self.load_w512(g512_base + 2 * h)
                sO = self.load_w512(g512_base + 2 * h + 1, 256)
            else:
                sW, sO = sW_next, sO_next

            def cast_tt(tt):
                for dc in range(DC):
                    self.copy("act" if dc % 2 == 0 else "dve", xb_sets[tt % 2][dc], self.x[dc][tt])
            cast_tt(0)
            for tt in range(4):
                xb = xb_sets[tt % 2]
                if tt == 1:
                    self.drain()
                if tt + 1 < 4:
                    cast_tt(tt + 1)
                ps = self.ps[cnt % 4]; cnt += 1
                for dc in range(DC):
                    self.mm(ps, sW[:, dc, 0:128], xb[dc], start=(dc == 0), stop=(dc == DC - 1))
                self.copy("act", qT[:, tt * 512:(tt + 1) * 512], ps)
                ps = self.ps[cnt % 4]; cnt += 1
                for dc in range(DC):
                    self.mm(ps, sW[:, dc, 128:256], xb[dc], start=(dc == 0), stop=(dc == DC - 1))
                self.act(kT[:, tt * 512:(tt + 1) * 512], ps, AF.Copy, scale=KS)
                for tile in range(4):
                    if DBG == 22:
                        continue
                    i = tt * 4 + tile
                    ps = self.ps[cnt % 4]; cnt += 1
                    for dc in range(DC):
                        self.mm(ps[:, 0:128], xb[dc][:, tile * 128:(tile + 1) * 128], sW[:, dc, 128:256],
                                start=(dc == 0), stop=(dc == DC - 1))
                    self.act(kTM[:, i, :], ps[:, 0:128], AF.Copy, scale=KS)
                    ps = self.ps[cnt % 4]; cnt += 1
                    for dc in range(DC):
                        self.mm(ps[:, 0:256], xb[dc][:, tile * 128:(tile + 1) * 128], sW[:, dc, 256:512],
                                start=(dc == 0), stop=(dc == DC - 1))
                    self.copy("dve", vaug[:, i, 0:256], ps[:, 0:256])
                    if DBG == 23:
                        continue
                    ps = self.ps[cnt % 4]; cnt += 1
                    for dc in range(DC):
                        self.mm(ps[:, 0:256], xb[dc][:, tile * 128:(tile + 1) * 128], sO[:, dc, 0:256],
                                start=(dc == 0), stop=(dc == DC - 1))
                    self.act(so[:, i, :], ps[:, 0:256], AF.Sigmoid)
                    self.drain(4)
                    if h == 0 and DBG not in (21, 22, 23):
                        ps = self.ps[cnt % 4]; cnt += 1
                        for dc in range(DC):
                            self.mm(ps[:, 0:16], xb[dc][:, tile * 128:(tile + 1) * 128], wgs[:, dc, :],
                                    start=(dc == 0), stop=(dc == DC - 1))
                        self.tt("dve", gp[:, i, :], ps[:, 0:16], self.cst[:, C_GBIAS + j * 16:C_GBIAS + (j + 1) * 16], ALU.add)
            if h < 3:
                self.win_rr = 0
                sW_next = self.load_w512(g512_base + 2 * (h + 1))
                sO_next = self.load_w512(g512_base + 2 * (h + 1) + 1, 256)
            if h == 0:
                for dr in range(2):
                    src = TR(gp.key, gp4[:, :, 1 + 2 * dr, :])
                    self.act(etmp[:, :, dr, :], src, AF.Exp, scale=-1.0)
                    self.act(lf[:, :, dr, :], etmp[:, :, dr, :], AF.Ln, bias=one_col)
                lf2 = TR(lf.key, lf.ap.rearrange("p t d h -> p (t d h)"))
                pg = self.ps[4]
                pgF = TR(pg.key, pg.ap[:, 0:128].rearrange("p (t d h) -> p t d h", t=16, d=2))
                pgB = TR(pg.key, pg.ap[:, 128:256].rearrange("p (t d h) -> p t d h", t=16, d=2))
                self.mm(pg[:, 0:128], self.cst[:, C_MTRIF:C_MTRIF + 128], lf2)
                self.mm(pg[:, 128:256], self.cst[:, C_MTRIB:C_MTRIB + 128], lf2)
                self.mm(self.ps[5][:, 0:128], self.cst[:, C_MBLK:C_MBLK + 128], lf2)
                self.copy("dve", gg[:, :, 0, :], pgF[:, :, 0, :])
                self.copy("dve", gg[:, :, 1, :], pgB[:, :, 1, :])
                for dr in range(2):
                    self.tt("dve", bb[:, :, dr, :], TR(gp.key, gp4[:, :, 2 * dr, :]), gg[:, :, dr, :], ALU.subtract)
                self.act(eb, bb, AF.Exp)
                self.act(emg, gg, AF.Exp, scale=-1.0)
                p5 = TR(self.ps[5].key, self.ps[5].ap[:, 0:128].rearrange("p (t d h) -> p t d h", t=16, d=2))
                self.tt("dve", egb, p5, bb, ALU.add)
                self.act(egb, egb, AF.Exp)
                self.act(eG, p5, AF.Exp)
            if DBG == 12:
                return
            LAG = 2
            RING = len(Cb)
            for dirn in range(2):
                mask = self.cst[:, C_MASKF:C_MASKF + 128] if dirn == 0 else self.cst[:, C_MASKB:C_MASKB + 128]
                if dirn == 0:
                    self.memset("dve", C32[0], 0.0)
                    self.memset("dve", Cb[0], 0.0)
                order = list(range(16)) if dirn == 0 else list(range(15, -1, -1))

                psA_t = [self.ps[0][:, 0:128], self.ps[1][:, 0:128]]
                pso_t = [self.ps[2], self.ps[3], self.ps[6]]

                def tsl_(t):
                    i = order[t]
                    return i, slice(i * 128, (i + 1) * 128)

                def V0(t):
                    i, _ = tsl_(t)
                    if dirn == 0:
                        self.act(vpp[t % 2][:, 0:257], vaug[:, i, 0:257], AF.Copy, scale=egb[:, i, dirn, h:h + 1])
                    else:
                        self.ts("dve", vpp[t % 2][:, 0:257], vaug[:, i, 0:257], egb[:, i, dirn, h:h + 1], None, ALU.mult)

                def L1(t):
                    i, tsl = tsl_(t)
                    self.mm(psA_t[t % 2], kT[:, tsl], qT[:, tsl])
                    self.mm(self.ps[4 + t % 2][:, 0:257], kTM[:, i, :], vpp[t % 2][:, 0:257])

                def L2(t):
                    i, tsl = tsl_(t)
                    ps_U = self.ps[4 + t % 2]
                    self.stt("dve", C32[(t + 1) % 2][:, 0:257], C32[t % 2][:, 0:257], eG[:, i, dirn, h:h + 1],
                             ps_U[:, 0:257], ALU.mult, ALU.add)
                    self.copy("pool", Cb[(t + 1) % RING][:, 0:257], C32[(t + 1) % 2][:, 0:257])
                    self.tt("dve", Am[t % 2], psA_t[t % 2], mask, ALU.mult)
                    self.act(vp[t % 2][:, 0:257], vaug[:, i, 0:257], AF.Copy, scale=eb[:, i, dirn, h:h + 1])

                def L3(t):
                    i, tsl = tsl_(t)
                    ps_o = pso_t[t % 3]
                    self.mm(ps_o[:, 0:257], Am[t % 2], vp[t % 2][:, 0:257], start=True, stop=False)
                    self.mm(ps_o[:, 0:257], qT[:, tsl], Cb[t % RING][:, 0:257], start=False, stop=True)
                    self.act(dd[t % 4][:, 0:1], ps_o[:, 256:257], AF.Abs)

                def L4(t):
                    i, tsl = tsl_(t)
                    ps_o = pso_t[t % 3]
                    d_ = dd[t % 4]
                    self.ts("dve", d_[:, 0:1], d_[:, 0:1], emg[:, i, dirn, h:h + 1], None, ALU.max)
                    P.add("dve", lambda e, o=d_: e.reciprocal(o.ap[:, 1:2], o.ap[:, 0:1]), reads=[d_], writes=[d_])
                    if dirn == 0:
                        self.ts("dve", h1[:, i, :], ps_o[:, 0:256], d_[:, 1:2], None, ALU.mult)
                    else:
                        hs_ = hs[t % 4]
                        s6 = st6[t % 4]
                        self.stt("dve", hs_, ps_o[:, 0:256], d_[:, 1:2], h1[:, i, :], ALU.mult, ALU.add)
                        P.add("dve", lambda e, o=s6, a=hs_: e.bn_stats(o.ap[:, 0:6], a.ap), reads=[hs_], writes=[s6])

                def L5(t):
                    s6 = st6[t % 4]
                    s4 = st4[t % 4]
                    P.add("dve", lambda e, o=s4, a=s6: e.bn_aggr(o.ap[:, 0:2], a.ap[:, 0:6]), reads=[s6], writes=[s4])
                    self.act(s4[:, 2:3], s4[:, 1:2], AF.Sqrt, bias=eps_col)

                def L6(t):
                    hs_ = hs[t % 4]
                    s4 = st4[t % 4]
                    P.add("dve", lambda e, o=s4: e.reciprocal(o.ap[:, 3:4], o.ap[:, 2:3]), reads=[s4], writes=[s4])
                    self.stt("dve", s4[:, 4:5], s4[:, 0:1], -1.0, s4[:, 3:4], ALU.mult, ALU.mult)
                    self.act(hs_, hs_, AF.Identity, bias=s4[:, 4:5], scale=s4[:, 3:4])

                def L7(t):
                    hs_ = hs[t % 4]
                    self.tt("pool", hs_, hs_, normg, ALU.mult)

                def L7b(t):
                    i, _ = tsl_(t)
                    self.tt("pool", hb[t % 2], hs[t % 4], so[:, i, :], ALU.mult)

                def L8(t):
                    i, _ = tsl_(t)
                    hb_ = hb[t % 2]
                    for k2 in range(2):
                        self.transpose(psT[:, k2 * 128:(k2 + 1) * 128], hb_[:, k2 * 128:(k2 + 1) * 128], self.ident_bf)
                    for k2 in range(2):
                        dst = TR(hk[2 * h + k2][i // 4], hT_all.ap[:, 2 * h + k2, i * 128:(i + 1) * 128])
                        self.copy("act", dst, psT[:, k2 * 128:(k2 + 1) * 128])

                stages_ = [(0, V0), (1, L1), (2, L2), (3, L3), (4, L4)]
                if dirn == 1:
                    stages_ += [(5, L5), (6, L6), (7, L7), (8, L7b), (9, L8)]
                maxlag = max(l_ for l_, _ in stages_)
                for it in range(16 + maxlag):
                    for l_, fn_ in sorted(stages_, key=lambda x: -x[0]):
                        t = it - l_
                        if 0 <= t < 16:
                            fn_(t)
                if DBG == 13 or (DBG == 15 and dirn == 1):
                    return
                if dirn == 0:
                    self.exchange(self.d_bounce_s, self.d_gath_s, ("s", j, h), [(C32[0], 0, 260)], rcv)
                    self.select_partner(C32[0], rcv, 0, 260)
                    self.copy("act", Cb[0][:, 0:257], C32[0][:, 0:257])
                    if DBG == 14:
                        return
        P.fence()
        self.arena_off = mark
        scratch = self.ln_scratch()
        ml_pend = []
        for st in range(2):
            tts = [2 * st, 2 * st + 1]
            ml_pend = self.outproj_ln(tts, lambda i: [TR(hk[k][tts[i]], hT_all.ap[:, k, tts[i] * 512:(tts[i] + 1) * 512]) for k in range(8)],
                                      lambda c: gwo_base + c, 128, 8 * 128,
                                      lambda slot, k: slot[:, k * 128:(k + 1) * 128],
                                      ALPHA, ln_idx, LN_EPS, scratch, defer=("all" if st == 0 else "last"), drain=ml_pend)
        self.pend = ml_pend


def _lay_w512(w, cols):
    out = np.zeros((128, DC, 512), np.float32)
    out[:, :, :len(cols)] = w[:, cols].reshape(DC, 128, len(cols)).transpose(1, 0, 2)
    return out.reshape(128, DC * 512)


def _lay_wo(w, c, kp, nk):
    out = np.zeros((128, FC * 128), np.float32)
    blk = w[:, c * 128:(c + 1) * 128].reshape(nk, kp, 128).transpose(1, 0, 2).reshape(kp, nk * 128)
    out[:kp, :nk * 128] = blk
    return out


def _prepare(inputs, stages_spec):
    x = np.asarray(inputs["x"], np.float32)
    ffn_w_in = np.asarray(inputs["ffn_w_in"], np.float32)
    ffn_w_out = np.asarray(inputs["ffn_w_out"], np.float32)
    w512 = []
    wo = []
    stages = []
    for spec in stages_spec:
        if spec[0] == "ffn":
            l, i = spec[1], spec[2]
            base512 = len(w512)
            wi = ffn_w_in[l, i]
            for jg in range(11):
                cols = list(range(jg * 256, (jg + 1) * 256)) + list(range(DFF + jg * 256, DFF + (jg + 1) * 256))
                w512.append(_lay_w512(wi, cols))
            basewo = len(wo)
            for c in range(DC):
                wo.append(_lay_wo(ffn_w_out[l, i], c, 128, FC))
            stages.append(("ffn", base512, basewo, l * 3 + (0 if i == 0 else 2)))
        elif spec[0] == "attn":
            j = spec[1]
            w = np.asarray(inputs["at_w_in"], np.float32)[j]
            base512 = len(w512)
            w512.append(_lay_w512(w, list(range(0, 512))))
            w512.append(_lay_w512(w, list(range(512, 1024))))
            kc = []
            for k in range(4):
                kc += list(range(1024 + k * 64, 1024 + (k + 1) * 64)) * 2
            w512.append(_lay_w512(w, kc))
            w512.append(_lay_w512(w, list(range(1280, 1536))))
            basewo = len(wo)
            wout = np.asarray(inputs["at_w_out"], np.float32)[j]
            for c in range(DC):
                wo.append(_lay_wo(wout, c, 64, 16))
            stages.append(("attn", base512, basewo, (2 * j + 1) * 3 + 1, j))
        elif spec[0] == "mlstm":
            j = spec[1]
            w = np.asarray(inputs["ml_w_in"], np.float32)[j]
            base512 = len(w512)
            for h in range(4):
                cols = (list(range(h * 128, (h + 1) * 128)) + list(range(512 + h * 128, 512 + (h + 1) * 128))
                        + list(range(1024 + h * 256, 1024 + (h + 1) * 256)))
                w512.append(_lay_w512(w, cols))
                w512.append(_lay_w512(w, list(range(2048 + h * 256, 2048 + (h + 1) * 256))))
            basewo = len(wo)
            wout = np.asarray(inputs["ml_w_out"], np.float32)[j]
            for c in range(DC):
                wo.append(_lay_wo(wout, c, 128, 8))
            stages.append(("mlstm", base512, basewo, (2 * j) * 3 + 1, j))
        else:
            raise NotImplementedError(spec)
    w512 = np.stack(w512) if w512 else np.zeros((1, 128, 4096), np.float32)
    wo = np.stack(wo) if wo else np.zeros((1, 128, FC * 128), np.float32)
    ml_w_in = np.asarray(inputs["ml_w_in"], np.float32)
    wg_even = np.zeros((2, 128, DC * 16), np.float32)
    wg_odd = np.zeros((2, 128, DC * 16), np.float32)
    for j in range(2):
        g = ml_w_in[j][:, 3072:3088]
        wg_even[j] = g.reshape(DC, 128, 16).transpose(1, 0, 2).reshape(128, DC * 16)
        g2 = g.reshape(1024, 4, 4)[:, [2, 3, 0, 1], :].reshape(1024, 16)
        wg_odd[j] = g2.reshape(DC, 128, 16).transpose(1, 0, 2).reshape(128, DC * 16)
    normg = np.ascontiguousarray(np.broadcast_to(np.asarray(inputs["ml_norm_g"], np.float32)[:, None, :], (2, 128, 1024)))
    in_maps = []
    for core in range(NCORES):
        b, half = core // 2, core % 2
        xs = x[b, half * T:(half + 1) * T]
        if half:
            xs = xs[::-1]
        xT = np.ascontiguousarray(xs.reshape(T, DC, 128).transpose(2, 1, 0)).reshape(128, DC * T)
        consts = _const_table(core, np.asarray(inputs["ln_g"], np.float32), np.asarray(inputs["ln_b"], np.float32),
                              np.asarray(inputs["ml_gate_b"], np.float32), np.asarray(inputs["at_sink"], np.float32))
        in_maps.append({"xT": xT, "w512": w512, "wo": wo, "wg": wg_odd if half else wg_even, "normg": normg,
                        "consts": consts})
    return stages, in_maps, w512.shape[0], wo.shape[0]


def _assemble(results):
    out = np.zeros((4, 2 * T, D), np.float32)
    for core in range(NCORES):
        b, half = core // 2, core % 2
        o = np.asarray(results[core]["outT"]).reshape(128, DC, T).transpose(2, 1, 0).reshape(T, D)
        if half:
            o = o[::-1]
        out[b, half * T:(half + 1) * T] = o
    return out


FULL_SPEC = []
for _l in range(DEPTH):
    FULL_SPEC.append(("ffn", _l, 0))
    FULL_SPEC.append(("mlstm", _l // 2) if _l % 2 == 0 else ("attn", _l // 2))
    FULL_SPEC.append(("ffn", _l, 1))


def run_spec(inputs, spec, trace=False):
    stages, in_maps, n512, nwo = _prepare(inputs, spec)
    b = Builder(stages, n512, nwo)
    nc = b.build()
    res = run_bass_kernel_spmd(nc, in_maps, core_ids=list(range(NCORES)))
    return _assemble(res.results)


def kernel(**inputs):
    return run_spec(inputs, FULL_SPEC)
```
